# Optimizing a Trainium2 kernel written in Bass

```python
import jax
import jax.numpy as jnp
from jax import lax
import numpy as np


D_MODEL = 1024
BATCH = 2
SEQ = 16384
DEPTH = 4

N_META = 16
N_MIXERS = 3
N_A = (DEPTH + 2) // 3
N_B = (DEPTH + 1) // 3
N_C = DEPTH // 3

DN_ALPHA = (2.0 * DEPTH) ** 0.25
DN_BETA = (8.0 * DEPTH) ** -0.25
LN_EPS = 1e-5
RMS_EPS = 1e-6

MLA_HEADS = 16
MLA_NOPE = 128
MLA_ROPE = 64
MLA_V = 128
MLA_Q_RANK = 256
MLA_KV_RANK = 128
MLA_IN = MLA_Q_RANK + MLA_KV_RANK + MLA_ROPE
ROPE_BASE = 10000.0
Q_BLOCK = 128

RW_HEAD = 64
RW_HEADS = D_MODEL // RW_HEAD
RW_DECAY_LORA = 64
RW_A_LORA = 64
RW_GATE_LORA = 160
RW_GN_EPS = 64e-5

LRU_WIDTH = 1280
LRU_BLOCKS = 5
LRU_BLOCK = LRU_WIDTH // LRU_BLOCKS
LRU_C = 8.0
LRU_CONV = 4
LRU_CONV_LEFT = 2

D_FF = 2816
FFN_CONV = 3
FFN_CONV_LEFT = 1

kernel_name = 'hybrid_mla_rwkv7_rglru_encoder'


def _layer_norm(x, g, b):
    xf = x.astype(jnp.float32)
    mu = jnp.mean(xf, axis=-1, keepdims=True)
    var = jnp.mean(jnp.square(xf - mu), axis=-1, keepdims=True)
    return ((xf - mu) * lax.rsqrt(var + LN_EPS)).astype(x.dtype) * g + b


def _rms_norm(x, g):
    xf = x.astype(jnp.float32)
    ms = jnp.mean(jnp.square(xf), axis=-1, keepdims=True)
    return (xf * lax.rsqrt(ms + RMS_EPS)).astype(x.dtype) * g


def _dw_conv(x, w, b, pad_left):
    k_width, c = w.shape
    y = lax.conv_general_dilated(
        x, w[:, None, :], window_strides=(1,),
        padding=[(pad_left, k_width - 1 - pad_left)],
        dimension_numbers=('NWC', 'WIO', 'NWC'), feature_group_count=c)
    return y + b


def _rope(x, cos, sin):
    x1, x2 = jnp.split(x, 2, axis=-1)
    return jnp.concatenate([x1 * cos - x2 * sin, x1 * sin + x2 * cos], axis=-1)


def _mla_mixer(x, cos, sin, w_in, q_norm, w_q_up, kv_norm, w_kv_up, w_o):
    B, T, _ = x.shape
    H = MLA_HEADS
    hproj = x @ w_in
    c_q, c_kv, k_pe = jnp.split(hproj, [MLA_Q_RANK, MLA_Q_RANK + MLA_KV_RANK], axis=-1)
    c_q = _rms_norm(c_q, q_norm)
    c_kv = _rms_norm(c_kv, kv_norm)
    q = (c_q @ w_q_up).reshape(B, T, H, MLA_NOPE + MLA_ROPE)
    q_nope, q_pe = jnp.split(q, [MLA_NOPE], axis=-1)
    q_pe = _rope(q_pe, cos[:, :, None, :], sin[:, :, None, :])
    k_pe = _rope(k_pe, cos, sin)
    w_uk = w_kv_up[:, :, :MLA_NOPE]
    w_uv = w_kv_up[:, :, MLA_NOPE:]
    q_lat = jnp.einsum('bthn,rhn->bthr', q_nope, w_uk)
    scale = (MLA_NOPE + MLA_ROPE) ** -0.5

    def attend(ql, qp):
        s = jnp.einsum('bqhr,bkr->bhqk', ql, c_kv) + jnp.einsum('bqhp,bkp->bhqk', qp, k_pe)
        p = jax.nn.softmax(s.astype(jnp.float32) * scale, axis=-1).astype(c_kv.dtype)
        return jnp.einsum('bhqk,bkr->bqhr', p, c_kv)

    o_meta = attend(q_lat[:, :N_META], q_pe[:, :N_META])
    n_blk = (T - N_META) // Q_BLOCK
    ql_b = jnp.moveaxis(q_lat[:, N_META:].reshape(B, n_blk, Q_BLOCK, H, MLA_KV_RANK), 1, 0)
    qp_b = jnp.moveaxis(q_pe[:, N_META:].reshape(B, n_blk, Q_BLOCK, H, MLA_ROPE), 1, 0)
    o_b = lax.map(lambda qs: attend(qs[0], qs[1]), (ql_b, qp_b))
    o_real = jnp.moveaxis(o_b, 0, 1).reshape(B, T - N_META, H, MLA_KV_RANK)
    o_lat = jnp.concatenate([o_meta, o_real], axis=1)
    o = jnp.einsum('bthr,rhv->bthv', o_lat, w_uv).reshape(B, T, H * MLA_V)
    return o @ w_o


def _wkv7_scan(r, w, a, b, k, v, reverse):
    B, T, H, N = r.shape

    def step(S, inp):
        r_t, w_t, a_t, b_t, k_t, v_t = inp
        sa = jnp.einsum('bhvk,bhk->bhv', S, a_t)
        S = S * w_t[:, :, None, :] + sa[..., None] * b_t[:, :, None, :] + v_t[..., None] * k_t[:, :, None, :]
        return S, jnp.einsum('bhvk,bhk->bhv', S, r_t)

    xs = tuple(jnp.swapaxes(t, 0, 1) for t in (r, w, a, b, k, v))
    S0 = jnp.zeros((B, H, N, N), jnp.float32)
    _, y = lax.scan(step, S0, xs, reverse=reverse)
    return jnp.swapaxes(y, 0, 1)


def _rwkv7_mixer(x, mu, w_rkv, w0, w1, w2, a0, a1, a2, g1, g2, k_k, k_a, r_k, gn_g, gn_b, w_o):
    B, T, D = x.shape
    H, N = RW_HEADS, RW_HEAD
    f32 = jnp.float32
    xp = jnp.pad(x, ((0, 0), (1, 1), (0, 0)))
    xx = 0.5 * (xp[:, :-2] + xp[:, 2:]) - x
    xr, xw, xk, xv, xa, xg = (x + xx * mu[i] for i in range(6))
    r = xr @ w_rkv[0]
    k = xk @ w_rkv[1]
    v = xv @ w_rkv[2]
    g = jax.nn.sigmoid(xg @ g1) @ g2

    def heads(t):
        return t.reshape(B, T, H, N).astype(f32)

    rf, kf, vf = heads(r), heads(k), heads(v)
    kk = heads(k * k_k)
    kk = kk * lax.rsqrt(jnp.maximum(jnp.sum(jnp.square(kk), axis=-1, keepdims=True), 1e-24))
    k_a_h = k_a.reshape(H, N).astype(f32)
    r_k_h = r_k.reshape(H, N).astype(f32)
    ys = []
    bonuses = []
    for d in range(2):
        w_log = -jax.nn.softplus(-(w0[d] + jnp.tanh(xw @ w1[d]) @ w2[d])) - 0.5
        decay = jnp.exp(-jnp.exp(heads(w_log)))
        a = heads(jax.nn.sigmoid(a0[d] + (xa @ a1[d]) @ a2[d]))
        kd = kf * (1.0 + (a - 1.0) * k_a_h)
        ys.append(_wkv7_scan(rf, decay, -kk, kk * a, kd, vf, reverse=(d == 1)))
        bonuses.append(jnp.sum(rf * kd * r_k_h, axis=-1, keepdims=True) * vf)
    y = ys[0] + ys[1]
    mu_y = jnp.mean(y, axis=-1, keepdims=True)
    var_y = jnp.mean(jnp.square(y - mu_y), axis=-1, keepdims=True)
    y = ((y - mu_y) * lax.rsqrt(var_y + RW_GN_EPS)).reshape(B, T, D) * gn_g + gn_b
    y = y + (bonuses[0] + bonuses[1]).reshape(B, T, D)
    return (y.astype(x.dtype) * g) @ w_o


def _linear_recurrence(a, b, reverse):
    def combine(lhs, rhs):
        a_l, b_l = lhs
        a_r, b_r = rhs
        return a_l * a_r, a_r * b_l + b_r
    _, h = lax.associative_scan(combine, (a, b), axis=1, reverse=reverse)
    return h


def _rglru_mixer(x, w_in, conv_w, conv_b, gate_w, gate_b, lam, w_o):
    B, T, _ = x.shape
    f32 = jnp.float32
    gate_branch, u = jnp.split(x @ w_in, 2, axis=-1)
    gate_branch = jax.nn.gelu(gate_branch, approximate=True)
    u = _dw_conv(u, conv_w, conv_b, LRU_CONV_LEFT)
    ub = u.reshape(B, T, LRU_BLOCKS, LRU_BLOCK)
    uf = u.astype(f32)
    hs = []
    for d in range(2):
        gates = jnp.einsum('btnj,gnjk->gbtnk', ub, gate_w[d]).reshape(2, B, T, LRU_WIDTH)
        gates = jax.nn.sigmoid((gates + gate_b[d][:, None, None, :]).astype(f32))
        r_gate, i_gate = gates[0], gates[1]
        log_a = -LRU_C * r_gate * jax.nn.softplus(-lam[d].astype(f32))
        a = jnp.exp(log_a)
        b = jnp.sqrt(-jnp.expm1(2.0 * log_a)) * (i_gate * uf)
        hs.append(_linear_recurrence(a, b, reverse=(d == 1)))
    h = (hs[0] + hs[1]).astype(x.dtype)
    return (h * gate_branch) @ w_o


def _conv_glu_ffn(x, w_in, conv_w, conv_b, w_out):
    g, u = jnp.split(x @ w_in, 2, axis=-1)
    g = _dw_conv(g, conv_w, conv_b, FFN_CONV_LEFT)
    return (jax.nn.silu(g) * u) @ w_out


def setup_inputs(seed: int = 0) -> dict:
    key = jax.random.key(seed)
    keys = iter(jax.random.split(key, 64))
    f32 = jnp.float32
    D = D_MODEL

    def nrm(shape, scale):
        return jax.random.normal(next(keys), shape, f32) * scale

    def gain(shape):
        return 1.0 + nrm(shape, 0.02)

    x = nrm((BATCH, SEQ, D), 1.0)
    offsets = jax.random.randint(next(keys), (BATCH, 1), 0, 4096, dtype=jnp.int32)
    positions = offsets + jnp.arange(SEQ, dtype=jnp.int32)[None, :]
    meta_tokens = nrm((N_META, D), 1.0)
    ln_g = gain((DEPTH, 2, D))
    ln_b = nrm((DEPTH, 2, D), 0.02)
    ffn_w_in = nrm((DEPTH, D, 2 * D_FF), D ** -0.5)
    ffn_conv_w = nrm((DEPTH, FFN_CONV, D_FF), FFN_CONV ** -0.5)
    ffn_conv_b = nrm((DEPTH, D_FF), 0.02)
    ffn_w_out = nrm((DEPTH, D_FF, D), DN_BETA * D_FF ** -0.5)
    mla_w_in = nrm((N_A, D, MLA_IN), D ** -0.5)
    mla_q_norm = gain((N_A, MLA_Q_RANK))
    mla_w_q_up = nrm((N_A, MLA_Q_RANK, MLA_HEADS * (MLA_NOPE + MLA_ROPE)), MLA_Q_RANK ** -0.5)
    mla_kv_norm = gain((N_A, MLA_KV_RANK))
    mla_w_kv_up = nrm((N_A, MLA_KV_RANK, MLA_HEADS, MLA_NOPE + MLA_V), MLA_KV_RANK ** -0.5)
    mla_w_o = nrm((N_A, MLA_HEADS * MLA_V, D), DN_BETA * (MLA_HEADS * MLA_V) ** -0.5)
    ratio = jnp.arange(D, dtype=f32) / (D - 1)
    rw_mu = jax.random.uniform(next(keys), (N_B, 6, D), f32)
    rw_w_rkv = nrm((N_B, 3, D, D), D ** -0.5)
    rw_w0 = (-6.5 + 5.0 * ratio ** 0.9) + nrm((N_B, 2, D), 0.1)
    rw_w1 = nrm((N_B, 2, D, RW_DECAY_LORA), D ** -0.5)
    rw_w2 = nrm((N_B, 2, RW_DECAY_LORA, D), 0.5 * RW_DECAY_LORA ** -0.5)
    rw_a0 = nrm((N_B, 2, D), 0.1)
    rw_a1 = nrm((N_B, 2, D, RW_A_LORA), D ** -0.5)
    rw_a2 = nrm((N_B, 2, RW_A_LORA, D), 0.5 * RW_A_LORA ** -0.5)
    rw_g1 = nrm((N_B, D, RW_GATE_LORA), D ** -0.5)
    rw_g2 = nrm((N_B, RW_GATE_LORA, D), RW_GATE_LORA ** -0.5)
    rw_k_k = 0.85 + nrm((N_B, D), 0.02)
    rw_k_a = 1.0 + nrm((N_B, D), 0.02)
    rw_r_k = nrm((N_B, D), 0.1)
    rw_gn_g = gain((N_B, D))
    rw_gn_b = nrm((N_B, D), 0.02)
    rw_w_o = nrm((N_B, D, D), DN_BETA * D ** -0.5)
    lru_w_in = nrm((N_C, D, 2 * LRU_WIDTH), D ** -0.5)
    lru_conv_w = nrm((N_C, LRU_CONV, LRU_WIDTH), LRU_CONV ** -0.5)
    lru_conv_b = nrm((N_C, LRU_WIDTH), 0.02)
    lru_gate_w = nrm((N_C, 2, 2, LRU_BLOCKS, LRU_BLOCK, LRU_BLOCK), LRU_BLOCK ** -0.5)
    lru_gate_b = nrm((N_C, 2, 2, LRU_WIDTH), 0.1)
    a_c = jax.random.uniform(next(keys), (N_C, 2, LRU_WIDTH), f32, 0.9, 0.999)
    a_base = a_c ** (1.0 / LRU_C)
    lru_lambda = jnp.log(a_base) - jnp.log1p(-a_base)
    lru_w_o = nrm((N_C, LRU_WIDTH, D), DN_BETA * LRU_WIDTH ** -0.5)
    return {
        'x': x, 'positions': positions, 'meta_tokens': meta_tokens,
        'ln_g': ln_g, 'ln_b': ln_b,
        'ffn_w_in': ffn_w_in, 'ffn_conv_w': ffn_conv_w, 'ffn_conv_b': ffn_conv_b, 'ffn_w_out': ffn_w_out,
        'mla_w_in': mla_w_in, 'mla_q_norm': mla_q_norm, 'mla_w_q_up': mla_w_q_up,
        'mla_kv_norm': mla_kv_norm, 'mla_w_kv_up': mla_w_kv_up, 'mla_w_o': mla_w_o,
        'rw_mu': rw_mu, 'rw_w_rkv': rw_w_rkv, 'rw_w0': rw_w0, 'rw_w1': rw_w1, 'rw_w2': rw_w2,
        'rw_a0': rw_a0, 'rw_a1': rw_a1, 'rw_a2': rw_a2, 'rw_g1': rw_g1, 'rw_g2': rw_g2,
        'rw_k_k': rw_k_k, 'rw_k_a': rw_k_a, 'rw_r_k': rw_r_k, 'rw_gn_g': rw_gn_g, 'rw_gn_b': rw_gn_b,
        'rw_w_o': rw_w_o,
        'lru_w_in': lru_w_in, 'lru_conv_w': lru_conv_w, 'lru_conv_b': lru_conv_b,
        'lru_gate_w': lru_gate_w, 'lru_gate_b': lru_gate_b, 'lru_lambda': lru_lambda, 'lru_w_o': lru_w_o,
    }


def reference(x, positions, meta_tokens, ln_g, ln_b,
              ffn_w_in, ffn_conv_w, ffn_conv_b, ffn_w_out,
              mla_w_in, mla_q_norm, mla_w_q_up, mla_kv_norm, mla_w_kv_up, mla_w_o,
              rw_mu, rw_w_rkv, rw_w0, rw_w1, rw_w2, rw_a0, rw_a1, rw_a2, rw_g1, rw_g2,
              rw_k_k, rw_k_a, rw_r_k, rw_gn_g, rw_gn_b, rw_w_o,
              lru_w_in, lru_conv_w, lru_conv_b, lru_gate_w, lru_gate_b, lru_lambda, lru_w_o):
    B = x.shape[0]
    dt = x.dtype
    h = jnp.concatenate([jnp.broadcast_to(meta_tokens[None].astype(dt), (B, N_META, D_MODEL)), x], axis=1)
    meta_pos = jnp.broadcast_to(jnp.arange(N_META, dtype=jnp.int32)[None, :], (B, N_META))
    pos = jnp.concatenate([meta_pos, positions + N_META], axis=1)
    inv_freq = ROPE_BASE ** (-jnp.arange(0, MLA_ROPE, 2, dtype=jnp.float32) / MLA_ROPE)
    ang = pos.astype(jnp.float32)[..., None] * inv_freq
    cos = jnp.cos(ang).astype(dt)
    sin = jnp.sin(ang).astype(dt)
    for i in range(DEPTH):
        kind = i % N_MIXERS
        j = i // N_MIXERS
        if kind == 0:
            m = _mla_mixer(h, cos, sin, mla_w_in[j], mla_q_norm[j], mla_w_q_up[j],
                           mla_kv_norm[j], mla_w_kv_up[j], mla_w_o[j])
        elif kind == 1:
            m = _rwkv7_mixer(h, rw_mu[j], rw_w_rkv[j], rw_w0[j], rw_w1[j], rw_w2[j],
                             rw_a0[j], rw_a1[j], rw_a2[j], rw_g1[j], rw_g2[j],
                             rw_k_k[j], rw_k_a[j], rw_r_k[j], rw_gn_g[j], rw_gn_b[j], rw_w_o[j])
        else:
            m = _rglru_mixer(h, lru_w_in[j], lru_conv_w[j], lru_conv_b[j], lru_gate_w[j],
                             lru_gate_b[j], lru_lambda[j], lru_w_o[j])
        h = _layer_norm(DN_ALPHA * h + m, ln_g[i, 0], ln_b[i, 0])
        f = _conv_glu_ffn(h, ffn_w_in[i], ffn_conv_w[i], ffn_conv_b[i], ffn_w_out[i])
        h = _layer_norm(DN_ALPHA * h + f, ln_g[i, 1], ln_b[i, 1])
    return h[:, N_META:]
```

```python
import numpy as np
import concourse.bass as bass
import concourse.mybir as mybir
from concourse.bass_utils import run_bass_kernel_spmd
from contextlib import ExitStack

F32 = mybir.dt.float32
BF16 = mybir.dt.bfloat16
AF = mybir.ActivationFunctionType
ALU = mybir.AluOpType
SEM_LIM = 30000

D = 1024
DFF = 2816
ALPHA = 8.0 ** 0.25
LN_EPS = 1e-5


class KB:
    def __init__(self):
        self.nc = bass.Bass("TRN2", target_bir_lowering=False)
        self.es = ExitStack()
        nc = self.nc
        self.eng = {"pe": nc.tensor, "act": nc.scalar, "dve": nc.vector, "pool": nc.gpsimd, "sp": nc.sync}
        self.sem, self.cnt, self.cur = {}, {}, {}
        self.nsem = 0
        self.seen = {e: {} for e in self.eng}
        self.wr, self.rd = {}, {}
        self.ndma = 16
        self.dma_rr = 0
        for e in ["pe", "act", "dve", "pool"]:
            self._newsem(e)
        for j in range(self.ndma):
            self._newsem("d%d" % j)
        self.ps = [self.es.enter_context(nc.psum_tensor("ps%d" % i, [128, 512], F32)) for i in range(8)]
        self.ninst = 0

    def _newsem(self, stream):
        key = "%s#%d" % (stream, self.nsem)
        self.nsem += 1
        self.sem[key] = self.es.enter_context(self.nc.semaphore("s%d" % self.nsem))
        self.cnt[key] = 0
        self.cur[stream] = key
        return key

    def din(self, name, shape, dt=F32):
        return self.nc.dram_tensor(name, list(shape), dt, kind="ExternalInput").ap()

    def dout(self, name, shape, dt=F32):
        return self.nc.dram_tensor(name, list(shape), dt, kind="ExternalOutput").ap()

    def sb(self, name, shape, dt=F32):
        return self.es.enter_context(self.nc.sbuf_tensor(name, list(shape), dt))

    def _wait(self, e, key, val):
        if val <= 0 or self.seen[e].get(key, 0) >= val:
            return
        self.eng[e].wait_ge(self.sem[key], val)
        self.seen[e][key] = val

    def op(self, e, fn, reads=(), writes=(), dma=False):
        own = None if e != "pe" else "pe#"
        for b in reads:
            for k, v in self.wr.get(b, {}).items():
                if own and k.startswith(own):
                    continue
                self._wait(e, k, v)
        for b in writes:
            for k, v in self.rd.get(b, {}).items():
                if own and k.startswith(own):
                    continue
                self._wait(e, k, v)
            if self.rd.get(b):
                for k, v in self.wr.get(b, {}).items():
                    if own and k.startswith(own):
                        continue
                    self._wait(e, k, v)
        if dma:
            stream = "d%d" % self.dma_rr
            self.dma_rr = (self.dma_rr + 1) % self.ndma
            inc = 16
            self._wait(e, self.cur[stream], self.cnt[self.cur[stream]])
        else:
            stream = e
            inc = 1
        key = self.cur[stream]
        if self.cnt[key] + inc > SEM_LIM:
            key = self._newsem(stream)
        ins = fn()
        ins.then_inc(self.sem[key], inc)
        self.cnt[key] += inc
        val = self.cnt[key]
        self.ninst += 1
        for b in reads:
            d = self.rd.setdefault(b, {})
            d[key] = max(d.get(key, 0), val)
        for b in writes:
            if self.rd.get(b):
                self.wr[b] = {key: val}
                self.rd[b] = {}
            else:
                self.wr.setdefault(b, {})[key] = val
                self.rd.setdefault(b, {})
        return ins

    def dma(self, out, in_, reads=(), writes=(), e="sp"):
        return self.op(e, lambda: self.eng[e].dma_start(out=out, in_=in_), reads=reads, writes=writes, dma=True)

    def barrier(self):
        for e in self.eng:
            for k, v in self.cnt.items():
                self._wait(e, k, v)

    def finish(self):
        for e in self.eng:
            for k, v in self.cnt.items():
                if e == "pe" and k.startswith("pe#"):
                    continue
                self._wait(e, k, v)
        self.es.close()
        return self.nc

    def load_w_bf16(self, dst, dst_name, src, K, N, nblk=2048):
        sv = src.rearrange("(c p) n -> p c n", p=128)
        for c in range(K // 128):
            for n0 in range(0, N, nblk):
                n1 = min(N, n0 + nblk)
                self.dma(dst[:, c, n0:n1], sv[:, c, n0:n1], writes=[dst_name], e="pool")

    def layernorm_inplace(self, s, sname, n, g, b, scr, ones, eps_t, ps_sum=6, ps_sq=7, out_bf=None, out_bf_name=None):
        nc = self.nc
        psum, pssq = self.ps[ps_sum], self.ps[ps_sq]
        pn1, pn2 = "ps%d" % ps_sum, "ps%d" % ps_sq
        for c in range(8):
            self.op("pe", lambda: nc.tensor.matmul(psum[:, :n], lhsT=ones[:], rhs=s[:, c, :n], start=(c == 0), stop=(c == 7)),
                    reads=[sname, "ones"], writes=[pn1])
        for c in range(8):
            sq = scr["sq"][c % 2]
            sqn = "sq%d" % (c % 2)
            self.op("act", lambda: nc.scalar.activation(out=sq[:, :n], in_=s[:, c, :n], func=AF.Square), reads=[sname], writes=[sqn])
            self.op("pe", lambda: nc.tensor.matmul(pssq[:, :n], lhsT=ones[:], rhs=sq[:, :n], start=(c == 0), stop=(c == 7)),
                    reads=[sqn, "ones"], writes=[pn2])
        mean, msq, var, rstd = scr["mean"], scr["msq"], scr["var"], scr["rstd"]
        self.op("act", lambda: nc.scalar.activation(out=mean[:, :n], in_=psum[:, :n], func=AF.Copy, scale=1.0 / D), reads=[pn1], writes=["mean"])
        self.op("dve", lambda: nc.vector.tensor_tensor(out=msq[:, :n], in0=mean[:, :n], in1=mean[:, :n], op=ALU.mult), reads=["mean"], writes=["msq"])
        self.op("dve", lambda: nc.vector.scalar_tensor_tensor(out=var[:, :n], in0=pssq[:, :n], scalar=1.0 / D, in1=msq[:, :n],
                                                              op0=ALU.mult, op1=ALU.subtract), reads=[pn2, "msq"], writes=["var"])
        self.op("act", lambda: nc.scalar.activation(out=var[:, :n], in_=var[:, :n], func=AF.Sqrt, bias=eps_t[:, 0:1], scale=1.0),
                reads=["var", "eps"], writes=["var"])
        self.op("dve", lambda: nc.vector.reciprocal(out=rstd[:, :n], in_=var[:, :n]), reads=["var"], writes=["rstd"])
        for c in range(8):
            self.op("dve", lambda: nc.vector.tensor_tensor(out=s[:, c, :n], in0=s[:, c, :n], in1=mean[:, :n], op=ALU.subtract),
                    reads=[sname, "mean"], writes=[sname])
            self.op("pool", lambda: nc.gpsimd.tensor_tensor(out=s[:, c, :n], in0=s[:, c, :n], in1=rstd[:, :n], op=ALU.mult),
                    reads=[sname, "rstd"], writes=[sname])
            self.op("act", lambda: nc.scalar.activation(out=s[:, c, :n], in_=s[:, c, :n], func=AF.Identity,
                                                        scale=g[:, c:c + 1], bias=b[:, c:c + 1]),
                    reads=[sname, "lnp"], writes=[sname])
            if out_bf is not None:
                self.op("pool", lambda: nc.gpsimd.tensor_copy(out=out_bf[:, c, :n], in_=s[:, c, :n]), reads=[sname], writes=[out_bf_name])

    def ln_scratch(self, n):
        scr = {"sq": [self.sb("sq0", [128, n]), self.sb("sq1", [128, n])]}
        for nm in ["mean", "msq", "var", "rstd"]:
            scr[nm] = self.sb(nm, [128, n])
        ones = self.sb("ones", [128, 128])
        eps_t = self.sb("eps", [128, 1])
        self.op("pool", lambda: self.nc.gpsimd.memset(ones[:], 1.0), writes=["ones"])
        self.op("pool", lambda: self.nc.gpsimd.memset(eps_t[:], LN_EPS), writes=["eps"])
        return scr, ones, eps_t


def build_ffn(ntok, TW):
    kb = KB()
    nc = kb.nc
    ntile = ntok // TW
    assert ntile * TW == ntok
    W = TW + 2
    NG = DFF // 128
    xT = kb.din("xT", [D, ntok + 2])
    edge = kb.din("edge", [128, 2])
    w_in = kb.din("w_in", [D, 2 * DFF])
    cw = kb.din("cw", [128, NG * 3])
    cb = kb.din("cb", [128, NG])
    w_out = kb.din("w_out", [DFF, D])
    lng = kb.din("lng", [128, 8])
    lnb = kb.din("lnb", [128, 8])
    yT = kb.dout("yT", [D, ntok])

    w_in_sb = kb.sb("w_in_sb", [128, 8, 2 * DFF], BF16)
    w_out_sb = kb.sb("w_out_sb", [128, NG, D], BF16)
    cw_sb = kb.sb("cw_sb", [128, NG * 3])
    cb_sb = kb.sb("cb_sb", [128, NG])
    g_sb = kb.sb("g_sb", [128, 8])
    b_sb = kb.sb("b_sb", [128, 8])
    edge_sb = kb.sb("edge_sb", [128, 2])
    xt = [kb.sb("x%d" % i, [128, 8, W]) for i in range(2)]
    xb = kb.sb("xb", [128, 8, W], BF16)
    ab = kb.sb("ab", [128, NG, TW], BF16)
    s2 = [kb.sb("s2_%d" % i, [128, 8, TW]) for i in range(2)]
    gs = [kb.sb("gs%d" % i, [128, W]) for i in range(2)]
    acc = [kb.sb("acc%d" % i, [128, TW]) for i in range(2)]
    sg = [kb.sb("sg%d" % i, [128, TW]) for i in range(2)]
    scr, ones, eps_t = kb.ln_scratch(TW)

    kb.dma(cw_sb[:], cw, writes=["cw"])
    kb.dma(cb_sb[:], cb, writes=["cb"])
    kb.dma(g_sb[:], lng, writes=["lnp"])
    kb.dma(b_sb[:], lnb, writes=["lnp"])
    kb.dma(edge_sb[:], edge, writes=["edge"])
    xv = xT.rearrange("(c p) t -> p c t", p=128)
    yv = yT.rearrange("(c p) t -> p c t", p=128)
    kb.dma(xt[0][:, :, :], xv[:, :, 0:W], writes=["x0"])
    kb.load_w_bf16(w_in_sb, "w_in", w_in, D, 2 * DFF)
    kb.load_w_bf16(w_out_sb, "w_out", w_out, DFF, D, nblk=1024)

    for i in range(ntile):
        t0 = i * TW
        x = xt[i % 2]
        xn = "x%d" % (i % 2)
        if i + 1 < ntile:
            kb.dma(xt[(i + 1) % 2][:, :, :], xv[:, :, t0 + TW:t0 + TW + W], writes=["x%d" % ((i + 1) % 2)])
        kb.op("pool", lambda: nc.gpsimd.tensor_copy(out=xb[:, :, :], in_=x[:, :, :]), reads=[xn], writes=["xb"])
        for gc in range(NG):
            bg, bu = gc % 2, 2 + gc % 2
            pg, pu = kb.ps[bg], kb.ps[bu]
            for kc in range(8):
                kb.op("pe", lambda: nc.tensor.matmul(pg[:, :W], lhsT=w_in_sb[:, kc, gc * 128:(gc + 1) * 128], rhs=xb[:, kc, :],
                                                     start=(kc == 0), stop=(kc == 7)), reads=["w_in", "xb"], writes=["ps%d" % bg])
            for kc in range(8):
                kb.op("pe", lambda: nc.tensor.matmul(pu[:, :W], lhsT=w_in_sb[:, kc, DFF + gc * 128:DFF + (gc + 1) * 128], rhs=xb[:, kc, :],
                                                     start=(kc == 0), stop=(kc == 7)), reads=["w_in", "xb"], writes=["ps%d" % bu])
            g_, a_, s_ = gs[gc % 2], acc[gc % 2], sg[gc % 2]
            gn, an, sn = "gs%d" % (gc % 2), "acc%d" % (gc % 2), "sg%d" % (gc % 2)
            kb.op("act", lambda: nc.scalar.copy(out=g_[:, :W], in_=pg[:, :W]), reads=["ps%d" % bg], writes=[gn])
            if i == 0:
                kb.op("dve", lambda: nc.vector.tensor_scalar(out=g_[:, 0:1], in0=g_[:, 0:1], scalar1=edge_sb[:, 0:1], scalar2=None, op0=ALU.mult),
                      reads=[gn, "edge"], writes=[gn])
            if i == ntile - 1:
                kb.op("dve", lambda: nc.vector.tensor_scalar(out=g_[:, W - 1:W], in0=g_[:, W - 1:W], scalar1=edge_sb[:, 1:2], scalar2=None, op0=ALU.mult),
                      reads=[gn, "edge"], writes=[gn])
            kb.op("dve", lambda: nc.vector.tensor_scalar(out=a_[:, :], in0=g_[:, 0:TW], scalar1=cw_sb[:, gc * 3:gc * 3 + 1], scalar2=None, op0=ALU.mult),
                  reads=[gn, "cw"], writes=[an])
            kb.op("dve", lambda: nc.vector.scalar_tensor_tensor(out=a_[:, :], in0=g_[:, 1:TW + 1], scalar=cw_sb[:, gc * 3 + 1:gc * 3 + 2], in1=a_[:, :],
                                                                op0=ALU.mult, op1=ALU.add), reads=[gn, "cw", an], writes=[an])
            kb.op("dve", lambda: nc.vector.scalar_tensor_tensor(out=a_[:, :], in0=g_[:, 2:TW + 2], scalar=cw_sb[:, gc * 3 + 2:gc * 3 + 3], in1=a_[:, :],
                                                                op0=ALU.mult, op1=ALU.add), reads=[gn, "cw", an], writes=[an])
            kb.op("act", lambda: nc.scalar.activation(out=s_[:, :], in_=a_[:, :], func=AF.Silu, bias=cb_sb[:, gc:gc + 1], scale=1.0),
                  reads=[an, "cb"], writes=[sn])
            kb.op("dve", lambda: nc.vector.tensor_tensor(out=ab[:, gc, :], in0=s_[:, :], in1=pu[:, 1:TW + 1], op=ALU.mult),
                  reads=[sn, "ps%d" % bu], writes=["ab"])
        s = s2[i % 2]
        sname = "s2_%d" % (i % 2)
        for oc in range(8):
            bf = 4 + oc % 2
            pf = kb.ps[bf]
            for kc in range(NG):
                kb.op("pe", lambda: nc.tensor.matmul(pf[:, :TW], lhsT=w_out_sb[:, kc, oc * 128:(oc + 1) * 128], rhs=ab[:, kc, :],
                                                     start=(kc == 0), stop=(kc == NG - 1)), reads=["w_out", "ab"], writes=["ps%d" % bf])
            kb.op("dve", lambda: nc.vector.scalar_tensor_tensor(out=s[:, oc, :], in0=x[:, oc, 1:TW + 1], scalar=ALPHA, in1=pf[:, :TW],
                                                                op0=ALU.mult, op1=ALU.add), reads=[xn, "ps%d" % bf], writes=[sname])
        kb.layernorm_inplace(s, sname, TW, g_sb, b_sb, scr, ones, eps_t)
        kb.dma(yv[:, :, t0:t0 + TW], s[:, :, :], reads=[sname])
    return kb.finish()


def _pcol(v):
    v = np.asarray(v)
    return np.ascontiguousarray(v.reshape(-1, 128).T)


def build_proj(ntok, TW, Dz):
    kb = KB()
    nc = kb.nc
    ntile = ntok // TW
    assert ntile * TW == ntok
    NZ = Dz // 128
    zT = kb.din("zT", [Dz, ntok])
    hT = kb.din("hT", [D, ntok])
    w_o = kb.din("w_o", [Dz, D])
    lng = kb.din("lng", [128, 8])
    lnb = kb.din("lnb", [128, 8])
    yT = kb.dout("yT", [D, ntok])
    w_sb = kb.sb("w_sb", [128, NZ, D], BF16)
    g_sb = kb.sb("g_sb", [128, 8])
    b_sb = kb.sb("b_sb", [128, 8])
    zb = [kb.sb("zb%d" % i, [128, NZ, TW], BF16) for i in range(2)]
    ht = [kb.sb("h%d" % i, [128, 8, TW]) for i in range(2)]
    s2 = [kb.sb("s2_%d" % i, [128, 8, TW]) for i in range(2)]
    scr, ones, eps_t = kb.ln_scratch(TW)
    kb.dma(g_sb[:], lng, writes=["lnp"])
    kb.dma(b_sb[:], lnb, writes=["lnp"])
    zv = zT.rearrange("(c p) t -> p c t", p=128)
    hv = hT.rearrange("(c p) t -> p c t", p=128)
    yv = yT.rearrange("(c p) t -> p c t", p=128)
    kb.load_w_bf16(w_sb, "w_o", w_o, Dz, D, nblk=1024)
    for i in range(ntile):
        t0 = i * TW
        z, zn = zb[i % 2], "zb%d" % (i % 2)
        h, hn = ht[i % 2], "h%d" % (i % 2)
        s, sname = s2[i % 2], "s2_%d" % (i % 2)
        for c in range(NZ):
            kb.dma(z[:, c, :], zv[:, c, t0:t0 + TW], writes=[zn], e="pool")
        kb.dma(h[:, :, :], hv[:, :, t0:t0 + TW], writes=[hn])
        for oc in range(8):
            bf = 4 + oc % 2
            pf = kb.ps[bf]
            for kc in range(NZ):
                kb.op("pe", lambda: nc.tensor.matmul(pf[:, :TW], lhsT=w_sb[:, kc, oc * 128:(oc + 1) * 128], rhs=z[:, kc, :],
                                                     start=(kc == 0), stop=(kc == NZ - 1)), reads=["w_o", zn], writes=["ps%d" % bf])
            kb.op("dve", lambda: nc.vector.scalar_tensor_tensor(out=s[:, oc, :], in0=h[:, oc, :], scalar=ALPHA, in1=pf[:, :TW],
                                                                op0=ALU.mult, op1=ALU.add), reads=[hn, "ps%d" % bf], writes=[sname])
        kb.layernorm_inplace(s, sname, TW, g_sb, b_sb, scr, ones, eps_t)
        kb.dma(yv[:, :, t0:t0 + TW], s[:, :, :], reads=[sname])
    return kb.finish()


I32 = mybir.dt.int32
TWO_PI = 6.283185307179586
CW1 = 6.28125
CW2 = TWO_PI - CW1
MAGIC = 12582912.0
PI_LO = 3.1415925
RMS_EPS = 1e-6
QR, KVR, ROPE, NOPE = 256, 128, 64, 128
MLA_IN = QR + KVR + ROPE


def build_mla(T, nq, QW, H):
    kb = KB()
    nc = kb.nc
    SCALE = float((NOPE + ROPE) ** -0.5)
    ktl = [(t0, min(512, T - t0)) for t0 in range(0, T, 512)]
    kts = [(t0, min(128, T - t0)) for t0 in range(0, T, 128)]
    NKT = len(kts)
    nqt = nq // QW
    assert nqt * QW == nq
    hT = kb.din("hT", [D, T])
    hqT = kb.din("hqT", [D, nq])
    posk = kb.din("posk", [64, T], I32)
    offk = kb.din("offk", [64, T])
    posq = kb.din("posq", [64, nq], I32)
    offq = kb.din("offq", [64, nq])
    rconst = kb.din("rconst", [64, 3])
    w_in = kb.din("w_in", [D, MLA_IN])
    w_pesw = kb.din("w_pesw", [D, ROPE])
    qn_g = kb.din("qn_g", [128, 2])
    kvn_g = kb.din("kvn_g", [128, 1])
    kvn_rep = kb.din("kvn_rep", [128, 128])
    w_q = kb.din("w_q", [QR, H * 192])
    w_qsw = kb.din("w_qsw", [QR, H * 64])
    w_ukT = kb.din("w_ukT", [128, H * 128])
    w_uv = kb.din("w_uv", [128, H * 128])
    oT = kb.dout("oT", [H * 128, nq])

    w_in_sb = kb.sb("w_in_sb", [128, 8, MLA_IN], BF16)
    w_pesw_sb = kb.sb("w_pesw_sb", [128, 8, ROPE], BF16)
    w_q_sb = kb.sb("w_q_sb", [128, 2, H * 192], BF16)
    w_qsw_sb = kb.sb("w_qsw_sb", [128, 2, H * 64], BF16)
    w_ukT_sb = kb.sb("w_ukT_sb", [128, 1, H * 128], BF16)
    w_uv_sb = kb.sb("w_uv_sb", [128, 1, H * 128], BF16)
    qn_sb = kb.sb("qn_sb", [128, 2])
    kvn_sb = kb.sb("kvn_sb", [128, 1])
    kvn_rep_sb = kb.sb("kvn_rep_sb", [128, 128])
    rc_sb = kb.sb("rc_sb", [64, 3])
    ckvT = kb.sb("ckvT", [128, T], BF16)
    kpeT = kb.sb("kpeT", [64, T], BF16)
    ckvK = kb.sb("ckvK", [128, NKT, 128], BF16)
    hb = [kb.sb("hb%d" % i, [128, 8, 512], BF16) for i in range(2)]
    ones = kb.sb("ones", [128, 128])
    ones_b = kb.sb("ones_b", [128, 128], BF16)
    eps_t = kb.sb("eps", [128, 1])
    sq = kb.sb("sq", [128, 512])
    rs = kb.sb("rs", [128, 512])
    rs1 = kb.sb("rs1", [128, 1])
    junk = kb.sb("junk", [128, 128])
    pos_i = kb.sb("pos_i", [64, 512], I32)
    off_t = kb.sb("off_t", [64, 512])
    ang = kb.sb("ang", [64, 512])
    nr = kb.sb("nr", [64, 512])
    rr = kb.sb("rr", [64, 512])
    rp = kb.sb("rp", [64, 512])
    CC = kb.sb("CC", [64, 512])
    SS = kb.sb("SS", [64, 512])
    tmp1 = kb.sb("tmp1", [64, 512])
    tmp2 = kb.sb("tmp2", [64, 512])
    cqb = kb.sb("cqb", [128, 2, QW], BF16)
    qnb = kb.sb("qnb", [128, QW], BF16)
    qlat = [kb.sb("qlat%d" % i, [128, QW], BF16) for i in range(2)]
    qpe = [kb.sb("qpe%d" % i, [64, QW], BF16) for i in range(2)]
    pT = [kb.sb("pT%d" % i, [128, QW], BF16) for i in range(3)]
    den = kb.sb("den", [128, QW])
    accA = kb.sb("accA", [128, QW])
    accB = kb.sb("accB", [128, QW])
    olat = kb.sb("olat", [128, QW], BF16)
    oh = [kb.sb("oh%d" % i, [128, QW]) for i in range(2)]

    kb.op("pool", lambda: nc.gpsimd.memset(ones[:], 1.0), writes=["ones"])
    kb.op("pool", lambda: nc.gpsimd.memset(ones_b[:], 1.0), writes=["ones_b"])
    kb.op("pool", lambda: nc.gpsimd.memset(eps_t[:], RMS_EPS), writes=["eps"])
    kb.dma(qn_sb[:], qn_g, writes=["par"])
    kb.dma(kvn_sb[:], kvn_g, writes=["par"])
    kb.dma(kvn_rep_sb[:], kvn_rep, writes=["par"])
    kb.dma(rc_sb[:], rconst, writes=["par"])
    kb.load_w_bf16(w_in_sb, "w", w_in, D, MLA_IN)
    kb.load_w_bf16(w_pesw_sb, "w", w_pesw, D, ROPE)
    kb.load_w_bf16(w_q_sb, "w", w_q, QR, H * 192)
    kb.load_w_bf16(w_qsw_sb, "w", w_qsw, QR, H * 64)
    kb.load_w_bf16(w_ukT_sb, "w", w_ukT, 128, H * 128)
    kb.load_w_bf16(w_uv_sb, "w", w_uv, 128, H * 128)

    def rope_tables(pos_ap, off_ap, n):
        kb.dma(pos_i[:, :n], pos_ap, writes=["pos_i"])
        kb.dma(off_t[:, :n], off_ap, writes=["off_t"])
        kb.op("dve", lambda: nc.vector.tensor_copy(out=ang[:, :n], in_=pos_i[:, :n]), reads=["pos_i"], writes=["ang"])
        kb.op("dve", lambda: nc.vector.tensor_tensor(out=ang[:, :n], in0=ang[:, :n], in1=off_t[:, :n], op=ALU.add), reads=["ang", "off_t"], writes=["ang"])
        kb.op("dve", lambda: nc.vector.tensor_scalar(out=ang[:, :n], in0=ang[:, :n], scalar1=rc_sb[:, 0:1], scalar2=None, op0=ALU.mult),
              reads=["ang", "par"], writes=["ang"])
        kb.op("dve", lambda: nc.vector.tensor_scalar(out=nr[:, :n], in0=ang[:, :n], scalar1=1.0 / TWO_PI, scalar2=MAGIC, op0=ALU.mult, op1=ALU.add),
              reads=["ang"], writes=["nr"])
        kb.op("dve", lambda: nc.vector.tensor_scalar(out=nr[:, :n], in0=nr[:, :n], scalar1=-MAGIC, scalar2=None, op0=ALU.add), reads=["nr"], writes=["nr"])
        kb.op("dve", lambda: nc.vector.scalar_tensor_tensor(out=rr[:, :n], in0=nr[:, :n], scalar=-CW1, in1=ang[:, :n], op0=ALU.mult, op1=ALU.add),
              reads=["nr", "ang"], writes=["rr"])
        kb.op("dve", lambda: nc.vector.scalar_tensor_tensor(out=rr[:, :n], in0=nr[:, :n], scalar=-CW2, in1=rr[:, :n], op0=ALU.mult, op1=ALU.add),
              reads=["nr", "rr"], writes=["rr"])
        for tab, tn, col in ((CC, "CC", 1), (SS, "SS", 2)):
            kb.op("dve", lambda: nc.vector.tensor_scalar(out=rp[:, :n], in0=rr[:, :n], scalar1=rc_sb[:, col:col + 1], scalar2=None, op0=ALU.add),
                  reads=["rr", "par"], writes=["rp"])
            kb.op("dve", lambda: nc.vector.tensor_scalar(out=nr[:, :n], in0=rp[:, :n], scalar1=1.0 / TWO_PI, scalar2=MAGIC, op0=ALU.mult, op1=ALU.add),
                  reads=["rp"], writes=["nr"])
            kb.op("dve", lambda: nc.vector.tensor_scalar(out=nr[:, :n], in0=nr[:, :n], scalar1=-MAGIC, scalar2=None, op0=ALU.add), reads=["nr"], writes=["nr"])
            kb.op("dve", lambda: nc.vector.scalar_tensor_tensor(out=rp[:, :n], in0=nr[:, :n], scalar=-TWO_PI, in1=rp[:, :n], op0=ALU.mult, op1=ALU.add),
                  reads=["nr", "rp"], writes=["rp"])
            kb.op("dve", lambda: nc.vector.tensor_scalar(out=rp[:, :n], in0=rp[:, :n], scalar1=-PI_LO, scalar2=PI_LO, op0=ALU.max, op1=ALU.min),
                  reads=["rp"], writes=["rp"])
            kb.op("act", lambda: nc.scalar.activation(out=tab[:, :n], in_=rp[:, :n], func=AF.Sin), reads=["rp"], writes=[tn])

    def rope_apply(dst, dst_name, p_pre, pn_pre, p_sw, pn_sw, n):
        kb.op("dve", lambda: nc.vector.tensor_tensor(out=tmp1[:, :n], in0=p_pre[0:64, :n], in1=CC[:, :n], op=ALU.mult), reads=[pn_pre, "CC"], writes=["tmp1"])
        kb.op("dve", lambda: nc.vector.tensor_tensor(out=tmp2[:, :n], in0=p_sw[0:64, :n], in1=SS[:, :n], op=ALU.mult), reads=[pn_sw, "SS"], writes=["tmp2"])
        kb.op("pool", lambda: nc.gpsimd.tensor_tensor(out=dst, in0=tmp1[:, :n], in1=tmp2[:, :n], op=ALU.add), reads=["tmp1", "tmp2"], writes=[dst_name])

    hv = hT.rearrange("(c p) t -> p c t", p=128)
    hqv = hqT.rearrange("(c p) t -> p c t", p=128)

    def load_hb(i, src, t0, n):
        for c in range(8):
            kb.dma(hb[i][:, c, :n], src[:, c, t0:t0 + n], writes=["hb%d" % i], e="pool")

    load_hb(0, hv, ktl[0][0], ktl[0][1])
    for i, (t0, n) in enumerate(ktl):
        x, xn = hb[i % 2], "hb%d" % (i % 2)
        if i + 1 < len(ktl):
            load_hb((i + 1) % 2, hv, ktl[i + 1][0], ktl[i + 1][1])
        rope_tables(posk[:, t0:t0 + n], offk[:, t0:t0 + n], n)
        p0, p1, p2, p7 = kb.ps[4], kb.ps[5], kb.ps[6], kb.ps[7]
        for kc in range(8):
            kb.op("pe", lambda: nc.tensor.matmul(p0[:, :n], lhsT=w_in_sb[:, kc, QR:QR + KVR], rhs=x[:, kc, :n], start=(kc == 0), stop=(kc == 7)),
                  reads=["w", xn], writes=["ps4"])
        for kc in range(8):
            kb.op("pe", lambda: nc.tensor.matmul(p1[0:64, :n], lhsT=w_in_sb[:, kc, QR + KVR:MLA_IN], rhs=x[:, kc, :n], start=(kc == 0), stop=(kc == 7)),
                  reads=["w", xn], writes=["ps5"])
        for kc in range(8):
            kb.op("pe", lambda: nc.tensor.matmul(p2[0:64, :n], lhsT=w_pesw_sb[:, kc, :], rhs=x[:, kc, :n], start=(kc == 0), stop=(kc == 7)),
                  reads=["w", xn], writes=["ps6"])
        kb.op("act", lambda: nc.scalar.activation(out=sq[:, :n], in_=p0[:, :n], func=AF.Square), reads=["ps4"], writes=["sq"])
        kb.op("pe", lambda: nc.tensor.matmul(p7[:, :n], lhsT=ones[:], rhs=sq[:, :n], start=True, stop=True), reads=["ones", "sq"], writes=["ps7"])
        kb.op("act", lambda: nc.scalar.activation(out=rs[:, :n], in_=p7[:, :n], func=AF.Sqrt, bias=eps_t[:, 0:1], scale=1.0 / KVR), reads=["ps7", "eps"], writes=["rs"])
        kb.op("dve", lambda: nc.vector.reciprocal(out=rs[:, :n], in_=rs[:, :n]), reads=["rs"], writes=["rs"])
        kb.op("dve", lambda: nc.vector.scalar_tensor_tensor(out=ckvT[:, t0:t0 + n], in0=p0[:, :n], scalar=kvn_sb[:, 0:1], in1=rs[:, :n],
                                                            op0=ALU.mult, op1=ALU.mult), reads=["ps4", "par", "rs"], writes=["ckvT"])
        rope_apply(kpeT[:, t0:t0 + n], "kpeT", p1, "ps5", p2, "ps6", n)
        for j0 in range(0, n, 128):
            jn = min(128, n - j0)
            j = (t0 + j0) // 128
            pk = kb.ps[j % 2]
            pkn = "ps%d" % (j % 2)
            for kc in range(8):
                kb.op("pe", lambda: nc.tensor.matmul(pk[:jn, :128], lhsT=x[:, kc, j0:j0 + jn], rhs=w_in_sb[:, kc, QR:QR + KVR], start=(kc == 0), stop=(kc == 7)),
                      reads=["w", xn], writes=[pkn])
            kb.op("act", lambda: nc.scalar.activation(out=junk[:jn, :], in_=pk[:jn, :128], func=AF.Square, accum_out=rs1[:jn, 0:1]),
                  reads=[pkn], writes=["junk", "rs1"])
            kb.op("act", lambda: nc.scalar.activation(out=rs1[:jn, :], in_=rs1[:jn, :], func=AF.Sqrt, bias=eps_t[:jn, 0:1], scale=1.0 / KVR),
                  reads=["rs1", "eps"], writes=["rs1"])
            kb.op("dve", lambda: nc.vector.reciprocal(out=rs1[:jn, :], in_=rs1[:jn, :]), reads=["rs1"], writes=["rs1"])
            kb.op("dve", lambda: nc.vector.scalar_tensor_tensor(out=ckvK[:jn, j, :], in0=pk[:jn, :128], scalar=rs1[:jn, 0:1], in1=kvn_rep_sb[:jn, :],
                                                                op0=ALU.mult, op1=ALU.mult), reads=[pkn, "rs1", "par"], writes=["ckvK"])

    ov = oT.rearrange("(h p) t -> p h t", p=128)
    hcount = 0
    for qi in range(nqt):
        q0 = qi * QW
        load_hb(0, hqv, q0, QW)
        x, xn = hb[0], "hb0"
        rope_tables(posq[:, q0:q0 + QW], offq[:, q0:q0 + QW], QW)
        p4, p5, p7 = kb.ps[4], kb.ps[5], kb.ps[7]
        for c, (pp, ppn) in enumerate(((p4, "ps4"), (p5, "ps5"))):
            for kc in range(8):
                kb.op("pe", lambda: nc.tensor.matmul(pp[:, :QW], lhsT=w_in_sb[:, kc, c * 128:(c + 1) * 128], rhs=x[:, kc, :QW], start=(kc == 0), stop=(kc == 7)),
                      reads=["w", xn], writes=[ppn])
        for c, (pp, ppn) in enumerate(((p4, "ps4"), (p5, "ps5"))):
            kb.op("act", lambda: nc.scalar.activation(out=sq[:, :QW], in_=pp[:, :QW], func=AF.Square), reads=[ppn], writes=["sq"])
            kb.op("pe", lambda: nc.tensor.matmul(p7[:, :QW], lhsT=ones[:], rhs=sq[:, :QW], start=(c == 0), stop=(c == 1)), reads=["ones", "sq"], writes=["ps7"])
        kb.op("act", lambda: nc.scalar.activation(out=rs[:, :QW], in_=p7[:, :QW], func=AF.Sqrt, bias=eps_t[:, 0:1], scale=1.0 / QR), reads=["ps7", "eps"], writes=["rs"])
        kb.op("dve", lambda: nc.vector.reciprocal(out=rs[:, :QW], in_=rs[:, :QW]), reads=["rs"], writes=["rs"])
        for c, (pp, ppn) in enumerate(((p4, "ps4"), (p5, "ps5"))):
            kb.op("dve", lambda: nc.vector.scalar_tensor_tensor(out=cqb[:, c, :], in0=pp[:, :QW], scalar=qn_sb[:, c:c + 1], in1=rs[:, :QW],
                                                                op0=ALU.mult, op1=ALU.mult), reads=[ppn, "par", "rs"], writes=["cqb"])
        for h in range(H):
            ql, qln = qlat[hcount % 2], "qlat%d" % (hcount % 2)
            qp, qpn = qpe[hcount % 2], "qpe%d" % (hcount % 2)
            o_sb, o_n = oh[hcount % 2], "oh%d" % (hcount % 2)
            hcount += 1
            p4, p5, p6 = kb.ps[4], kb.ps[5], kb.ps[6]
            for kc in range(2):
                kb.op("pe", lambda: nc.tensor.matmul(p4[:, :QW], lhsT=w_q_sb[:, kc, h * 192:h * 192 + 128], rhs=cqb[:, kc, :], start=(kc == 0), stop=(kc == 1)),
                      reads=["w", "cqb"], writes=["ps4"])
            kb.op("act", lambda: nc.scalar.copy(out=qnb[:, :], in_=p4[:, :QW]), reads=["ps4"], writes=["qnb"])
            for kc in range(2):
                kb.op("pe", lambda: nc.tensor.matmul(p5[0:64, :QW], lhsT=w_q_sb[:, kc, h * 192 + 128:h * 192 + 192], rhs=cqb[:, kc, :], start=(kc == 0), stop=(kc == 1)),
                      reads=["w", "cqb"], writes=["ps5"])
            for kc in range(2):
                kb.op("pe", lambda: nc.tensor.matmul(p6[0:64, :QW], lhsT=w_qsw_sb[:, kc, h * 64:h * 64 + 64], rhs=cqb[:, kc, :], start=(kc == 0), stop=(kc == 1)),
                      reads=["w", "cqb"], writes=["ps6"])
            rope_apply(qp[:, :], qpn, p5, "ps5", p6, "ps6", QW)
            kb.op("pe", lambda: nc.tensor.matmul(p4[:, :QW], lhsT=w_ukT_sb[:, 0, h * 128:(h + 1) * 128], rhs=qnb[:, :], start=True, stop=True),
                  reads=["w", "qnb"], writes=["ps4"])
            kb.op("act", lambda: nc.scalar.copy(out=ql[:, :], in_=p4[:, :QW]), reads=["ps4"], writes=[qln])
            pO, pD = kb.ps[2], kb.ps[3]

            def qk(j):
                k0, kn = kts[j]
                ps_, psn = kb.ps[j % 2], "ps%d" % (j % 2)
                kb.op("pe", lambda: nc.tensor.matmul(ps_[:kn, :QW], lhsT=ckvT[:, k0:k0 + kn], rhs=ql[:, :], start=True, stop=False),
                      reads=["ckvT", qln], writes=[psn])
                kb.op("pe", lambda: nc.tensor.matmul(ps_[:kn, :QW], lhsT=kpeT[:, k0:k0 + kn], rhs=qp[:, :], start=False, stop=True),
                      reads=["kpeT", qpn], writes=[psn])

            qk(0)
            for j in range(NKT):
                k0, kn = kts[j]
                ps_, psn = kb.ps[j % 2], "ps%d" % (j % 2)
                pt, ptn = pT[j % 3], "pT%d" % (j % 3)
                kb.op("act", lambda: nc.scalar.activation(out=pt[:kn, :], in_=ps_[:kn, :QW], func=AF.Exp, scale=SCALE), reads=[psn], writes=[ptn])
                if j + 1 < NKT:
                    qk(j + 1)
                kb.op("pe", lambda: nc.tensor.matmul(pO[:, :QW], lhsT=ckvK[:kn, j, :], rhs=pt[:kn, :], start=(j == 0), stop=(j == NKT - 1)),
                      reads=["ckvK", ptn], writes=["ps2"])
                ae, an_ = ("dve", nc.vector) if j % 2 == 0 else ("pool", nc.gpsimd)
                ac, acn = (accA, "accA") if j % 2 == 0 else (accB, "accB")
                if j < 2:
                    kb.op(ae, lambda: an_.tensor_copy(out=ac[:kn, :], in_=pt[:kn, :]), reads=[ptn], writes=[acn])
                else:
                    kb.op(ae, lambda: an_.tensor_tensor(out=ac[:kn, :], in0=ac[:kn, :], in1=pt[:kn, :], op=ALU.add), reads=[ptn, acn], writes=[acn])
            kb.op("pe", lambda: nc.tensor.matmul(pD[:, :QW], lhsT=ones[:], rhs=accA[:, :], start=True, stop=False), reads=["ones", "accA"], writes=["ps3"])
            kb.op("pe", lambda: nc.tensor.matmul(pD[:, :QW], lhsT=ones[:], rhs=accB[:, :], start=False, stop=True), reads=["ones", "accB"], writes=["ps3"])
            kb.op("dve", lambda: nc.vector.reciprocal(out=den[:, :], in_=pD[:, :QW]), reads=["ps3"], writes=["den"])
            kb.op("dve", lambda: nc.vector.tensor_tensor(out=olat[:, :], in0=pO[:, :QW], in1=den[:, :], op=ALU.mult), reads=["ps2", "den"], writes=["olat"])
            kb.op("pe", lambda: nc.tensor.matmul(p4[:, :QW], lhsT=w_uv_sb[:, 0, h * 128:(h + 1) * 128], rhs=olat[:, :], start=True, stop=True),
                  reads=["w", "olat"], writes=["ps4"])
            kb.op("act", lambda: nc.scalar.copy(out=o_sb[:, :], in_=p4[:, :QW]), reads=["ps4"], writes=[o_n])
            kb.dma(ov[:, h, q0:q0 + QW], o_sb[:, :], reads=[o_n])
    return kb.finish()


def mla_weights_host(w_in, qn, w_q, kvn, w_kv_up, H):
    w_in = np.asarray(w_in, np.float32)
    w_q = np.asarray(w_q, np.float32)
    w_kv_up = np.asarray(w_kv_up, np.float32)
    pe = w_in[:, QR + KVR:]
    w_pesw = np.concatenate([pe[:, 32:], pe[:, :32]], axis=1)
    wq3 = w_q.reshape(QR, H, 192)
    w_qsw = np.concatenate([wq3[:, :, 160:192], wq3[:, :, 128:160]], axis=2).reshape(QR, H * 64)
    w_ukT = np.transpose(w_kv_up[:, :, :128], (2, 1, 0)).reshape(128, H * 128)
    w_uv = w_kv_up[:, :, 128:].reshape(128, H * 128)
    inv_freq = (10000.0 ** (-np.arange(0, 64, 2, dtype=np.float32) / np.float32(64))).astype(np.float32)
    rconst = np.zeros((64, 3), np.float32)
    rconst[:, 0] = np.concatenate([inv_freq, inv_freq])
    rconst[:, 1] = np.float32(np.pi / 2)
    rconst[:32, 2] = np.float32(np.pi)
    return {
        "w_in": np.ascontiguousarray(w_in), "w_pesw": np.ascontiguousarray(w_pesw),
        "qn_g": _pcol(qn), "kvn_g": _pcol(kvn), "kvn_rep": np.ascontiguousarray(np.tile(np.asarray(kvn, np.float32)[None, :], (128, 1))),
        "w_q": np.ascontiguousarray(w_q), "w_qsw": np.ascontiguousarray(w_qsw),
        "w_ukT": np.ascontiguousarray(w_ukT), "w_uv": np.ascontiguousarray(w_uv), "rconst": rconst,
    }


def mla_core_inputs(hT, posbase, off, q0, nq):
    T = hT.shape[1]
    return {
        "hT": hT, "hqT": np.ascontiguousarray(hT[:, q0:q0 + nq]),
        "posk": np.ascontiguousarray(np.tile(posbase[None, :], (64, 1))), "offk": np.ascontiguousarray(np.tile(off[None, :], (64, 1))),
        "posq": np.ascontiguousarray(np.tile(posbase[None, q0:q0 + nq], (64, 1))), "offq": np.ascontiguousarray(np.tile(off[None, q0:q0 + nq], (64, 1))),
    }


LRU_W = 1280
NU = 5


def build_lru(T):
    kb = KB()
    nc = kb.nc
    TWL = 508
    tiles = [(t0, min(TWL, T - t0)) for t0 in range(0, T, TWL)]
    xT = kb.din("xT", [D, T + 4])
    w_u = kb.din("w_u", [D, NU * 256])
    taps = kb.din("taps", [128, NU * 2 * 5])
    cbias = kb.din("cbias", [128, NU * 2])
    gw = kb.din("gw", [256, NU * 256])
    gb = kb.din("gb", [128, NU * 2])
    lam = kb.din("lam", [128, NU])
    hs = kb.dout("hs", [NU * 128, T])

    w_u_sb = kb.sb("w_u_sb", [128, 8, NU * 256], BF16)
    gw_sb = kb.sb("gw_sb", [128, 2, NU * 256], BF16)
    taps_sb = kb.sb("taps_sb", [128, NU * 10])
    cb_sb = kb.sb("cb_sb", [128, NU * 2])
    gb_sb = kb.sb("gb_sb", [128, NU * 2])
    lam_sb = kb.sb("lam_sb", [128, NU])
    cl_sb = kb.sb("cl_sb", [128, NU])
    one_t = kb.sb("one_t", [128, 1])
    xb = [kb.sb("xb%d" % i, [128, 8, TWL + 4], BF16) for i in range(2)]
    uc = kb.sb("uc", [128, 2, TWL])
    ucb = kb.sb("ucb", [128, 2, TWL], BF16)
    rg = kb.sb("rg", [128, TWL])
    ig = kb.sb("ig", [128, TWL])
    aa = kb.sb("aa", [128, TWL])
    a2 = kb.sb("a2", [128, TWL])
    bx = kb.sb("bx", [128, TWL])
    ho = [kb.sb("ho%d" % i, [128, TWL]) for i in range(2)]
    carry = kb.sb("carry", [128, NU])

    kb.op("pool", lambda: nc.gpsimd.memset(one_t[:], 1.0), writes=["one"])
    kb.op("pool", lambda: nc.gpsimd.memset(carry[:], 0.0), writes=["carry"])
    kb.dma(taps_sb[:], taps, writes=["par"])
    kb.dma(cb_sb[:], cbias, writes=["par"])
    kb.dma(gb_sb[:], gb, writes=["par"])
    kb.dma(lam_sb[:], lam, writes=["lam"])
    kb.op("act", lambda: nc.scalar.activation(out=cl_sb[:], in_=lam_sb[:], func=AF.Exp, scale=-1.0), reads=["lam"], writes=["cl"])
    kb.op("act", lambda: nc.scalar.activation(out=cl_sb[:], in_=cl_sb[:], func=AF.Ln, bias=one_t[:, 0:1], scale=1.0), reads=["cl", "one"], writes=["cl"])
    kb.op("dve", lambda: nc.vector.tensor_scalar(out=cl_sb[:], in0=cl_sb[:], scalar1=-8.0, scalar2=None, op0=ALU.mult), reads=["cl"], writes=["cl"])
    xv = xT.rearrange("(c p) t -> p c t", p=128)

    def load_x(i, t0, n):
        for c in range(8):
            kb.dma(xb[i][:, c, :n + 4], xv[:, c, t0:t0 + n + 4], writes=["xb%d" % i], e="pool")

    load_x(0, tiles[0][0], tiles[0][1])
    kb.load_w_bf16(w_u_sb, "w", w_u, D, NU * 256, nblk=1280)
    kb.load_w_bf16(gw_sb, "w", gw, 256, NU * 256, nblk=1280)
    cnt = 0
    for i, (t0, n) in enumerate(tiles):
        x, xn = xb[i % 2], "xb%d" % (i % 2)
        W = n + 4
        if i + 1 < len(tiles):
            load_x((i + 1) % 2, tiles[i + 1][0], tiles[i + 1][1])
        for j in range(NU):
            for c in range(2):
                pu, pun = kb.ps[c], "ps%d" % c
                for kc in range(8):
                    kb.op("pe", lambda: nc.tensor.matmul(pu[:, :W], lhsT=w_u_sb[:, kc, j * 256 + c * 128:j * 256 + (c + 1) * 128], rhs=x[:, kc, :W],
                                                         start=(kc == 0), stop=(kc == 7)), reads=["w", xn], writes=[pun])
                tb = j * 10 + c * 5
                kb.op("dve", lambda: nc.vector.tensor_scalar(out=uc[:, c, :n], in0=pu[:, 0:n], scalar1=taps_sb[:, tb:tb + 1], scalar2=cb_sb[:, j * 2 + c:j * 2 + c + 1],
                                                             op0=ALU.mult, op1=ALU.add), reads=[pun, "par"], writes=["uc"])
                for k in range(1, 5):
                    kb.op("dve", lambda: nc.vector.scalar_tensor_tensor(out=uc[:, c, :n], in0=pu[:, k:k + n], scalar=taps_sb[:, tb + k:tb + k + 1], in1=uc[:, c, :n],
                                                                        op0=ALU.mult, op1=ALU.add), reads=[pun, "par", "uc"], writes=["uc"])
            kb.op("pool", lambda: nc.gpsimd.tensor_copy(out=ucb[:, :, :n], in_=uc[:, :, :n]), reads=["uc"], writes=["ucb"])
            pr, pi = kb.ps[2], kb.ps[3]
            for kc in range(2):
                kb.op("pe", lambda: nc.tensor.matmul(pr[:, :n], lhsT=gw_sb[:, kc, j * 256:j * 256 + 128], rhs=ucb[:, kc, :n], start=(kc == 0), stop=(kc == 1)),
                      reads=["w", "ucb"], writes=["ps2"])
            for kc in range(2):
                kb.op("pe", lambda: nc.tensor.matmul(pi[:, :n], lhsT=gw_sb[:, kc, j * 256 + 128:j * 256 + 256], rhs=ucb[:, kc, :n], start=(kc == 0), stop=(kc == 1)),
                      reads=["w", "ucb"], writes=["ps3"])
            kb.op("act", lambda: nc.scalar.activation(out=rg[:, :n], in_=pr[:, :n], func=AF.Sigmoid, bias=gb_sb[:, 2 * j:2 * j + 1], scale=1.0), reads=["ps2", "par"], writes=["rg"])
            kb.op("act", lambda: nc.scalar.activation(out=ig[:, :n], in_=pi[:, :n], func=AF.Sigmoid, bias=gb_sb[:, 2 * j + 1:2 * j + 2], scale=1.0), reads=["ps3", "par"], writes=["ig"])
            kb.op("act", lambda: nc.scalar.activation(out=aa[:, :n], in_=rg[:, :n], func=AF.Exp, scale=cl_sb[:, j:j + 1]), reads=["rg", "cl"], writes=["aa"])
            kb.op("pool", lambda: nc.gpsimd.tensor_tensor(out=a2[:, :n], in0=aa[:, :n], in1=aa[:, :n], op=ALU.mult), reads=["aa"], writes=["a2"])
            kb.op("act", lambda: nc.scalar.activation(out=a2[:, :n], in_=a2[:, :n], func=AF.Sqrt, bias=one_t[:, 0:1], scale=-1.0), reads=["a2", "one"], writes=["a2"])
            kb.op("pool", lambda: nc.gpsimd.tensor_tensor(out=bx[:, :n], in0=ig[:, :n], in1=uc[:, 0, :n], op=ALU.mult), reads=["ig", "uc"], writes=["bx"])
            kb.op("dve", lambda: nc.vector.tensor_tensor(out=bx[:, :n], in0=bx[:, :n], in1=a2[:, :n], op=ALU.mult), reads=["bx", "a2"], writes=["bx"])
            h_, hn = ho[cnt % 2], "ho%d" % (cnt % 2)
            cnt += 1
            kb.op("dve", lambda: nc.vector.tensor_tensor_scan(out=h_[:, :n], data0=aa[:, :n], data1=bx[:, :n], initial=carry[:, j:j + 1],
                                                              op0=ALU.mult, op1=ALU.add), reads=["aa", "bx", "carry"], writes=[hn])
            kb.op("dve", lambda: nc.vector.tensor_copy(out=carry[:, j:j + 1], in_=h_[:, n - 1:n]), reads=[hn], writes=["carry"])
            kb.dma(hs[j * 128:(j + 1) * 128, t0:t0 + n], h_[:, :n], reads=[hn])
    return kb.finish()


def lru_core_inputs(core, x_b, p):
    q = core % 4
    d = q // 2
    slices = list(range(5)) if q % 2 == 0 else list(range(5, 10))
    T = x_b.shape[0]
    xx = x_b[::-1] if d == 1 else x_b
    xT = np.zeros((D, T + 4), np.float32)
    xT[:, 2:T + 2] = xx.T
    w_in, conv_w, conv_b = p["lru_w_in"], p["lru_conv_w"], p["lru_conv_b"]
    gate_w, gate_b, lam = p["lru_gate_w"], p["lru_gate_b"], p["lru_lambda"]
    w_u = np.zeros((D, NU * 256), np.float32)
    taps = np.zeros((128, NU, 2, 5), np.float32)
    cbias = np.zeros((128, NU, 2), np.float32)
    gw = np.zeros((256, NU, 2, 128), np.float32)
    gbv = np.zeros((128, NU, 2), np.float32)
    lamv = np.zeros((128, NU), np.float32)
    for j, s in enumerate(slices):
        n, half = s // 2, s % 2
        own = np.arange(n * 256 + half * 128, n * 256 + half * 128 + 128)
        oth = np.arange(n * 256 + (1 - half) * 128, n * 256 + (1 - half) * 128 + 128)
        ch = np.concatenate([own, oth])
        w_u[:, j * 256:(j + 1) * 256] = w_in[:, LRU_W + ch]
        cwj = conv_w[:, ch]
        t5 = np.zeros((5, 256), np.float32)
        if d == 0:
            t5[0:4] = cwj
        else:
            t5[1:5] = cwj[::-1]
        taps[:, j, :, :] = t5.T.reshape(2, 128, 5).transpose(1, 0, 2)
        cbias[:, j, :] = conv_b[ch].reshape(2, 128).T
        loc = ch - n * 256
        for g in range(2):
            gw[:, j, g, :] = gate_w[d, g, n][loc][:, half * 128:(half + 1) * 128]
            gbv[:, j, g] = gate_b[d, g, own]
        lamv[:, j] = lam[d, own]
    return {"xT": xT, "w_u": w_u, "taps": np.ascontiguousarray(taps.reshape(128, -1)), "cbias": np.ascontiguousarray(cbias.reshape(128, -1)),
            "gw": np.ascontiguousarray(gw.reshape(256, -1)), "gb": np.ascontiguousarray(gbv.reshape(128, -1)), "lam": lamv}, d, slices


NH = 8
CH = 64
RW_TW = 256
EXPM05 = float(np.exp(-0.5))


def build_rwkv(TP, NLANE=4):
    kb = KB()
    nc = kb.nc
    tiles = [(t0, min(RW_TW, TP - t0)) for t0 in range(0, TP, RW_TW)]
    xT = kb.din("xT", [D, TP + 2])
    mu = kb.din("mu", [128, 48])
    w_rkv = kb.din("w_rkv", [D, 3 * 512])
    w1 = kb.din("w1", [D, 64])
    a1 = kb.din("a1", [D, 64])
    g1 = kb.din("g1", [D, 160])
    w2 = kb.din("w2", [64, 512])
    a2 = kb.din("a2", [64, 512])
    g2 = kb.din("g2", [160, 512])
    hp = kb.din("hp", [64, 5 * NH])
    cmask = kb.din("cmask", [64, RW_TW])
    mask4 = kb.din("mask4", [64, 512])
    maskl = kb.din("maskl", [64, 128])
    ident2 = kb.din("ident2", [64, 128])
    yT = kb.dout("yT", [NH * 64, TP])
    bonT = kb.dout("bonT", [NH * 64, TP])
    gT = kb.dout("gT", [NH * 64, TP])

    w_rkv_sb = kb.sb("w_rkv_sb", [128, 8, 1536], BF16)
    w1_sb = kb.sb("w1_sb", [128, 8, 64], BF16)
    a1_sb = kb.sb("a1_sb", [128, 8, 64], BF16)
    g1_sb = kb.sb("g1_sb", [128, 8, 160], BF16)
    w2_sb = kb.sb("w2_sb", [64, 512], BF16)
    a2_sb = kb.sb("a2_sb", [64, 512], BF16)
    g2a_sb = kb.sb("g2a_sb", [128, 512], BF16)
    g2b_sb = kb.sb("g2b_sb", [32, 512], BF16)
    mu_sb = kb.sb("mu_sb", [128, 48])
    hp_sb = kb.sb("hp_sb", [64, 5 * NH])
    omka = kb.sb("omka", [64, NH])
    cmask_sb = kb.sb("cmask_sb", [64, RW_TW])
    mask4_sb = kb.sb("mask4_sb", [64, 512])
    maskl_sb = kb.sb("maskl_sb", [64, 128])
    id2_sb = kb.sb("id2_sb", [64, 128])
    ones64 = kb.sb("ones64", [64, 64])
    n = RW_TW
    NCH = n // CH
    xf = [kb.sb("xf0", [128, 8, n + 2])] * 2
    xx = kb.sb("xx", [128, 8, n])
    xm = [kb.sb("xm%d" % i, [128, 8, n], BF16) for i in range(6)]
    hidw = kb.sb("hidw", [64, n], BF16)
    hida = kb.sb("hida", [64, n], BF16)
    hidg = kb.sb("hidg", [128, n], BF16)
    hidg2 = kb.sb("hidg2", [32, n], BF16)
    Rt = kb.sb("Rt", [64, n])
    Kx = kb.sb("Kx", [64, n])
    LW = kb.sb("LW", [64, n])
    AS = kb.sb("AS", [64, n])
    KK = kb.sb("KK", [64, n])
    SQ = kb.sb("SQ", [64, n])
    NR = kb.sb("NR", [64, n])
    TT = kb.sb("TT", [64, n])
    KD = kb.sb("KD", [64, n])
    BV = kb.sb("BV", [64, n])
    CL = kb.sb("CL", [64, n])
    EG = kb.sb("EG", [64, n])
    EGi = kb.sb("EGi", [64, n])
    EX = kb.sb("EX", [64, n])
    AR = kb.sb("AR", [64, NH, NCH, 128])
    Bf = kb.sb("Bf", [64, NH, n])
    Kf = kb.sb("Kf", [64, NH, n])
    Vf = kb.sb("Vf", [64, NH, n])
    gC = kb.sb("gC", [64, NH, NCH])
    BON = [kb.sb("BON%d" % i, [64, n]) for i in range(2)]
    GO = [kb.sb("GO%d" % i, [64, n]) for i in range(2)]
    YO = kb.sb("YO", [64, NH, n])
    S0 = kb.sb("S0", [64, NH, 64])
    gram = [kb.sb("gram%d" % l, [64, 512]) for l in range(NLANE)]
    Lsb = [kb.sb("Lsb%d" % l, [64, 128]) for l in range(NLANE)]
    NL = [[kb.sb("NL%d_%d" % (l, i), [64, 256]) for i in range(2)] for l in range(NLANE)]
    Gm = [kb.sb("Gm%d" % l, [64, 128]) for l in range(NLANE)]
    tok = [kb.sb("tok%d" % l, [64, 384]) for l in range(NLANE)]
    Xs = [kb.sb("Xs%d" % l, [64, 128]) for l in range(NLANE)]
    Us = [kb.sb("Us%d" % l, [64, 128]) for l in range(NLANE)]
    tmpS = [kb.sb("tmpS%d" % l, [64, 128]) for l in range(NLANE)]

    kb.op("pool", lambda: nc.gpsimd.memset(ones64[:], 1.0), writes=["ones64"])
    kb.op("pool", lambda: nc.gpsimd.memset(S0[:], 0.0), writes=["S0_%d" % hh for hh in range(NH)])
    kb.dma(mu_sb[:], mu, writes=["par"])
    kb.dma(hp_sb[:], hp, writes=["par"])
    kb.dma(cmask_sb[:], cmask, writes=["par"])
    kb.dma(mask4_sb[:], mask4, writes=["par"])
    kb.dma(maskl_sb[:], maskl, writes=["par"])
    kb.dma(id2_sb[:], ident2, writes=["par"])
    kb.op("dve", lambda: nc.vector.tensor_scalar(out=omka[:], in0=hp_sb[:, 3 * NH:4 * NH], scalar1=-1.0, scalar2=1.0, op0=ALU.mult, op1=ALU.add),
          reads=["par"], writes=["omka"])
    xv = xT.rearrange("(c p) t -> p c t", p=128)
    kb.dma(xf[0][:, :, :tiles[0][1] + 2], xv[:, :, 0:tiles[0][1] + 2], writes=["xf0"])
    kb.load_w_bf16(w_rkv_sb, "w", w_rkv, D, 1536, nblk=1536)
    kb.load_w_bf16(w1_sb, "w", w1, D, 64)
    kb.load_w_bf16(a1_sb, "w", a1, D, 64)
    kb.load_w_bf16(g1_sb, "w", g1, D, 160)
    kb.dma(w2_sb[:], w2, writes=["w"], e="pool")
    kb.dma(a2_sb[:], a2, writes=["w"], e="pool")
    kb.dma(g2a_sb[:], g2[0:128, :], writes=["w"], e="pool")
    kb.dma(g2b_sb[:], g2[128:160, :], writes=["w"], e="pool")
    yv = yT.rearrange("(h k) t -> k h t", k=64)
    bv = bonT.rearrange("(h k) t -> k h t", k=64)
    gv = gT.rearrange("(h k) t -> k h t", k=64)
    P = kb.ps

    def hcol(q, hh):
        return hp_sb[:, q * NH + hh:q * NH + hh + 1]

    for ti, (t0, n) in enumerate(tiles):
        nch = n // CH
        x, xn = xf[0], "xf0"
        kb.op("pool", lambda: nc.gpsimd.tensor_tensor(out=xx[:, :, :n], in0=x[:, :, 0:n], in1=x[:, :, 2:n + 2], op=ALU.add), reads=[xn], writes=["xx"])
        kb.op("dve", lambda: nc.vector.scalar_tensor_tensor(out=xx[:, :, :n], in0=xx[:, :, :n], scalar=0.5, in1=x[:, :, 1:n + 1], op0=ALU.mult, op1=ALU.subtract),
              reads=["xx", xn], writes=["xx"])
        for i in range(6):
            for c in range(8):
                kb.op("dve", lambda: nc.vector.scalar_tensor_tensor(out=xm[i][:, c, :n], in0=xx[:, c, :n], scalar=mu_sb[:, i * 8 + c:i * 8 + c + 1], in1=x[:, c, 1:n + 1],
                                                                    op0=ALU.mult, op1=ALU.add), reads=["xx", xn, "par"], writes=["xm%d" % i])
        if ti + 1 < len(tiles):
            t1, n1 = tiles[ti + 1]
            kb.dma(xf[0][:, :, :n1 + 2], xv[:, :, t1:t1 + n1 + 2], writes=["xf0"])
        for kc in range(8):
            kb.op("pe", lambda: nc.tensor.matmul(P[6][0:64, :n], lhsT=w1_sb[:, kc, :], rhs=xm[1][:, kc, :n], start=(kc == 0), stop=(kc == 7)), reads=["w", "xm1"], writes=["ps6"])
        kb.op("act", lambda: nc.scalar.activation(out=hidw[:, :n], in_=P[6][0:64, :n], func=AF.Tanh), reads=["ps6"], writes=["hidw"])
        for kc in range(8):
            kb.op("pe", lambda: nc.tensor.matmul(P[6][0:64, :n], lhsT=a1_sb[:, kc, :], rhs=xm[4][:, kc, :n], start=(kc == 0), stop=(kc == 7)), reads=["w", "xm4"], writes=["ps6"])
        kb.op("act", lambda: nc.scalar.copy(out=hida[:, :n], in_=P[6][0:64, :n]), reads=["ps6"], writes=["hida"])
        for kc in range(8):
            kb.op("pe", lambda: nc.tensor.matmul(P[6][:, :n], lhsT=g1_sb[:, kc, 0:128], rhs=xm[5][:, kc, :n], start=(kc == 0), stop=(kc == 7)), reads=["w", "xm5"], writes=["ps6"])
        kb.op("act", lambda: nc.scalar.activation(out=hidg[:, :n], in_=P[6][:, :n], func=AF.Sigmoid), reads=["ps6"], writes=["hidg"])
        for kc in range(8):
            kb.op("pe", lambda: nc.tensor.matmul(P[6][0:32, :n], lhsT=g1_sb[:, kc, 128:160], rhs=xm[5][:, kc, :n], start=(kc == 0), stop=(kc == 7)), reads=["w", "xm5"], writes=["ps6"])
        kb.op("act", lambda: nc.scalar.activation(out=hidg2[:, :n], in_=P[6][0:32, :n], func=AF.Sigmoid), reads=["ps6"], writes=["hidg2"])
        for hh in range(NH):
            cs = slice(hh * 64, (hh + 1) * 64)
            for q, (pp, ppn, xi) in enumerate(((P[0], "ps0", 0), (P[1], "ps1", 2), (P[2], "ps2", 3))):
                for kc in range(8):
                    kb.op("pe", lambda: nc.tensor.matmul(pp[0:64, :n], lhsT=w_rkv_sb[:, kc, q * 512 + hh * 64:q * 512 + (hh + 1) * 64], rhs=xm[xi][:, kc, :n],
                                                         start=(kc == 0), stop=(kc == 7)), reads=["w", "xm%d" % xi], writes=[ppn])
            kb.op("pe", lambda: nc.tensor.matmul(P[3][0:64, :n], lhsT=w2_sb[:, cs], rhs=hidw[:, :n], start=True, stop=True), reads=["w", "hidw"], writes=["ps3"])
            kb.op("pe", lambda: nc.tensor.matmul(P[4][0:64, :n], lhsT=a2_sb[:, cs], rhs=hida[:, :n], start=True, stop=True), reads=["w", "hida"], writes=["ps4"])
            kb.op("pe", lambda: nc.tensor.matmul(P[7][0:64, :n], lhsT=g2a_sb[:, cs], rhs=hidg[:, :n], start=True, stop=False), reads=["w", "hidg"], writes=["ps7"])
            kb.op("pe", lambda: nc.tensor.matmul(P[7][0:64, :n], lhsT=g2b_sb[:, cs], rhs=hidg2[:, :n], start=False, stop=True), reads=["w", "hidg2"], writes=["ps7"])
            kb.op("act", lambda: nc.scalar.copy(out=Rt[:, :n], in_=P[0][0:64, :n]), reads=["ps0"], writes=["Rt"])
            kb.op("act", lambda: nc.scalar.copy(out=Kx[:, :n], in_=P[1][0:64, :n]), reads=["ps1"], writes=["Kx"])
            kb.op("act", lambda: nc.scalar.copy(out=Vf[:, hh, :n], in_=P[2][0:64, :n]), reads=["ps2"], writes=["Vf"])
            kb.op("act", lambda: nc.scalar.copy(out=GO[hh % 2][:, :n], in_=P[7][0:64, :n]), reads=["ps7"], writes=["GO%d" % (hh % 2)])
            kb.dma(gv[:, hh, t0:t0 + n], GO[hh % 2][:, :n], reads=["GO%d" % (hh % 2)])
            kb.op("act", lambda: nc.scalar.activation(out=LW[:, :n], in_=P[3][0:64, :n], func=AF.Sigmoid, bias=hcol(0, hh), scale=1.0), reads=["ps3", "par"], writes=["LW"])
            kb.op("pool", lambda: nc.gpsimd.tensor_scalar(out=LW[:, :n], in0=LW[:, :n], scalar1=-EXPM05, scalar2=None, op0=ALU.mult), reads=["LW"], writes=["LW"])
            kb.op("act", lambda: nc.scalar.activation(out=AS[:, :n], in_=P[4][0:64, :n], func=AF.Sigmoid, bias=hcol(1, hh), scale=1.0), reads=["ps4", "par"], writes=["AS"])
            kb.op("dve", lambda: nc.vector.tensor_scalar(out=KK[:, :n], in0=Kx[:, :n], scalar1=hcol(2, hh), scalar2=None, op0=ALU.mult), reads=["Kx", "par"], writes=["KK"])
            kb.op("act", lambda: nc.scalar.activation(out=SQ[:, :n], in_=KK[:, :n], func=AF.Square), reads=["KK"], writes=["SQ"])
            kb.op("pe", lambda: nc.tensor.matmul(P[5][0:64, :n], lhsT=ones64[:], rhs=SQ[:, :n], start=True, stop=True), reads=["ones64", "SQ"], writes=["ps5"])
            kb.op("dve", lambda: nc.vector.tensor_scalar(out=NR[:, :n], in0=P[5][0:64, :n], scalar1=1e-24, scalar2=None, op0=ALU.max), reads=["ps5"], writes=["NR"])
            kb.op("act", lambda: nc.scalar.activation(out=NR[:, :n], in_=NR[:, :n], func=AF.Sqrt), reads=["NR"], writes=["NR"])
            kb.op("dve", lambda: nc.vector.reciprocal(out=NR[:, :n], in_=NR[:, :n]), reads=["NR"], writes=["NR"])
            kb.op("pool", lambda: nc.gpsimd.tensor_tensor(out=KK[:, :n], in0=KK[:, :n], in1=NR[:, :n], op=ALU.mult), reads=["KK", "NR"], writes=["KK"])
            kb.op("dve", lambda: nc.vector.tensor_scalar(out=TT[:, :n], in0=AS[:, :n], scalar1=hcol(3, hh), scalar2=omka[:, hh:hh + 1], op0=ALU.mult, op1=ALU.add),
                  reads=["AS", "par", "omka"], writes=["TT"])
            kb.op("pool", lambda: nc.gpsimd.tensor_tensor(out=KD[:, :n], in0=Kx[:, :n], in1=TT[:, :n], op=ALU.mult), reads=["Kx", "TT"], writes=["KD"])
            kb.op("pool", lambda: nc.gpsimd.tensor_tensor(out=BV[:, :n], in0=KK[:, :n], in1=AS[:, :n], op=ALU.mult), reads=["KK", "AS"], writes=["BV"])
            kb.op("dve", lambda: nc.vector.scalar_tensor_tensor(out=SQ[:, :n], in0=Rt[:, :n], scalar=hcol(4, hh), in1=KD[:, :n], op0=ALU.mult, op1=ALU.mult),
                  reads=["Rt", "par", "KD", "SQ"], writes=["SQ"])
            kb.op("pe", lambda: nc.tensor.matmul(P[5][0:64, :n], lhsT=ones64[:], rhs=SQ[:, :n], start=True, stop=True), reads=["ones64", "SQ"], writes=["ps5"])
            kb.op("dve", lambda: nc.vector.tensor_tensor(out=BON[hh % 2][:, :n], in0=P[5][0:64, :n], in1=Vf[:, hh, :n], op=ALU.mult), reads=["ps5", "Vf"], writes=["BON%d" % (hh % 2)])
            kb.dma(bv[:, hh, t0:t0 + n], BON[hh % 2][:, :n], reads=["BON%d" % (hh % 2)])
            kb.op("dve", lambda: nc.vector.tensor_tensor_scan(out=CL[:, :n], data0=cmask_sb[:, :n], data1=LW[:, :n], initial=0.0, op0=ALU.mult, op1=ALU.add),
                  reads=["par", "LW"], writes=["CL"])
            kb.op("act", lambda: nc.scalar.activation(out=EG[:, :n], in_=CL[:, :n], func=AF.Exp), reads=["CL"], writes=["EG"])
            kb.op("act", lambda: nc.scalar.activation(out=EGi[:, :n], in_=CL[:, :n], func=AF.Exp, scale=-1.0), reads=["CL"], writes=["EGi"])
            kb.op("pool", lambda: nc.gpsimd.tensor_tensor(out=EX[:, :n], in0=CL[:, :n], in1=LW[:, :n], op=ALU.subtract), reads=["CL", "LW"], writes=["EX"])
            kb.op("act", lambda: nc.scalar.activation(out=EX[:, :n], in_=EX[:, :n], func=AF.Exp), reads=["EX"], writes=["EX"])
            kb.op("dve", lambda: nc.vector.scalar_tensor_tensor(out=AR[:, hh, :nch, 0:64], in0=KK[:, :n].rearrange("p (c t) -> p c t", t=CH), scalar=-1.0,
                                                                in1=EX[:, :n].rearrange("p (c t) -> p c t", t=CH), op0=ALU.mult, op1=ALU.mult),
                  reads=["KK", "EX"], writes=["AR"])
            kb.op("dve", lambda: nc.vector.tensor_tensor(out=AR[:, hh, :nch, 64:128], in0=Rt[:, :n].rearrange("p (c t) -> p c t", t=CH),
                                                         in1=EG[:, :n].rearrange("p (c t) -> p c t", t=CH), op=ALU.mult), reads=["Rt", "EG"], writes=["AR"])
            kb.op("pool", lambda: nc.gpsimd.tensor_tensor(out=Bf[:, hh, :n], in0=BV[:, :n], in1=EGi[:, :n], op=ALU.mult), reads=["BV", "EGi"], writes=["Bf"])
            kb.op("pool", lambda: nc.gpsimd.tensor_tensor(out=Kf[:, hh, :n], in0=KD[:, :n], in1=EGi[:, :n], op=ALU.mult), reads=["KD", "EGi"], writes=["Kf"])
            kb.op("dve", lambda: nc.vector.tensor_copy(out=gC[:, hh, :nch], in_=EG[:, :n].rearrange("p (c t) -> p c t", t=CH)[:, :, CH - 1]), reads=["EG"], writes=["gC"])
        kb.barrier()
        for c in range(nch):
            tsl = slice(c * CH, (c + 1) * CH)
            for lg in range(0, NH // 2, NLANE):
                lanes = list(range(NLANE))
                pairs = [(2 * (lg + l), 2 * (lg + l) + 1) for l in lanes]
                BA = [P[2 * l] for l in lanes]
                BB = [P[2 * l + 1] for l in lanes]
                nA = ["gA%d" % l for l in lanes]
                nBa = ["gB%da" % l for l in lanes]
                nBb = ["gB%db" % l for l in lanes]
                nBc = ["gB%dc" % l for l in lanes]
                for l in lanes:
                    for e, hh in enumerate(pairs[l]):
                        kb.op("pe", lambda: nc.tensor.matmul(BA[l][0:64, e * 256:e * 256 + 128], lhsT=Bf[:, hh, tsl], rhs=AR[:, hh, c, :], start=True, stop=True), reads=["Bf", "AR"], writes=[nA[l]])
                        kb.op("pe", lambda: nc.tensor.matmul(BA[l][0:64, e * 256 + 128:e * 256 + 256], lhsT=Kf[:, hh, tsl], rhs=AR[:, hh, c, :], start=True, stop=True), reads=["Kf", "AR"], writes=[nA[l]])
                        kb.op("pe", lambda: nc.tensor.matmul(BB[l][0:64, e * 64:(e + 1) * 64], lhsT=AR[:, hh, c, 0:64], rhs=Bf[:, hh, tsl], start=True, stop=True), reads=["Bf", "AR"], writes=[nBa[l]])
                for l in lanes:
                    gr, grn = gram[l], "gram%d" % l
                    kb.op("dve", lambda: nc.vector.tensor_tensor(out=gr[:, :], in0=BA[l][0:64, :], in1=mask4_sb[:, :], op=ALU.mult), reads=[nA[l], "par"], writes=[grn])
                    kb.op("dve", lambda: nc.vector.tensor_tensor(out=Lsb[l][:, :], in0=BB[l][0:64, 0:128], in1=maskl_sb[:, :], op=ALU.mult), reads=[nBa[l], "par"], writes=["Lsb%d" % l])
                    g3 = gr[:, :].rearrange("p (e x) -> p e x", e=2)
                    kb.op("pool", lambda: nc.gpsimd.tensor_tensor(out=Gm[l][:, :].rearrange("p (e x) -> p e x", e=2), in0=id2_sb[:, :].rearrange("p (e x) -> p e x", e=2),
                                                                  in1=g3[:, :, 0:64], op=ALU.add), reads=[grn, "par"], writes=["Gm%d" % l])
                for l in lanes:
                    for q, (src, sn) in enumerate(((Bf, "Bf"), (Kf, "Kf"), (Vf, "Vf"))):
                        for e, hh in enumerate(pairs[l]):
                            kb.op("pe", lambda: nc.tensor.matmul(BA[l][0:64, q * 128 + e * 64:q * 128 + (e + 1) * 64], lhsT=src[:, hh, tsl], rhs=id2_sb[:, 0:64], start=True, stop=True),
                                  reads=[sn, "par"], writes=[nA[l]])
                for l in lanes:
                    kb.op("act", lambda: nc.scalar.copy(out=tok[l][:, :], in_=BA[l][0:64, 0:384]), reads=[nA[l]], writes=["tok%d" % l])
                Ncur = [[gram[l][:, 0:64], gram[l][:, 256:320]] for l in lanes]
                Lcur = [[Lsb[l][:, 0:64], Lsb[l][:, 64:128]] for l in lanes]
                ncn = ["gram%d" % l for l in lanes]
                lcn = ["Lsb%d" % l for l in lanes]
                for lvl in range(5):
                    last = lvl == 4
                    for l in lanes:
                        for e in range(2):
                            if not last:
                                kb.op("pe", lambda: nc.tensor.matmul(BB[l][0:64, 128 + e * 64:128 + (e + 1) * 64], lhsT=Lcur[l][e], rhs=Ncur[l][e], start=True, stop=True), reads=[ncn[l], lcn[l]], writes=[nBb[l]])
                            kb.op("pe", lambda: nc.tensor.matmul(BB[l][0:64, 256 + e * 64:256 + (e + 1) * 64], lhsT=Ncur[l][e], rhs=Lcur[l][e], start=True, stop=True), reads=[ncn[l], lcn[l]], writes=[nBb[l]])
                    for l in lanes:
                        nl, nln = NL[l][lvl % 2], "NL%d_%d" % (l, lvl % 2)
                        if not last:
                            kb.op("act", lambda: nc.scalar.copy(out=nl[:, :], in_=BB[l][0:64, 128:384]), reads=[nBb[l]], writes=[nln])
                        else:
                            kb.op("act", lambda: nc.scalar.copy(out=nl[:, 128:256], in_=BB[l][0:64, 256:384]), reads=[nBb[l]], writes=[nln])
                        Ncur[l] = [nl[:, 0:64], nl[:, 64:128]]
                        Lcur[l] = [nl[:, 128:192], nl[:, 192:256]]
                        ncn[l] = lcn[l] = nln
                    for l in lanes:
                        for e in range(2):
                            kb.op("pe", lambda: nc.tensor.matmul(BB[l][0:64, 384 + e * 64:384 + (e + 1) * 64], lhsT=Lcur[l][e], rhs=Gm[l][:, e * 64:(e + 1) * 64], start=True, stop=True), reads=[lcn[l], "Gm%d" % l], writes=[nBc[l]])
                    for l in lanes:
                        kb.op("dve", lambda: nc.vector.tensor_tensor(out=Gm[l][:, :], in0=Gm[l][:, :], in1=BB[l][0:64, 384:512], op=ALU.add), reads=["Gm%d" % l, nBc[l]], writes=["Gm%d" % l])
                for l in lanes:
                    for e, hh in enumerate(pairs[l]):
                        kb.op("pe", lambda: nc.tensor.matmul(BB[l][0:64, e * 64:(e + 1) * 64], lhsT=AR[:, hh, c, 0:64], rhs=S0[:, hh, :], start=True, stop=False), reads=["AR", "S0_%d" % hh], writes=[nBa[l]])
                        kb.op("pe", lambda: nc.tensor.matmul(BB[l][0:64, e * 64:(e + 1) * 64], lhsT=gram[l][:, e * 256 + 128:e * 256 + 192], rhs=tok[l][:, 256 + e * 64:256 + (e + 1) * 64], start=False, stop=True),
                              reads=["gram%d" % l, "tok%d" % l], writes=[nBa[l]])
                for l in lanes:
                    kb.op("dve", lambda: nc.vector.tensor_copy(out=Xs[l][:, :], in_=BB[l][0:64, 0:128]), reads=[nBa[l]], writes=["Xs%d" % l])
                for l in lanes:
                    for e in range(2):
                        kb.op("pe", lambda: nc.tensor.matmul(BB[l][0:64, 128 + e * 64:128 + (e + 1) * 64], lhsT=Gm[l][:, e * 64:(e + 1) * 64], rhs=Xs[l][:, e * 64:(e + 1) * 64], start=True, stop=True), reads=["Gm%d" % l, "Xs%d" % l], writes=[nBb[l]])
                for l in lanes:
                    kb.op("act", lambda: nc.scalar.copy(out=Us[l][:, :], in_=BB[l][0:64, 128:256]), reads=[nBb[l]], writes=["Us%d" % l])
                for l in lanes:
                    for e, hh in enumerate(pairs[l]):
                        es = slice(384 + e * 64, 384 + (e + 1) * 64)
                        ue = slice(e * 64, (e + 1) * 64)
                        kb.op("pe", lambda: nc.tensor.matmul(BB[l][0:64, es], lhsT=S0[:, hh, :], rhs=AR[:, hh, c, 64:128], start=True, stop=False), reads=["S0_%d" % hh, "AR"], writes=[nBc[l]])
                        kb.op("pe", lambda: nc.tensor.matmul(BB[l][0:64, es], lhsT=Us[l][:, ue], rhs=gram[l][:, e * 256 + 64:e * 256 + 128], start=False, stop=False), reads=["Us%d" % l, "gram%d" % l], writes=[nBc[l]])
                        kb.op("pe", lambda: nc.tensor.matmul(BB[l][0:64, es], lhsT=tok[l][:, 256 + e * 64:256 + (e + 1) * 64], rhs=gram[l][:, e * 256 + 192:e * 256 + 256], start=False, stop=True),
                              reads=["tok%d" % l, "gram%d" % l], writes=[nBc[l]])
                    for e, hh in enumerate(pairs[l]):
                        es = slice(384 + e * 64, 384 + (e + 1) * 64)
                        ue = slice(e * 64, (e + 1) * 64)
                        kb.op("pe", lambda: nc.tensor.matmul(BA[l][0:64, es], lhsT=tok[l][:, e * 64:(e + 1) * 64], rhs=Us[l][:, ue], start=True, stop=False), reads=["tok%d" % l, "Us%d" % l], writes=[nA[l]])
                        kb.op("pe", lambda: nc.tensor.matmul(BA[l][0:64, es], lhsT=tok[l][:, 128 + e * 64:128 + (e + 1) * 64], rhs=tok[l][:, 256 + e * 64:256 + (e + 1) * 64], start=False, stop=True),
                              reads=["tok%d" % l], writes=[nA[l]])
                for l in lanes:
                    h0 = pairs[l][0]
                    kb.op("act", lambda: nc.scalar.copy(out=YO[:, h0:h0 + 2, tsl], in_=BB[l][0:64, 384:512].rearrange("p (e t) -> p e t", e=2)), reads=[nBc[l]], writes=["YO"])
                    for e, hh in enumerate(pairs[l]):
                        es = slice(384 + e * 64, 384 + (e + 1) * 64)
                        ue = slice(e * 64, (e + 1) * 64)
                        kb.op("dve", lambda: nc.vector.tensor_tensor(out=tmpS[l][:, ue], in0=BA[l][0:64, es], in1=S0[:, hh, :], op=ALU.add), reads=[nA[l], "S0_%d" % hh], writes=["tmpS%d" % l])
                        kb.op("dve", lambda: nc.vector.tensor_scalar(out=S0[:, hh, :], in0=tmpS[l][:, ue], scalar1=gC[:, hh, c:c + 1], scalar2=None, op0=ALU.mult),
                              reads=["tmpS%d" % l, "gC"], writes=["S0_%d" % hh])
        kb.barrier()
        kb.dma(yv[:, :, t0:t0 + n], YO[:, :, :n], reads=["YO"])
    return kb.finish()


def rwkv_consts():
    i = np.arange(64)
    su = (i[:, None] < i[None, :]).astype(np.float32)
    uu = (i[:, None] <= i[None, :]).astype(np.float32)
    blk = np.concatenate([su, uu, su, uu], axis=1)
    cm = np.ones((64, RW_TW), np.float32)
    cm[:, ::CH] = 0.0
    return {"cmask": cm, "mask4": np.ascontiguousarray(np.concatenate([blk, blk], axis=1)),
            "maskl": np.ascontiguousarray(np.concatenate([su.T, su.T], axis=1)),
            "ident2": np.ascontiguousarray(np.concatenate([np.eye(64, dtype=np.float32)] * 2, axis=1))}


def rwkv_core_inputs(core, x_b, p, TP):
    q = core % 4
    d, half = q // 2, q % 2
    T = x_b.shape[0]
    xx = x_b[::-1] if d == 1 else x_b
    xT = np.zeros((D, TP + 2), np.float32)
    xT[:, 1:T + 1] = xx.T
    cols = slice(half * 512, (half + 1) * 512)

    def hl(v):
        return np.asarray(v, np.float32)[cols].reshape(NH, 64).T

    m = {"xT": xT, "mu": np.ascontiguousarray(np.concatenate([_pcol(p["rw_mu"][i]) for i in range(6)], axis=1)),
         "w_rkv": np.ascontiguousarray(np.concatenate([p["rw_w_rkv"][i][:, cols] for i in range(3)], axis=1)),
         "w1": np.ascontiguousarray(p["rw_w1"][d]), "a1": np.ascontiguousarray(p["rw_a1"][d]), "g1": np.ascontiguousarray(p["rw_g1"]),
         "w2": np.ascontiguousarray(p["rw_w2"][d][:, cols]), "a2": np.ascontiguousarray(p["rw_a2"][d][:, cols]), "g2": np.ascontiguousarray(p["rw_g2"][:, cols]),
         "hp": np.ascontiguousarray(np.concatenate([hl(p["rw_w0"][d]), hl(p["rw_a0"][d]), hl(p["rw_k_k"]), hl(p["rw_k_a"]), hl(p["rw_r_k"])], axis=1))}
    m.update(rwkv_consts())
    return m, d, half


GN_EPS = 64e-5
GELU_C = 2.0 * float(np.sqrt(2.0 / np.pi))


def build_proj2(ntok, TW, mode):
    kb = KB()
    nc = kb.nc
    ntile = ntok // TW
    assert ntile * TW == ntok
    Dz = {"plain": 2048, "rwkv": 1024, "lru": 1280}[mode]
    NZ = Dz // 128
    hT = kb.din("hT", [D, ntok])
    w_o = kb.din("w_o", [Dz, D])
    lng = kb.din("lng", [128, 8])
    lnb = kb.din("lnb", [128, 8])
    yT = kb.dout("yT", [D, ntok])
    w_sb = kb.sb("w_sb", [128, NZ, D], BF16)
    g_sb = kb.sb("g_sb", [128, 8])
    b_sb = kb.sb("b_sb", [128, 8])
    zb = [kb.sb("zb%d" % i, [128, NZ, TW], BF16) for i in range(2)]
    ht = [kb.sb("h%d" % i, [128, 8, TW]) for i in range(2)]
    s2 = [kb.sb("s2_%d" % i, [128, 8, TW]) for i in range(2)]
    scr, ones, eps_t = kb.ln_scratch(TW)
    kb.dma(g_sb[:], lng, writes=["lnp"])
    kb.dma(b_sb[:], lnb, writes=["lnp"])
    hv = hT.rearrange("(c p) t -> p c t", p=128)
    yv = yT.rearrange("(c p) t -> p c t", p=128)
    if mode == "plain":
        zT = kb.din("zT", [Dz, ntok])
        zv = zT.rearrange("(c p) t -> p c t", p=128)
    elif mode == "rwkv":
        srcs = [kb.din(nm, [D, ntok]).rearrange("(c p) t -> p c t", p=128) for nm in ("y0T", "y1T", "b0T", "b1T", "gT")]
        gnp = kb.din("gnp", [128, 16])
        bones = kb.din("bones", [128, 128])
        gnp_sb = kb.sb("gnp_sb", [128, 16])
        bones_sb = kb.sb("bones_sb", [128, 128])
        geps = kb.sb("geps", [128, 1])
        rin = [kb.sb("rin%d" % i, [128, 8, TW]) for i in range(5)]
        t1 = kb.sb("t1", [128, TW])
        t2 = kb.sb("t2", [128, TW])
        t3 = kb.sb("t3", [128, TW])
        kb.dma(gnp_sb[:], gnp, writes=["gnp"])
        kb.dma(bones_sb[:], bones, writes=["bones"])
        kb.op("pool", lambda: nc.gpsimd.memset(geps[:], GN_EPS), writes=["geps"])
    else:
        srcs = [kb.din(nm, [LRU_W, ntok]).rearrange("(c p) t -> p c t", p=128) for nm in ("h0T", "h1T")]
        w_g = kb.din("w_g", [D, LRU_W])
        w_g_sb = kb.sb("w_g_sb", [128, 8, LRU_W], BF16)
        rin = [kb.sb("rin%d" % i, [128, NZ, TW]) for i in range(2)]
        xbf = kb.sb("xbf", [128, 8, TW], BF16)
        t1 = kb.sb("t1", [128, TW])
        t2 = kb.sb("t2", [128, TW])
        t3 = kb.sb("t3", [128, TW])
        kb.load_w_bf16(w_g_sb, "w_g", w_g, D, LRU_W, nblk=1280)
    kb.load_w_bf16(w_sb, "w_o", w_o, Dz, D, nblk=1024)
    for i in range(ntile):
        t0 = i * TW
        z, zn = zb[i % 2], "zb%d" % (i % 2)
        h, hn = ht[i % 2], "h%d" % (i % 2)
        s, sname = s2[i % 2], "s2_%d" % (i % 2)
        kb.dma(h[:, :, :], hv[:, :, t0:t0 + TW], writes=[hn])
        if mode == "plain":
            for c in range(NZ):
                kb.dma(z[:, c, :], zv[:, c, t0:t0 + TW], writes=[zn], e="pool")
        elif mode == "rwkv":
            for q in range(5):
                kb.dma(rin[q][:, :, :], srcs[q][:, :, t0:t0 + TW], writes=["rin%d" % q])
            y0, y1, b0, b1, gg = rin
            kb.op("pool", lambda: nc.gpsimd.tensor_tensor(out=y0[:, :, :], in0=y0[:, :, :], in1=y1[:, :, :], op=ALU.add), reads=["rin0", "rin1"], writes=["rin0"])
            kb.op("pool", lambda: nc.gpsimd.tensor_tensor(out=b0[:, :, :], in0=b0[:, :, :], in1=b1[:, :, :], op=ALU.add), reads=["rin2", "rin3"], writes=["rin2"])
            for c in range(8):
                pa, pb = kb.ps[c % 2], kb.ps[2 + c % 2]
                pan, pbn = "ps%d" % (c % 2), "ps%d" % (2 + c % 2)
                kb.op("pe", lambda: nc.tensor.matmul(pa[:, :TW], lhsT=bones_sb[:], rhs=y0[:, c, :], start=True, stop=True), reads=["bones", "rin0"], writes=[pan])
                kb.op("act", lambda: nc.scalar.activation(out=t1[:, :], in_=y0[:, c, :], func=AF.Square), reads=["rin0"], writes=["t1"])
                kb.op("pe", lambda: nc.tensor.matmul(pb[:, :TW], lhsT=bones_sb[:], rhs=t1[:, :], start=True, stop=True), reads=["bones", "t1"], writes=[pbn])
                kb.op("act", lambda: nc.scalar.activation(out=t2[:, :], in_=pa[:, :TW], func=AF.Copy, scale=1.0 / 64), reads=[pan], writes=["t2"])
                kb.op("dve", lambda: nc.vector.tensor_tensor(out=t3[:, :], in0=t2[:, :], in1=t2[:, :], op=ALU.mult), reads=["t2"], writes=["t3"])
                kb.op("dve", lambda: nc.vector.scalar_tensor_tensor(out=t3[:, :], in0=pb[:, :TW], scalar=1.0 / 64, in1=t3[:, :], op0=ALU.mult, op1=ALU.subtract),
                      reads=[pbn, "t3"], writes=["t3"])
                kb.op("act", lambda: nc.scalar.activation(out=t3[:, :], in_=t3[:, :], func=AF.Sqrt, bias=geps[:, 0:1], scale=1.0), reads=["t3", "geps"], writes=["t3"])
                kb.op("dve", lambda: nc.vector.reciprocal(out=t3[:, :], in_=t3[:, :]), reads=["t3"], writes=["t3"])
                kb.op("dve", lambda: nc.vector.tensor_tensor(out=t1[:, :], in0=y0[:, c, :], in1=t2[:, :], op=ALU.subtract), reads=["rin0", "t2", "t1"], writes=["t1"])
                kb.op("pool", lambda: nc.gpsimd.tensor_tensor(out=t1[:, :], in0=t1[:, :], in1=t3[:, :], op=ALU.mult), reads=["t1", "t3"], writes=["t1"])
                kb.op("act", lambda: nc.scalar.activation(out=t1[:, :], in_=t1[:, :], func=AF.Identity, scale=gnp_sb[:, c:c + 1], bias=gnp_sb[:, 8 + c:9 + c]),
                      reads=["t1", "gnp"], writes=["t1"])
                kb.op("pool", lambda: nc.gpsimd.tensor_tensor(out=t1[:, :], in0=t1[:, :], in1=b0[:, c, :], op=ALU.add), reads=["t1", "rin2"], writes=["t1"])
                kb.op("dve", lambda: nc.vector.tensor_tensor(out=z[:, c, :], in0=t1[:, :], in1=gg[:, c, :], op=ALU.mult), reads=["t1", "rin4"], writes=[zn])
        else:
            for q in range(2):
                kb.dma(rin[q][:, :, :], srcs[q][:, :, t0:t0 + TW], writes=["rin%d" % q])
            h0, h1 = rin
            kb.op("pool", lambda: nc.gpsimd.tensor_copy(out=xbf[:, :, :], in_=h[:, :, :]), reads=[hn], writes=["xbf"])
            kb.op("pool", lambda: nc.gpsimd.tensor_tensor(out=h0[:, :, :], in0=h0[:, :, :], in1=h1[:, :, :], op=ALU.add), reads=["rin0", "rin1"], writes=["rin0"])
            for c in range(NZ):
                pa, pan = kb.ps[c % 2], "ps%d" % (c % 2)
                for kc in range(8):
                    kb.op("pe", lambda: nc.tensor.matmul(pa[:, :TW], lhsT=w_g_sb[:, kc, c * 128:(c + 1) * 128], rhs=xbf[:, kc, :], start=(kc == 0), stop=(kc == 7)),
                          reads=["w_g", "xbf"], writes=[pan])
                kb.op("act", lambda: nc.scalar.activation(out=t1[:, :], in_=pa[:, :TW], func=AF.Square), reads=[pan], writes=["t1"])
                kb.op("dve", lambda: nc.vector.tensor_scalar(out=t1[:, :], in0=t1[:, :], scalar1=0.044715, scalar2=1.0, op0=ALU.mult, op1=ALU.add), reads=["t1"], writes=["t1"])
                kb.op("dve", lambda: nc.vector.tensor_tensor(out=t1[:, :], in0=t1[:, :], in1=pa[:, :TW], op=ALU.mult), reads=["t1", pan], writes=["t1"])
                kb.op("act", lambda: nc.scalar.activation(out=t2[:, :], in_=t1[:, :], func=AF.Sigmoid, scale=GELU_C), reads=["t1"], writes=["t2"])
                kb.op("dve", lambda: nc.vector.tensor_tensor(out=t3[:, :], in0=t2[:, :], in1=pa[:, :TW], op=ALU.mult), reads=["t2", pan], writes=["t3"])
                kb.op("pool", lambda: nc.gpsimd.tensor_tensor(out=z[:, c, :], in0=t3[:, :], in1=h0[:, c, :], op=ALU.mult), reads=["t3", "rin0"], writes=[zn])
        for oc in range(8):
            bf = 4 + oc % 2
            pf = kb.ps[bf]
            for kc in range(NZ):
                kb.op("pe", lambda: nc.tensor.matmul(pf[:, :TW], lhsT=w_sb[:, kc, oc * 128:(oc + 1) * 128], rhs=z[:, kc, :],
                                                     start=(kc == 0), stop=(kc == NZ - 1)), reads=["w_o", zn], writes=["ps%d" % bf])
            kb.op("dve", lambda: nc.vector.scalar_tensor_tensor(out=s[:, oc, :], in0=h[:, oc, :], scalar=ALPHA, in1=pf[:, :TW],
                                                                op0=ALU.mult, op1=ALU.add), reads=[hn, "ps%d" % bf], writes=[sname])
        kb.layernorm_inplace(s, sname, TW, g_sb, b_sb, scr, ones, eps_t)
        kb.dma(yv[:, :, t0:t0 + TW], s[:, :, :], reads=[sname])
    return kb.finish()


BATCH, SEQ, NMETA, DEPTH = 2, 16384, 16, 4
TT_ = SEQ + NMETA
NQ = TT_ // 4
NCORES = 8
_PROGS = {}


def _prog(key, fn):
    if key not in _PROGS:
        _PROGS[key] = fn()
    return _PROGS[key]


def _run(nc, in_maps):
    res = run_bass_kernel_spmd(nc, in_maps, core_ids=list(range(NCORES)))
    return res.results


def kernel(**inp):
    f = lambda k: np.asarray(inp[k], np.float32)
    x = f("x")
    positions = np.asarray(inp["positions"]).astype(np.int32)
    meta = f("meta_tokens")
    ln_g, ln_b = f("ln_g"), f("ln_b")
    T = TT_
    hT = [np.ascontiguousarray(np.concatenate([meta, x[b]], axis=0).T) for b in range(BATCH)]
    posbase = [np.concatenate([np.arange(NMETA, dtype=np.int32), positions[b]]) for b in range(BATCH)]
    off = np.concatenate([np.zeros(NMETA, np.float32), np.full(SEQ, float(NMETA), np.float32)])
    bones = np.zeros((128, 128), np.float32)
    bones[:64, :64] = 1.0
    bones[64:, 64:] = 1.0

    def lnp(i, r):
        return {"lng": _pcol(ln_g[i, r]), "lnb": _pcol(ln_b[i, r])}

    for i in range(DEPTH):
        kind, j = i % 3, i // 3
        if kind == 0:
            wk = mla_weights_host(f("mla_w_in")[j], f("mla_q_norm")[j], f("mla_w_q_up")[j], f("mla_kv_norm")[j], f("mla_w_kv_up")[j], 16)
            in_maps = []
            for c in range(NCORES):
                b, q0 = c // 4, (c % 4) * NQ
                m = dict(wk)
                m.update(mla_core_inputs(hT[b], posbase[b], off, q0, NQ))
                in_maps.append(m)
            res = _run(_prog(("mla",), lambda: build_mla(T, NQ, 410, 16)), in_maps)
            w_o = np.ascontiguousarray(f("mla_w_o")[j])
            in_maps = []
            for c in range(NCORES):
                b, q0 = c // 4, (c % 4) * NQ
                m = {"zT": res[c]["oT"], "hT": np.ascontiguousarray(hT[b][:, q0:q0 + NQ]), "w_o": w_o}
                m.update(lnp(i, 0))
                in_maps.append(m)
            res = _run(_prog(("proj", "plain"), lambda: build_proj2(NQ, 205, "plain")), in_maps)
        elif kind == 1:
            p = {k: f(k)[j] for k in ("rw_mu", "rw_w_rkv", "rw_w0", "rw_w1", "rw_w2", "rw_a0", "rw_a1", "rw_a2", "rw_g1", "rw_g2", "rw_k_k", "rw_k_a", "rw_r_k")}
            TP = ((T + 63) // 64) * 64
            in_maps, metas = [], []
            for c in range(NCORES):
                m, d, half = rwkv_core_inputs(c, hT[c // 4].T, p, TP)
                in_maps.append(m)
                metas.append((c // 4, d, half))
            res = _run(_prog(("rwkv",), lambda: build_rwkv(TP)), in_maps)
            ys = np.zeros((BATCH, 2, D, T), np.float32)
            bs = np.zeros((BATCH, 2, D, T), np.float32)
            gs = np.zeros((BATCH, D, T), np.float32)
            for c, (b, d, half) in enumerate(metas):
                rows = slice(half * 512, (half + 1) * 512)
                yy, bb = res[c]["yT"][:, :T], res[c]["bonT"][:, :T]
                ys[b, d, rows] = yy[:, ::-1] if d == 1 else yy
                bs[b, d, rows] = bb[:, ::-1] if d == 1 else bb
                if d == 0:
                    gs[b, rows] = res[c]["gT"][:, :T]
            w_o = np.ascontiguousarray(f("rw_w_o")[j])
            gnp = np.ascontiguousarray(np.concatenate([_pcol(f("rw_gn_g")[j]), _pcol(f("rw_gn_b")[j])], axis=1))
            in_maps = []
            for c in range(NCORES):
                b, q0 = c // 4, (c % 4) * NQ
                sl = slice(q0, q0 + NQ)
                m = {"y0T": np.ascontiguousarray(ys[b, 0][:, sl]), "y1T": np.ascontiguousarray(ys[b, 1][:, sl]),
                     "b0T": np.ascontiguousarray(bs[b, 0][:, sl]), "b1T": np.ascontiguousarray(bs[b, 1][:, sl]),
                     "gT": np.ascontiguousarray(gs[b][:, sl]), "gnp": gnp, "bones": bones,
                     "hT": np.ascontiguousarray(hT[b][:, sl]), "w_o": w_o}
                m.update(lnp(i, 0))
                in_maps.append(m)
            res = _run(_prog(("proj", "rwkv"), lambda: build_proj2(NQ, 205, "rwkv")), in_maps)
        else:
            p = {k: f(k)[j] for k in ("lru_w_in", "lru_conv_w", "lru_conv_b", "lru_gate_w", "lru_gate_b", "lru_lambda")}
            in_maps, metas = [], []
            for c in range(NCORES):
                m, d, sl = lru_core_inputs(c, hT[c // 4].T, p)
                in_maps.append(m)
                metas.append((c // 4, d, sl))
            res = _run(_prog(("lru",), lambda: build_lru(T)), in_maps)
            hd = np.zeros((BATCH, 2, LRU_W, T), np.float32)
            for c, (b, d, sl) in enumerate(metas):
                o = res[c]["hs"]
                o = o[:, ::-1] if d == 1 else o
                for jj, s in enumerate(sl):
                    hd[b, d, s * 128:(s + 1) * 128] = o[jj * 128:(jj + 1) * 128]
            w_o = np.ascontiguousarray(f("lru_w_o")[j])
            w_g = np.ascontiguousarray(p["lru_w_in"][:, :LRU_W])
            in_maps = []
            for c in range(NCORES):
                b, q0 = c // 4, (c % 4) * NQ
                sl = slice(q0, q0 + NQ)
                m = {"h0T": np.ascontiguousarray(hd[b, 0][:, sl]), "h1T": np.ascontiguousarray(hd[b, 1][:, sl]),
                     "w_g": w_g, "hT": np.ascontiguousarray(hT[b][:, sl]), "w_o": w_o}
                m.update(lnp(i, 0))
                in_maps.append(m)
            res = _run(_prog(("proj", "lru"), lambda: build_proj2(NQ, 205, "lru")), in_maps)
        h1 = [np.concatenate([res[b * 4 + q]["yT"] for q in range(4)], axis=1) for b in range(BATCH)]
        cwl = f("ffn_conv_w")[i]
        ffw = {"w_in": np.ascontiguousarray(f("ffn_w_in")[i]), "w_out": np.ascontiguousarray(f("ffn_w_out")[i]),
               "cw": np.ascontiguousarray(np.stack([_pcol(cwl[k]) for k in range(3)], axis=2).reshape(128, -1)),
               "cb": _pcol(f("ffn_conv_b")[i])}
        ffw.update(lnp(i, 1))
        in_maps = []
        for c in range(NCORES):
            b, q0 = c // 4, (c % 4) * NQ
            xT = np.zeros((D, NQ + 2), np.float32)
            lo, hi = max(q0 - 1, 0), min(q0 + NQ + 1, T)
            xT[:, lo - (q0 - 1):hi - (q0 - 1)] = h1[b][:, lo:hi]
            e = np.array([1.0 if q0 > 0 else 0.0, 1.0 if q0 + NQ < T else 0.0], np.float32)
            m = dict(ffw)
            m.update({"xT": xT, "edge": np.ascontiguousarray(np.tile(e[None, :], (128, 1)))})
            in_maps.append(m)
        res = _run(_prog(("ffn",), lambda: build_ffn(NQ, 205)), in_maps)
        hT = [np.ascontiguousarray(np.concatenate([res[b * 4 + q]["yT"] for q in range(4)], axis=1)) for b in range(BATCH)]
    out = np.stack([hT[b][:, NMETA:].T for b in range(BATCH)], axis=0)
    return np.ascontiguousarray(out.astype(np.float32))
```

```python
import numpy as np
import concourse.bass as bass
import concourse.mybir as mybir
from concourse.bass_utils import run_bass_kernel_spmd
from contextlib import ExitStack

F32 = mybir.dt.float32
BF16 = mybir.dt.bfloat16
AF = mybir.ActivationFunctionType
ALU = mybir.AluOpType
SEM_LIM = 30000

D = 1024
DFF = 2816
ALPHA = 8.0 ** 0.25
LN_EPS = 1e-5


class KB:
    def __init__(self):
        self.nc = bass.Bass("TRN2", target_bir_lowering=False)
        self.es = ExitStack()
        nc = self.nc
        self.eng = {"pe": nc.tensor, "act": nc.scalar, "dve": nc.vector, "pool": nc.gpsimd, "sp": nc.sync}
        self.sem, self.cnt, self.cur = {}, {}, {}
        self.nsem = 0
        self.seen = {e: {} for e in self.eng}
        self.wr, self.rd = {}, {}
        self.ndma = 16
        self.dma_rr = 0
        for e in ["pe", "act", "dve", "pool"]:
            self._newsem(e)
        for j in range(self.ndma):
            self._newsem("d%d" % j)
        self.ps = [self.es.enter_context(nc.psum_tensor("ps%d" % i, [128, 512], F32)) for i in range(8)]
        self.ninst = 0

    def _newsem(self, stream):
        key = "%s#%d" % (stream, self.nsem)
        self.nsem += 1
        self.sem[key] = self.es.enter_context(self.nc.semaphore("s%d" % self.nsem))
        self.cnt[key] = 0
        self.cur[stream] = key
        return key

    def din(self, name, shape, dt=F32):
        return self.nc.dram_tensor(name, list(shape), dt, kind="ExternalInput").ap()

    def dout(self, name, shape, dt=F32):
        return self.nc.dram_tensor(name, list(shape), dt, kind="ExternalOutput").ap()

    def sb(self, name, shape, dt=F32):
        return self.es.enter_context(self.nc.sbuf_tensor(name, list(shape), dt))

    def _wait(self, e, key, val):
        if val <= 0 or self.seen[e].get(key, 0) >= val:
            return
        self.eng[e].wait_ge(self.sem[key], val)
        self.seen[e][key] = val

    def op(self, e, fn, reads=(), writes=(), dma=False):
        own = None if e != "pe" else "pe#"
        for b in reads:
            for k, v in self.wr.get(b, {}).items():
                if own and k.startswith(own):
                    continue
                self._wait(e, k, v)
        for b in writes:
            for k, v in self.rd.get(b, {}).items():
                if own and k.startswith(own):
                    continue
                self._wait(e, k, v)
            if self.rd.get(b):
                for k, v in self.wr.get(b, {}).items():
                    if own and k.startswith(own):
                        continue
                    self._wait(e, k, v)
        if dma:
            stream = "d%d" % self.dma_rr
            self.dma_rr = (self.dma_rr + 1) % self.ndma
            inc = 16
            self._wait(e, self.cur[stream], self.cnt[self.cur[stream]])
        else:
            stream = e
            inc = 1
        key = self.cur[stream]
        if self.cnt[key] + inc > SEM_LIM:
            key = self._newsem(stream)
        ins = fn()
        ins.then_inc(self.sem[key], inc)
        self.cnt[key] += inc
        val = self.cnt[key]
        self.ninst += 1
        for b in reads:
            d = self.rd.setdefault(b, {})
            d[key] = max(d.get(key, 0), val)
        for b in writes:
            if self.rd.get(b):
                self.wr[b] = {key: val}
                self.rd[b] = {}
            else:
                self.wr.setdefault(b, {})[key] = val
                self.rd.setdefault(b, {})
        return ins

    def dma(self, out, in_, reads=(), writes=(), e="sp"):
        return self.op(e, lambda: self.eng[e].dma_start(out=out, in_=in_), reads=reads, writes=writes, dma=True)

    def barrier(self):
        for e in self.eng:
            for k, v in self.cnt.items():
                self._wait(e, k, v)

    def finish(self):
        for e in self.eng:
            for k, v in self.cnt.items():
                if e == "pe" and k.startswith("pe#"):
                    continue
                self._wait(e, k, v)
        self.es.close()
        return self.nc

    def load_w_bf16(self, dst, dst_name, src, K, N, nblk=2048):
        sv = src.rearrange("(c p) n -> p c n", p=128)
        for c in range(K // 128):
            for n0 in range(0, N, nblk):
                n1 = min(N, n0 + nblk)
                self.dma(dst[:, c, n0:n1], sv[:, c, n0:n1], writes=[dst_name], e="pool")

    def layernorm_inplace(self, s, sname, n, g, b, scr, ones, eps_t, ps_sum=6, ps_sq=7, out_bf=None, out_bf_name=None):
        nc = self.nc
        psum, pssq = self.ps[ps_sum], self.ps[ps_sq]
        pn1, pn2 = "ps%d" % ps_sum, "ps%d" % ps_sq
        for c in range(8):
            self.op("pe", lambda: nc.tensor.matmul(psum[:, :n], lhsT=ones[:], rhs=s[:, c, :n], start=(c == 0), stop=(c == 7)),
                    reads=[sname, "ones"], writes=[pn1])
        for c in range(8):
            sq = scr["sq"][c % 2]
            sqn = "sq%d" % (c % 2)
            self.op("act", lambda: nc.scalar.activation(out=sq[:, :n], in_=s[:, c, :n], func=AF.Square), reads=[sname], writes=[sqn])
            self.op("pe", lambda: nc.tensor.matmul(pssq[:, :n], lhsT=ones[:], rhs=sq[:, :n], start=(c == 0), stop=(c == 7)),
                    reads=[sqn, "ones"], writes=[pn2])
        mean, msq, var, rstd = scr["mean"], scr["msq"], scr["var"], scr["rstd"]
        self.op("act", lambda: nc.scalar.activation(out=mean[:, :n], in_=psum[:, :n], func=AF.Copy, scale=1.0 / D), reads=[pn1], writes=["mean"])
        self.op("dve", lambda: nc.vector.tensor_tensor(out=msq[:, :n], in0=mean[:, :n], in1=mean[:, :n], op=ALU.mult), reads=["mean"], writes=["msq"])
        self.op("dve", lambda: nc.vector.scalar_tensor_tensor(out=var[:, :n], in0=pssq[:, :n], scalar=1.0 / D, in1=msq[:, :n],
                                                              op0=ALU.mult, op1=ALU.subtract), reads=[pn2, "msq"], writes=["var"])
        self.op("act", lambda: nc.scalar.activation(out=var[:, :n], in_=var[:, :n], func=AF.Sqrt, bias=eps_t[:, 0:1], scale=1.0),
                reads=["var", "eps"], writes=["var"])
        self.op("dve", lambda: nc.vector.reciprocal(out=rstd[:, :n], in_=var[:, :n]), reads=["var"], writes=["rstd"])
        for c in range(8):
            self.op("dve", lambda: nc.vector.tensor_tensor(out=s[:, c, :n], in0=s[:, c, :n], in1=mean[:, :n], op=ALU.subtract),
                    reads=[sname, "mean"], writes=[sname])
            self.op("pool", lambda: nc.gpsimd.tensor_tensor(out=s[:, c, :n], in0=s[:, c, :n], in1=rstd[:, :n], op=ALU.mult),
                    reads=[sname, "rstd"], writes=[sname])
            self.op("act", lambda: nc.scalar.activation(out=s[:, c, :n], in_=s[:, c, :n], func=AF.Identity,
                                                        scale=g[:, c:c + 1], bias=b[:, c:c + 1]),
                    reads=[sname, "lnp"], writes=[sname])
            if out_bf is not None:
                self.op("pool", lambda: nc.gpsimd.tensor_copy(out=out_bf[:, c, :n], in_=s[:, c, :n]), reads=[sname], writes=[out_bf_name])

    def ln_scratch(self, n):
        scr = {"sq": [self.sb("sq0", [128, n]), self.sb("sq1", [128, n])]}
        for nm in ["mean", "msq", "var", "rstd"]:
            scr[nm] = self.sb(nm, [128, n])
        ones = self.sb("ones", [128, 128])
        eps_t = self.sb("eps", [128, 1])
        self.op("pool", lambda: self.nc.gpsimd.memset(ones[:], 1.0), writes=["ones"])
        self.op("pool", lambda: self.nc.gpsimd.memset(eps_t[:], LN_EPS), writes=["eps"])
        return scr, ones, eps_t


def build_ffn(ntok, TW):
    kb = KB()
    nc = kb.nc
    ntile = ntok // TW
    assert ntile * TW == ntok
    W = TW + 2
    NG = DFF // 128
    xT = kb.din("xT", [D, ntok + 2])
    edge = kb.din("edge", [128, 2])
    w_in = kb.din("w_in", [D, 2 * DFF])
    cw = kb.din("cw", [128, NG * 3])
    cb = kb.din("cb", [128, NG])
    w_out = kb.din("w_out", [DFF, D])
    lng = kb.din("lng", [128, 8])
    lnb = kb.din("lnb", [128, 8])
    yT = kb.dout("yT", [D, ntok])

    w_in_sb = kb.sb("w_in_sb", [128, 8, 2 * DFF], BF16)
    w_out_sb = kb.sb("w_out_sb", [128, NG, D], BF16)
    cw_sb = kb.sb("cw_sb", [128, NG * 3])
    cb_sb = kb.sb("cb_sb", [128, NG])
    g_sb = kb.sb("g_sb", [128, 8])
    b_sb = kb.sb("b_sb", [128, 8])
    edge_sb = kb.sb("edge_sb", [128, 2])
    xt = [kb.sb("x%d" % i, [128, 8, W]) for i in range(2)]
    xb = kb.sb("xb", [128, 8, W], BF16)
    ab = kb.sb("ab", [128, NG, TW], BF16)
    s2 = [kb.sb("s2_%d" % i, [128, 8, TW]) for i in range(2)]
    gs = [kb.sb("gs%d" % i, [128, W]) for i in range(2)]
    acc = [kb.sb("acc%d" % i, [128, TW]) for i in range(2)]
    sg = [kb.sb("sg%d" % i, [128, TW]) for i in range(2)]
    scr, ones, eps_t = kb.ln_scratch(TW)

    kb.dma(cw_sb[:], cw, writes=["cw"])
    kb.dma(cb_sb[:], cb, writes=["cb"])
    kb.dma(g_sb[:], lng, writes=["lnp"])
    kb.dma(b_sb[:], lnb, writes=["lnp"])
    kb.dma(edge_sb[:], edge, writes=["edge"])
    xv = xT.rearrange("(c p) t -> p c t", p=128)
    yv = yT.rearrange("(c p) t -> p c t", p=128)
    kb.dma(xt[0][:, :, :], xv[:, :, 0:W], writes=["x0"])
    kb.load_w_bf16(w_in_sb, "w_in", w_in, D, 2 * DFF)
    kb.load_w_bf16(w_out_sb, "w_out", w_out, DFF, D, nblk=1024)

    for i in range(ntile):
        t0 = i * TW
        x = xt[i % 2]
        xn = "x%d" % (i % 2)
        if i + 1 < ntile:
            kb.dma(xt[(i + 1) % 2][:, :, :], xv[:, :, t0 + TW:t0 + TW + W], writes=["x%d" % ((i + 1) % 2)])
        kb.op("pool", lambda: nc.gpsimd.tensor_copy(out=xb[:, :, :], in_=x[:, :, :]), reads=[xn], writes=["xb"])
        for gc in range(NG):
            bg, bu = gc % 2, 2 + gc % 2
            pg, pu = kb.ps[bg], kb.ps[bu]
            for kc in range(8):
                kb.op("pe", lambda: nc.tensor.matmul(pg[:, :W], lhsT=w_in_sb[:, kc, gc * 128:(gc + 1) * 128], rhs=xb[:, kc, :],
                                                     start=(kc == 0), stop=(kc == 7)), reads=["w_in", "xb"], writes=["ps%d" % bg])
            for kc in range(8):
                kb.op("pe", lambda: nc.tensor.matmul(pu[:, :W], lhsT=w_in_sb[:, kc, DFF + gc * 128:DFF + (gc + 1) * 128], rhs=xb[:, kc, :],
                                                     start=(kc == 0), stop=(kc == 7)), reads=["w_in", "xb"], writes=["ps%d" % bu])
            g_, a_, s_ = gs[gc % 2], acc[gc % 2], sg[gc % 2]
            gn, an, sn = "gs%d" % (gc % 2), "acc%d" % (gc % 2), "sg%d" % (gc % 2)
            kb.op("act", lambda: nc.scalar.copy(out=g_[:, :W], in_=pg[:, :W]), reads=["ps%d" % bg], writes=[gn])
            if i == 0:
                kb.op("dve", lambda: nc.vector.tensor_scalar(out=g_[:, 0:1], in0=g_[:, 0:1], scalar1=edge_sb[:, 0:1], scalar2=None, op0=ALU.mult),
                      reads=[gn, "edge"], writes=[gn])
            if i == ntile - 1:
                kb.op("dve", lambda: nc.vector.tensor_scalar(out=g_[:, W - 1:W], in0=g_[:, W - 1:W], scalar1=edge_sb[:, 1:2], scalar2=None, op0=ALU.mult),
                      reads=[gn, "edge"], writes=[gn])
            kb.op("dve", lambda: nc.vector.tensor_scalar(out=a_[:, :], in0=g_[:, 0:TW], scalar1=cw_sb[:, gc * 3:gc * 3 + 1], scalar2=None, op0=ALU.mult),
                  reads=[gn, "cw"], writes=[an])
            kb.op("dve", lambda: nc.vector.scalar_tensor_tensor(out=a_[:, :], in0=g_[:, 1:TW + 1], scalar=cw_sb[:, gc * 3 + 1:gc * 3 + 2], in1=a_[:, :],
                                                                op0=ALU.mult, op1=ALU.add), reads=[gn, "cw", an], writes=[an])
            kb.op("dve", lambda: nc.vector.scalar_tensor_tensor(out=a_[:, :], in0=g_[:, 2:TW + 2], scalar=cw_sb[:, gc * 3 + 2:gc * 3 + 3], in1=a_[:, :],
                                                                op0=ALU.mult, op1=ALU.add), reads=[gn, "cw", an], writes=[an])
            kb.op("act", lambda: nc.scalar.activation(out=s_[:, :], in_=a_[:, :], func=AF.Silu, bias=cb_sb[:, gc:gc + 1], scale=1.0),
                  reads=[an, "cb"], writes=[sn])
            kb.op("dve", lambda: nc.vector.tensor_tensor(out=ab[:, gc, :], in0=s_[:, :], in1=pu[:, 1:TW + 1], op=ALU.mult),
                  reads=[sn, "ps%d" % bu], writes=["ab"])
        s = s2[i % 2]
        sname = "s2_%d" % (i % 2)
        for oc in range(8):
            bf = 4 + oc % 2
            pf = kb.ps[bf]
            for kc in range(NG):
                kb.op("pe", lambda: nc.tensor.matmul(pf[:, :TW], lhsT=w_out_sb[:, kc, oc * 128:(oc + 1) * 128], rhs=ab[:, kc, :],
                                                     start=(kc == 0), stop=(kc == NG - 1)), reads=["w_out", "ab"], writes=["ps%d" % bf])
            kb.op("dve", lambda: nc.vector.scalar_tensor_tensor(out=s[:, oc, :], in0=x[:, oc, 1:TW + 1], scalar=ALPHA, in1=pf[:, :TW],
                                                                op0=ALU.mult, op1=ALU.add), reads=[xn, "ps%d" % bf], writes=[sname])
        kb.layernorm_inplace(s, sname, TW, g_sb, b_sb, scr, ones, eps_t)
        kb.dma(yv[:, :, t0:t0 + TW], s[:, :, :], reads=[sname])
    return kb.finish()


def _pcol(v):
    v = np.asarray(v)
    return np.ascontiguousarray(v.reshape(-1, 128).T)


def build_proj(ntok, TW, Dz):
    kb = KB()
    nc = kb.nc
    ntile = ntok // TW
    assert ntile * TW == ntok
    NZ = Dz // 128
    zT = kb.din("zT", [Dz, ntok])
    hT = kb.din("hT", [D, ntok])
    w_o = kb.din("w_o", [Dz, D])
    lng = kb.din("lng", [128, 8])
    lnb = kb.din("lnb", [128, 8])
    yT = kb.dout("yT", [D, ntok])
    w_sb = kb.sb("w_sb", [128, NZ, D], BF16)
    g_sb = kb.sb("g_sb", [128, 8])
    b_sb = kb.sb("b_sb", [128, 8])
    zb = [kb.sb("zb%d" % i, [128, NZ, TW], BF16) for i in range(2)]
    ht = [kb.sb("h%d" % i, [128, 8, TW]) for i in range(2)]
    s2 = [kb.sb("s2_%d" % i, [128, 8, TW]) for i in range(2)]
    scr, ones, eps_t = kb.ln_scratch(TW)
    kb.dma(g_sb[:], lng, writes=["lnp"])
    kb.dma(b_sb[:], lnb, writes=["lnp"])
    zv = zT.rearrange("(c p) t -> p c t", p=128)
    hv = hT.rearrange("(c p) t -> p c t", p=128)
    yv = yT.rearrange("(c p) t -> p c t", p=128)
    kb.load_w_bf16(w_sb, "w_o", w_o, Dz, D, nblk=1024)
    for i in range(ntile):
        t0 = i * TW
        z, zn = zb[i % 2], "zb%d" % (i % 2)
        h, hn = ht[i % 2], "h%d" % (i % 2)
        s, sname = s2[i % 2], "s2_%d" % (i % 2)
        for c in range(NZ):
            kb.dma(z[:, c, :], zv[:, c, t0:t0 + TW], writes=[zn], e="pool")
        kb.dma(h[:, :, :], hv[:, :, t0:t0 + TW], writes=[hn])
        for oc in range(8):
            bf = 4 + oc % 2
            pf = kb.ps[bf]
            for kc in range(NZ):
                kb.op("pe", lambda: nc.tensor.matmul(pf[:, :TW], lhsT=w_sb[:, kc, oc * 128:(oc + 1) * 128], rhs=z[:, kc, :],
                                                     start=(kc == 0), stop=(kc == NZ - 1)), reads=["w_o", zn], writes=["ps%d" % bf])
            kb.op("dve", lambda: nc.vector.scalar_tensor_tensor(out=s[:, oc, :], in0=h[:, oc, :], scalar=ALPHA, in1=pf[:, :TW],
                                                                op0=ALU.mult, op1=ALU.add), reads=[hn, "ps%d" % bf], writes=[sname])
        kb.layernorm_inplace(s, sname, TW, g_sb, b_sb, scr, ones, eps_t)
        kb.dma(yv[:, :, t0:t0 + TW], s[:, :, :], reads=[sname])
    return kb.finish()


I32 = mybir.dt.int32
TWO_PI = 6.283185307179586
CW1 = 6.28125
CW2 = TWO_PI - CW1
MAGIC = 12582912.0
PI_LO = 3.1415925
RMS_EPS = 1e-6
QR, KVR, ROPE, NOPE = 256, 128, 64, 128
MLA_IN = QR + KVR + ROPE


def build_mla(T, nq, QW, H):
    kb = KB()
    nc = kb.nc
    SCALE = float((NOPE + ROPE) ** -0.5)
    ktl = [(t0, min(512, T - t0)) for t0 in range(0, T, 512)]
    kts = [(t0, min(128, T - t0)) for t0 in range(0, T, 128)]
    NKT = len(kts)
    nqt = nq // QW
    assert nqt * QW == nq
    hT = kb.din("hT", [D, T])
    hqT = kb.din("hqT", [D, nq])
    posk = kb.din("posk", [64, T], I32)
    offk = kb.din("offk", [64, T])
    posq = kb.din("posq", [64, nq], I32)
    offq = kb.din("offq", [64, nq])
    rconst = kb.din("rconst", [64, 3])
    w_in = kb.din("w_in", [D, MLA_IN])
    w_pesw = kb.din("w_pesw", [D, ROPE])
    qn_g = kb.din("qn_g", [128, 2])
    kvn_g = kb.din("kvn_g", [128, 1])
    kvn_rep = kb.din("kvn_rep", [128, 128])
    w_q = kb.din("w_q", [QR, H * 192])
    w_qsw = kb.din("w_qsw", [QR, H * 64])
    w_ukT = kb.din("w_ukT", [128, H * 128])
    w_uv = kb.din("w_uv", [128, H * 128])
    oT = kb.dout("oT", [H * 128, nq])

    w_in_sb = kb.sb("w_in_sb", [128, 8, MLA_IN], BF16)
    w_pesw_sb = kb.sb("w_pesw_sb", [128, 8, ROPE], BF16)
    w_q_sb = kb.sb("w_q_sb", [128, 2, H * 192], BF16)
    w_qsw_sb = kb.sb("w_qsw_sb", [128, 2, H * 64], BF16)
    w_ukT_sb = kb.sb("w_ukT_sb", [128, 1, H * 128], BF16)
    w_uv_sb = kb.sb("w_uv_sb", [128, 1, H * 128], BF16)
    qn_sb = kb.sb("qn_sb", [128, 2])
    kvn_sb = kb.sb("kvn_sb", [128, 1])
    kvn_rep_sb = kb.sb("kvn_rep_sb", [128, 128])
    rc_sb = kb.sb("rc_sb", [64, 3])
    ckvT = kb.sb("ckvT", [128, T], BF16)
    kpeT = kb.sb("kpeT", [64, T], BF16)
    ckvK = kb.sb("ckvK", [128, NKT, 128], BF16)
    hb = [kb.sb("hb%d" % i, [128, 8, 512], BF16) for i in range(2)]
    ones = kb.sb("ones", [128, 128])
    ones_b = kb.sb("ones_b", [128, 128], BF16)
    eps_t = kb.sb("eps", [128, 1])
    sq = kb.sb("sq", [128, 512])
    rs = kb.sb("rs", [128, 512])
    rs1 = kb.sb("rs1", [128, 1])
    junk = kb.sb("junk", [128, 128])
    pos_i = kb.sb("pos_i", [64, 512], I32)
    off_t = kb.sb("off_t", [64, 512])
    ang = kb.sb("ang", [64, 512])
    nr = kb.sb("nr", [64, 512])
    rr = kb.sb("rr", [64, 512])
    rp = kb.sb("rp", [64, 512])
    CC = kb.sb("CC", [64, 512])
    SS = kb.sb("SS", [64, 512])
    tmp1 = kb.sb("tmp1", [64, 512])
    tmp2 = kb.sb("tmp2", [64, 512])
    cqb = kb.sb("cqb", [128, 2, QW], BF16)
    qnb = kb.sb("qnb", [128, QW], BF16)
    qlat = [kb.sb("qlat%d" % i, [128, QW], BF16) for i in range(2)]
    qpe = [kb.sb("qpe%d" % i, [64, QW], BF16) for i in range(2)]
    pT = [kb.sb("pT%d" % i, [128, QW], BF16) for i in range(4)]
    den = kb.sb("den", [128, QW])
    olat = kb.sb("olat", [128, QW], BF16)
    oh = [kb.sb("oh0", [128, QW])] * 2

    kb.op("pool", lambda: nc.gpsimd.memset(ones[:], 1.0), writes=["ones"])
    kb.op("pool", lambda: nc.gpsimd.memset(ones_b[:], 1.0), writes=["ones_b"])
    kb.op("pool", lambda: nc.gpsimd.memset(eps_t[:], RMS_EPS), writes=["eps"])
    kb.dma(qn_sb[:], qn_g, writes=["par"])
    kb.dma(kvn_sb[:], kvn_g, writes=["par"])
    kb.dma(kvn_rep_sb[:], kvn_rep, writes=["par"])
    kb.dma(rc_sb[:], rconst, writes=["par"])
    kb.load_w_bf16(w_in_sb, "w", w_in, D, MLA_IN)
    kb.load_w_bf16(w_pesw_sb, "w", w_pesw, D, ROPE)
    kb.load_w_bf16(w_q_sb, "w", w_q, QR, H * 192)
    kb.load_w_bf16(w_qsw_sb, "w", w_qsw, QR, H * 64)
    kb.load_w_bf16(w_ukT_sb, "w", w_ukT, 128, H * 128)
    kb.load_w_bf16(w_uv_sb, "w", w_uv, 128, H * 128)

    def rope_tables(pos_ap, off_ap, n):
        kb.dma(pos_i[:, :n], pos_ap, writes=["pos_i"])
        kb.dma(off_t[:, :n], off_ap, writes=["off_t"])
        kb.op("dve", lambda: nc.vector.tensor_copy(out=ang[:, :n], in_=pos_i[:, :n]), reads=["pos_i"], writes=["ang"])
        kb.op("dve", lambda: nc.vector.tensor_tensor(out=ang[:, :n], in0=ang[:, :n], in1=off_t[:, :n], op=ALU.add), reads=["ang", "off_t"], writes=["ang"])
        kb.op("dve", lambda: nc.vector.tensor_scalar(out=ang[:, :n], in0=ang[:, :n], scalar1=rc_sb[:, 0:1], scalar2=None, op0=ALU.mult),
              reads=["ang", "par"], writes=["ang"])
        kb.op("dve", lambda: nc.vector.tensor_scalar(out=nr[:, :n], in0=ang[:, :n], scalar1=1.0 / TWO_PI, scalar2=MAGIC, op0=ALU.mult, op1=ALU.add),
              reads=["ang"], writes=["nr"])
        kb.op("dve", lambda: nc.vector.tensor_scalar(out=nr[:, :n], in0=nr[:, :n], scalar1=-MAGIC, scalar2=None, op0=ALU.add), reads=["nr"], writes=["nr"])
        kb.op("dve", lambda: nc.vector.scalar_tensor_tensor(out=rr[:, :n], in0=nr[:, :n], scalar=-CW1, in1=ang[:, :n], op0=ALU.mult, op1=ALU.add),
              reads=["nr", "ang"], writes=["rr"])
        kb.op("dve", lambda: nc.vector.scalar_tensor_tensor(out=rr[:, :n], in0=nr[:, :n], scalar=-CW2, in1=rr[:, :n], op0=ALU.mult, op1=ALU.add),
              reads=["nr", "rr"], writes=["rr"])
        for tab, tn, col in ((CC, "CC", 1), (SS, "SS", 2)):
            kb.op("dve", lambda: nc.vector.tensor_scalar(out=rp[:, :n], in0=rr[:, :n], scalar1=rc_sb[:, col:col + 1], scalar2=None, op0=ALU.add),
                  reads=["rr", "par"], writes=["rp"])
            kb.op("dve", lambda: nc.vector.tensor_scalar(out=nr[:, :n], in0=rp[:, :n], scalar1=1.0 / TWO_PI, scalar2=MAGIC, op0=ALU.mult, op1=ALU.add),
                  reads=["rp"], writes=["nr"])
            kb.op("dve", lambda: nc.vector.tensor_scalar(out=nr[:, :n], in0=nr[:, :n], scalar1=-MAGIC, scalar2=None, op0=ALU.add), reads=["nr"], writes=["nr"])
            kb.op("dve", lambda: nc.vector.scalar_tensor_tensor(out=rp[:, :n], in0=nr[:, :n], scalar=-TWO_PI, in1=rp[:, :n], op0=ALU.mult, op1=ALU.add),
                  reads=["nr", "rp"], writes=["rp"])
            kb.op("dve", lambda: nc.vector.tensor_scalar(out=rp[:, :n], in0=rp[:, :n], scalar1=-PI_LO, scalar2=PI_LO, op0=ALU.max, op1=ALU.min),
                  reads=["rp"], writes=["rp"])
            kb.op("act", lambda: nc.scalar.activation(out=tab[:, :n], in_=rp[:, :n], func=AF.Sin), reads=["rp"], writes=[tn])

    def rope_apply(dst, dst_name, p_pre, pn_pre, p_sw, pn_sw, n):
        kb.op("dve", lambda: nc.vector.tensor_tensor(out=tmp1[:, :n], in0=p_pre[0:64, :n], in1=CC[:, :n], op=ALU.mult), reads=[pn_pre, "CC"], writes=["tmp1"])
        kb.op("dve", lambda: nc.vector.tensor_tensor(out=tmp2[:, :n], in0=p_sw[0:64, :n], in1=SS[:, :n], op=ALU.mult), reads=[pn_sw, "SS"], writes=["tmp2"])
        kb.op("pool", lambda: nc.gpsimd.tensor_tensor(out=dst, in0=tmp1[:, :n], in1=tmp2[:, :n], op=ALU.add), reads=["tmp1", "tmp2"], writes=[dst_name])

    hv = hT.rearrange("(c p) t -> p c t", p=128)
    hqv = hqT.rearrange("(c p) t -> p c t", p=128)

    def load_hb(i, src, t0, n):
        for c in range(8):
            kb.dma(hb[i][:, c, :n], src[:, c, t0:t0 + n], writes=["hb%d" % i], e="pool")

    load_hb(0, hv, ktl[0][0], ktl[0][1])
    for i, (t0, n) in enumerate(ktl):
        x, xn = hb[i % 2], "hb%d" % (i % 2)
        if i + 1 < len(ktl):
            load_hb((i + 1) % 2, hv, ktl[i + 1][0], ktl[i + 1][1])
        rope_tables(posk[:, t0:t0 + n], offk[:, t0:t0 + n], n)
        p0, p1, p2, p7 = kb.ps[4], kb.ps[5], kb.ps[6], kb.ps[7]
        for kc in range(8):
            kb.op("pe", lambda: nc.tensor.matmul(p0[:, :n], lhsT=w_in_sb[:, kc, QR:QR + KVR], rhs=x[:, kc, :n], start=(kc == 0), stop=(kc == 7)),
                  reads=["w", xn], writes=["ps4"])
        for kc in range(8):
            kb.op("pe", lambda: nc.tensor.matmul(p1[0:64, :n], lhsT=w_in_sb[:, kc, QR + KVR:MLA_IN], rhs=x[:, kc, :n], start=(kc == 0), stop=(kc == 7)),
                  reads=["w", xn], writes=["ps5"])
        for kc in range(8):
            kb.op("pe", lambda: nc.tensor.matmul(p2[0:64, :n], lhsT=w_pesw_sb[:, kc, :], rhs=x[:, kc, :n], start=(kc == 0), stop=(kc == 7)),
                  reads=["w", xn], writes=["ps6"])
        kb.op("act", lambda: nc.scalar.activation(out=sq[:, :n], in_=p0[:, :n], func=AF.Square), reads=["ps4"], writes=["sq"])
        kb.op("pe", lambda: nc.tensor.matmul(p7[:, :n], lhsT=ones[:], rhs=sq[:, :n], start=True, stop=True), reads=["ones", "sq"], writes=["ps7"])
        kb.op("act", lambda: nc.scalar.activation(out=rs[:, :n], in_=p7[:, :n], func=AF.Sqrt, bias=eps_t[:, 0:1], scale=1.0 / KVR), reads=["ps7", "eps"], writes=["rs"])
        kb.op("dve", lambda: nc.vector.reciprocal(out=rs[:, :n], in_=rs[:, :n]), reads=["rs"], writes=["rs"])
        kb.op("dve", lambda: nc.vector.scalar_tensor_tensor(out=ckvT[:, t0:t0 + n], in0=p0[:, :n], scalar=kvn_sb[:, 0:1], in1=rs[:, :n],
                                                            op0=ALU.mult, op1=ALU.mult), reads=["ps4", "par", "rs"], writes=["ckvT"])
        rope_apply(kpeT[:, t0:t0 + n], "kpeT", p1, "ps5", p2, "ps6", n)
        for j0 in range(0, n, 128):
            jn = min(128, n - j0)
            j = (t0 + j0) // 128
            pk = kb.ps[j % 2]
            pkn = "ps%d" % (j % 2)
            for kc in range(8):
                kb.op("pe", lambda: nc.tensor.matmul(pk[:jn, :128], lhsT=x[:, kc, j0:j0 + jn], rhs=w_in_sb[:, kc, QR:QR + KVR], start=(kc == 0), stop=(kc == 7)),
                      reads=["w", xn], writes=[pkn])
            kb.op("act", lambda: nc.scalar.activation(out=junk[:jn, :], in_=pk[:jn, :128], func=AF.Square, accum_out=rs1[:jn, 0:1]),
                  reads=[pkn], writes=["junk", "rs1"])
            kb.op("act", lambda: nc.scalar.activation(out=rs1[:jn, :], in_=rs1[:jn, :], func=AF.Sqrt, bias=eps_t[:jn, 0:1], scale=1.0 / KVR),
                  reads=["rs1", "eps"], writes=["rs1"])
            kb.op("dve", lambda: nc.vector.reciprocal(out=rs1[:jn, :], in_=rs1[:jn, :]), reads=["rs1"], writes=["rs1"])
            kb.op("dve", lambda: nc.vector.scalar_tensor_tensor(out=ckvK[:jn, j, :], in0=pk[:jn, :128], scalar=rs1[:jn, 0:1], in1=kvn_rep_sb[:jn, :],
                                                                op0=ALU.mult, op1=ALU.mult), reads=[pkn, "rs1", "par"], writes=["ckvK"])

    ov = oT.rearrange("(h p) t -> p h t", p=128)
    hcount = 0
    for qi in range(nqt):
        q0 = qi * QW
        load_hb(0, hqv, q0, QW)
        x, xn = hb[0], "hb0"
        rope_tables(posq[:, q0:q0 + QW], offq[:, q0:q0 + QW], QW)
        p4, p5, p7 = kb.ps[4], kb.ps[5], kb.ps[7]
        for c, (pp, ppn) in enumerate(((p4, "ps4"), (p5, "ps5"))):
            for kc in range(8):
                kb.op("pe", lambda: nc.tensor.matmul(pp[:, :QW], lhsT=w_in_sb[:, kc, c * 128:(c + 1) * 128], rhs=x[:, kc, :QW], start=(kc == 0), stop=(kc == 7)),
                      reads=["w", xn], writes=[ppn])
        for c, (pp, ppn) in enumerate(((p4, "ps4"), (p5, "ps5"))):
            kb.op("act", lambda: nc.scalar.activation(out=sq[:, :QW], in_=pp[:, :QW], func=AF.Square), reads=[ppn], writes=["sq"])
            kb.op("pe", lambda: nc.tensor.matmul(p7[:, :QW], lhsT=ones[:], rhs=sq[:, :QW], start=(c == 0), stop=(c == 1)), reads=["ones", "sq"], writes=["ps7"])
        kb.op("act", lambda: nc.scalar.activation(out=rs[:, :QW], in_=p7[:, :QW], func=AF.Sqrt, bias=eps_t[:, 0:1], scale=1.0 / QR), reads=["ps7", "eps"], writes=["rs"])
        kb.op("dve", lambda: nc.vector.reciprocal(out=rs[:, :QW], in_=rs[:, :QW]), reads=["rs"], writes=["rs"])
        for c, (pp, ppn) in enumerate(((p4, "ps4"), (p5, "ps5"))):
            kb.op("dve", lambda: nc.vector.scalar_tensor_tensor(out=cqb[:, c, :], in0=pp[:, :QW], scalar=qn_sb[:, c:c + 1], in1=rs[:, :QW],
                                                                op0=ALU.mult, op1=ALU.mult), reads=[ppn, "par", "rs"], writes=["cqb"])
        for h in range(H):
            ql, qln = qlat[hcount % 2], "qlat%d" % (hcount % 2)
            qp, qpn = qpe[hcount % 2], "qpe%d" % (hcount % 2)
            o_sb, o_n = oh[0], "oh0"
            hcount += 1
            p4, p5, p6 = kb.ps[4], kb.ps[5], kb.ps[6]
            for kc in range(2):
                kb.op("pe", lambda: nc.tensor.matmul(p4[:, :QW], lhsT=w_q_sb[:, kc, h * 192:h * 192 + 128], rhs=cqb[:, kc, :], start=(kc == 0), stop=(kc == 1)),
                      reads=["w", "cqb"], writes=["ps4"])
            kb.op("act", lambda: nc.scalar.copy(out=qnb[:, :], in_=p4[:, :QW]), reads=["ps4"], writes=["qnb"])
            for kc in range(2):
                kb.op("pe", lambda: nc.tensor.matmul(p5[0:64, :QW], lhsT=w_q_sb[:, kc, h * 192 + 128:h * 192 + 192], rhs=cqb[:, kc, :], start=(kc == 0), stop=(kc == 1)),
                      reads=["w", "cqb"], writes=["ps5"])
            for kc in range(2):
                kb.op("pe", lambda: nc.tensor.matmul(p6[0:64, :QW], lhsT=w_qsw_sb[:, kc, h * 64:h * 64 + 64], rhs=cqb[:, kc, :], start=(kc == 0), stop=(kc == 1)),
                      reads=["w", "cqb"], writes=["ps6"])
            rope_apply(qp[:, :], qpn, p5, "ps5", p6, "ps6", QW)
            kb.op("pe", lambda: nc.tensor.matmul(p4[:, :QW], lhsT=w_ukT_sb[:, 0, h * 128:(h + 1) * 128], rhs=qnb[:, :], start=True, stop=True),
                  reads=["w", "qnb"], writes=["ps4"])
            kb.op("act", lambda: nc.scalar.copy(out=ql[:, :], in_=p4[:, :QW]), reads=["ps4"], writes=[qln])
            pO, pD = kb.ps[2], kb.ps[3]

            def qk(j):
                k0, kn = kts[j]
                ps_, psn = kb.ps[j % 2], "ps%d" % (j % 2)
                kb.op("pe", lambda: nc.tensor.matmul(ps_[:kn, :QW], lhsT=ckvT[:, k0:k0 + kn], rhs=ql[:, :], start=True, stop=False),
                      reads=["ckvT", qln], writes=[psn])
                kb.op("pe", lambda: nc.tensor.matmul(ps_[:kn, :QW], lhsT=kpeT[:, k0:k0 + kn], rhs=qp[:, :], start=False, stop=True),
                      reads=["kpeT", qpn], writes=[psn])

            qk(0)
            for j in range(NKT):
                k0, kn = kts[j]
                ps_, psn = kb.ps[j % 2], "ps%d" % (j % 2)
                pt, ptn = pT[j % 4], "pT%d" % (j % 4)
                kb.op("act", lambda: nc.scalar.activation(out=pt[:kn, :], in_=ps_[:kn, :QW], func=AF.Exp, scale=SCALE), reads=[psn], writes=[ptn])
                if j + 1 < NKT:
                    qk(j + 1)
                kb.op("pe", lambda: nc.tensor.matmul(pO[:, :QW], lhsT=ckvK[:kn, j, :], rhs=pt[:kn, :], start=(j == 0), stop=(j == NKT - 1)),
                      reads=["ckvK", ptn], writes=["ps2"])
                kb.op("pe", lambda: nc.tensor.matmul(pD[:, :QW], lhsT=ones_b[:kn, :], rhs=pt[:kn, :], start=(j == 0), stop=(j == NKT - 1)),
                      reads=["ones_b", ptn], writes=["ps3"])
            kb.op("dve", lambda: nc.vector.reciprocal(out=den[:, :], in_=pD[:, :QW]), reads=["ps3"], writes=["den"])
            kb.op("dve", lambda: nc.vector.tensor_tensor(out=olat[:, :], in0=pO[:, :QW], in1=den[:, :], op=ALU.mult), reads=["ps2", "den"], writes=["olat"])
            kb.op("pe", lambda: nc.tensor.matmul(p4[:, :QW], lhsT=w_uv_sb[:, 0, h * 128:(h + 1) * 128], rhs=olat[:, :], start=True, stop=True),
                  reads=["w", "olat"], writes=["ps4"])
            kb.op("act", lambda: nc.scalar.copy(out=o_sb[:, :], in_=p4[:, :QW]), reads=["ps4"], writes=[o_n])
            kb.dma(ov[:, h, q0:q0 + QW], o_sb[:, :], reads=[o_n])
    return kb.finish()


def mla_weights_host(w_in, qn, w_q, kvn, w_kv_up, H):
    w_in = np.asarray(w_in, np.float32)
    w_q = np.asarray(w_q, np.float32)
    w_kv_up = np.asarray(w_kv_up, np.float32)
    pe = w_in[:, QR + KVR:]
    w_pesw = np.concatenate([pe[:, 32:], pe[:, :32]], axis=1)
    wq3 = w_q.reshape(QR, H, 192)
    w_qsw = np.concatenate([wq3[:, :, 160:192], wq3[:, :, 128:160]], axis=2).reshape(QR, H * 64)
    w_ukT = np.transpose(w_kv_up[:, :, :128], (2, 1, 0)).reshape(128, H * 128)
    w_uv = w_kv_up[:, :, 128:].reshape(128, H * 128)
    inv_freq = (10000.0 ** (-np.arange(0, 64, 2, dtype=np.float32) / np.float32(64))).astype(np.float32)
    rconst = np.zeros((64, 3), np.float32)
    rconst[:, 0] = np.concatenate([inv_freq, inv_freq])
    rconst[:, 1] = np.float32(np.pi / 2)
    rconst[:32, 2] = np.float32(np.pi)
    return {
        "w_in": np.ascontiguousarray(w_in), "w_pesw": np.ascontiguousarray(w_pesw),
        "qn_g": _pcol(qn), "kvn_g": _pcol(kvn), "kvn_rep": np.ascontiguousarray(np.tile(np.asarray(kvn, np.float32)[None, :], (128, 1))),
        "w_q": np.ascontiguousarray(w_q), "w_qsw": np.ascontiguousarray(w_qsw),
        "w_ukT": np.ascontiguousarray(w_ukT), "w_uv": np.ascontiguousarray(w_uv), "rconst": rconst,
    }


def mla_core_inputs(hT, posbase, off, q0, nq):
    T = hT.shape[1]
    return {
        "hT": hT, "hqT": np.ascontiguousarray(hT[:, q0:q0 + nq]),
        "posk": np.ascontiguousarray(np.tile(posbase[None, :], (64, 1))), "offk": np.ascontiguousarray(np.tile(off[None, :], (64, 1))),
        "posq": np.ascontiguousarray(np.tile(posbase[None, q0:q0 + nq], (64, 1))), "offq": np.ascontiguousarray(np.tile(off[None, q0:q0 + nq], (64, 1))),
    }


LRU_W = 1280
NU = 5


def build_lru(T):
    kb = KB()
    nc = kb.nc
    TWL = 508
    tiles = [(t0, min(TWL, T - t0)) for t0 in range(0, T, TWL)]
    xT = kb.din("xT", [D, T + 4])
    w_u = kb.din("w_u", [D, NU * 256])
    taps = kb.din("taps", [128, NU * 2 * 5])
    cbias = kb.din("cbias", [128, NU * 2])
    gw = kb.din("gw", [256, NU * 256])
    gb = kb.din("gb", [128, NU * 2])
    lam = kb.din("lam", [128, NU])
    hs = kb.dout("hs", [NU * 128, T])

    w_u_sb = kb.sb("w_u_sb", [128, 8, NU * 256], BF16)
    gw_sb = kb.sb("gw_sb", [128, 2, NU * 256], BF16)
    taps_sb = kb.sb("taps_sb", [128, NU * 10])
    cb_sb = kb.sb("cb_sb", [128, NU * 2])
    gb_sb = kb.sb("gb_sb", [128, NU * 2])
    lam_sb = kb.sb("lam_sb", [128, NU])
    cl_sb = kb.sb("cl_sb", [128, NU])
    one_t = kb.sb("one_t", [128, 1])
    xb = [kb.sb("xb%d" % i, [128, 8, TWL + 4], BF16) for i in range(2)]
    uc = kb.sb("uc", [128, 2, TWL])
    ucb = kb.sb("ucb", [128, 2, TWL], BF16)
    rg = kb.sb("rg", [128, TWL])
    ig = kb.sb("ig", [128, TWL])
    aa = kb.sb("aa", [128, TWL])
    a2 = kb.sb("a2", [128, TWL])
    bx = kb.sb("bx", [128, TWL])
    ho = [kb.sb("ho%d" % i, [128, TWL]) for i in range(2)]
    carry = kb.sb("carry", [128, NU])

    kb.op("pool", lambda: nc.gpsimd.memset(one_t[:], 1.0), writes=["one"])
    kb.op("pool", lambda: nc.gpsimd.memset(carry[:], 0.0), writes=["carry"])
    kb.dma(taps_sb[:], taps, writes=["par"])
    kb.dma(cb_sb[:], cbias, writes=["par"])
    kb.dma(gb_sb[:], gb, writes=["par"])
    kb.dma(lam_sb[:], lam, writes=["lam"])
    kb.op("act", lambda: nc.scalar.activation(out=cl_sb[:], in_=lam_sb[:], func=AF.Exp, scale=-1.0), reads=["lam"], writes=["cl"])
    kb.op("act", lambda: nc.scalar.activation(out=cl_sb[:], in_=cl_sb[:], func=AF.Ln, bias=one_t[:, 0:1], scale=1.0), reads=["cl", "one"], writes=["cl"])
    kb.op("dve", lambda: nc.vector.tensor_scalar(out=cl_sb[:], in0=cl_sb[:], scalar1=-8.0, scalar2=None, op0=ALU.mult), reads=["cl"], writes=["cl"])
    xv = xT.rearrange("(c p) t -> p c t", p=128)

    def load_x(i, t0, n):
        for c in range(8):
            kb.dma(xb[i][:, c, :n + 4], xv[:, c, t0:t0 + n + 4], writes=["xb%d" % i], e="pool")

    load_x(0, tiles[0][0], tiles[0][1])
    kb.load_w_bf16(w_u_sb, "w", w_u, D, NU * 256, nblk=1280)
    kb.load_w_bf16(gw_sb, "w", gw, 256, NU * 256, nblk=1280)
    cnt = 0
    for i, (t0, n) in enumerate(tiles):
        x, xn = xb[i % 2], "xb%d" % (i % 2)
        W = n + 4
        if i + 1 < len(tiles):
            load_x((i + 1) % 2, tiles[i + 1][0], tiles[i + 1][1])
        for j in range(NU):
            for c in range(2):
                pu, pun = kb.ps[c], "ps%d" % c
                for kc in range(8):
                    kb.op("pe", lambda: nc.tensor.matmul(pu[:, :W], lhsT=w_u_sb[:, kc, j * 256 + c * 128:j * 256 + (c + 1) * 128], rhs=x[:, kc, :W],
                                                         start=(kc == 0), stop=(kc == 7)), reads=["w", xn], writes=[pun])
                tb = j * 10 + c * 5
                kb.op("dve", lambda: nc.vector.tensor_scalar(out=uc[:, c, :n], in0=pu[:, 0:n], scalar1=taps_sb[:, tb:tb + 1], scalar2=cb_sb[:, j * 2 + c:j * 2 + c + 1],
                                                             op0=ALU.mult, op1=ALU.add), reads=[pun, "par"], writes=["uc"])
                for k in range(1, 5):
                    kb.op("dve", lambda: nc.vector.scalar_tensor_tensor(out=uc[:, c, :n], in0=pu[:, k:k + n], scalar=taps_sb[:, tb + k:tb + k + 1], in1=uc[:, c, :n],
                                                                        op0=ALU.mult, op1=ALU.add), reads=[pun, "par", "uc"], writes=["uc"])
            kb.op("pool", lambda: nc.gpsimd.tensor_copy(out=ucb[:, :, :n], in_=uc[:, :, :n]), reads=["uc"], writes=["ucb"])
            pr, pi = kb.ps[2], kb.ps[3]
            for kc in range(2):
                kb.op("pe", lambda: nc.tensor.matmul(pr[:, :n], lhsT=gw_sb[:, kc, j * 256:j * 256 + 128], rhs=ucb[:, kc, :n], start=(kc == 0), stop=(kc == 1)),
                      reads=["w", "ucb"], writes=["ps2"])
            for kc in range(2):
                kb.op("pe", lambda: nc.tensor.matmul(pi[:, :n], lhsT=gw_sb[:, kc, j * 256 + 128:j * 256 + 256], rhs=ucb[:, kc, :n], start=(kc == 0), stop=(kc == 1)),
                      reads=["w", "ucb"], writes=["ps3"])
            kb.op("act", lambda: nc.scalar.activation(out=rg[:, :n], in_=pr[:, :n], func=AF.Sigmoid, bias=gb_sb[:, 2 * j:2 * j + 1], scale=1.0), reads=["ps2", "par"], writes=["rg"])
            kb.op("act", lambda: nc.scalar.activation(out=ig[:, :n], in_=pi[:, :n], func=AF.Sigmoid, bias=gb_sb[:, 2 * j + 1:2 * j + 2], scale=1.0), reads=["ps3", "par"], writes=["ig"])
            kb.op("act", lambda: nc.scalar.activation(out=aa[:, :n], in_=rg[:, :n], func=AF.Exp, scale=cl_sb[:, j:j + 1]), reads=["rg", "cl"], writes=["aa"])
            kb.op("pool", lambda: nc.gpsimd.tensor_tensor(out=a2[:, :n], in0=aa[:, :n], in1=aa[:, :n], op=ALU.mult), reads=["aa"], writes=["a2"])
            kb.op("act", lambda: nc.scalar.activation(out=a2[:, :n], in_=a2[:, :n], func=AF.Sqrt, bias=one_t[:, 0:1], scale=-1.0), reads=["a2", "one"], writes=["a2"])
            kb.op("pool", lambda: nc.gpsimd.tensor_tensor(out=bx[:, :n], in0=ig[:, :n], in1=uc[:, 0, :n], op=ALU.mult), reads=["ig", "uc"], writes=["bx"])
            kb.op("dve", lambda: nc.vector.tensor_tensor(out=bx[:, :n], in0=bx[:, :n], in1=a2[:, :n], op=ALU.mult), reads=["bx", "a2"], writes=["bx"])
            h_, hn = ho[cnt % 2], "ho%d" % (cnt % 2)
            cnt += 1
            kb.op("dve", lambda: nc.vector.tensor_tensor_scan(out=h_[:, :n], data0=aa[:, :n], data1=bx[:, :n], initial=carry[:, j:j + 1],
                                                              op0=ALU.mult, op1=ALU.add), reads=["aa", "bx", "carry"], writes=[hn])
            kb.op("dve", lambda: nc.vector.tensor_copy(out=carry[:, j:j + 1], in_=h_[:, n - 1:n]), reads=[hn], writes=["carry"])
            kb.dma(hs[j * 128:(j + 1) * 128, t0:t0 + n], h_[:, :n], reads=[hn])
    return kb.finish()


def lru_core_inputs(core, x_b, p):
    q = core % 4
    d = q // 2
    slices = list(range(5)) if q % 2 == 0 else list(range(5, 10))
    T = x_b.shape[0]
    xx = x_b[::-1] if d == 1 else x_b
    xT = np.zeros((D, T + 4), np.float32)
    xT[:, 2:T + 2] = xx.T
    w_in, conv_w, conv_b = p["lru_w_in"], p["lru_conv_w"], p["lru_conv_b"]
    gate_w, gate_b, lam = p["lru_gate_w"], p["lru_gate_b"], p["lru_lambda"]
    w_u = np.zeros((D, NU * 256), np.float32)
    taps = np.zeros((128, NU, 2, 5), np.float32)
    cbias = np.zeros((128, NU, 2), np.float32)
    gw = np.zeros((256, NU, 2, 128), np.float32)
    gbv = np.zeros((128, NU, 2), np.float32)
    lamv = np.zeros((128, NU), np.float32)
    for j, s in enumerate(slices):
        n, half = s // 2, s % 2
        own = np.arange(n * 256 + half * 128, n * 256 + half * 128 + 128)
        oth = np.arange(n * 256 + (1 - half) * 128, n * 256 + (1 - half) * 128 + 128)
        ch = np.concatenate([own, oth])
        w_u[:, j * 256:(j + 1) * 256] = w_in[:, LRU_W + ch]
        cwj = conv_w[:, ch]
        t5 = np.zeros((5, 256), np.float32)
        if d == 0:
            t5[0:4] = cwj
        else:
            t5[1:5] = cwj[::-1]
        taps[:, j, :, :] = t5.T.reshape(2, 128, 5).transpose(1, 0, 2)
        cbias[:, j, :] = conv_b[ch].reshape(2, 128).T
        loc = ch - n * 256
        for g in range(2):
            gw[:, j, g, :] = gate_w[d, g, n][loc][:, half * 128:(half + 1) * 128]
            gbv[:, j, g] = gate_b[d, g, own]
        lamv[:, j] = lam[d, own]
    return {"xT": xT, "w_u": w_u, "taps": np.ascontiguousarray(taps.reshape(128, -1)), "cbias": np.ascontiguousarray(cbias.reshape(128, -1)),
            "gw": np.ascontiguousarray(gw.reshape(256, -1)), "gb": np.ascontiguousarray(gbv.reshape(128, -1)), "lam": lamv}, d, slices


NH = 8
CH = 64
RW_TW = 256
EXPM05 = float(np.exp(-0.5))


def build_rwkv(TP, NLANE=4):
    kb = KB()
    nc = kb.nc
    tiles = [(t0, min(RW_TW, TP - t0)) for t0 in range(0, TP, RW_TW)]
    xT = kb.din("xT", [D, TP + 2])
    mu = kb.din("mu", [128, 48])
    w_rkv = kb.din("w_rkv", [D, 3 * 512])
    w1 = kb.din("w1", [D, 64])
    a1 = kb.din("a1", [D, 64])
    g1 = kb.din("g1", [D, 160])
    w2 = kb.din("w2", [64, 512])
    a2 = kb.din("a2", [64, 512])
    g2 = kb.din("g2", [160, 512])
    hp = kb.din("hp", [64, 5 * NH])
    cmask = kb.din("cmask", [64, RW_TW])
    mask4 = kb.din("mask4", [64, 512])
    maskl = kb.din("maskl", [64, 128])
    ident2 = kb.din("ident2", [64, 128])
    yT = kb.dout("yT", [NH * 64, TP])
    bonT = kb.dout("bonT", [NH * 64, TP])
    gT = kb.dout("gT", [NH * 64, TP])

    w_rkv_sb = kb.sb("w_rkv_sb", [128, 8, 1536], BF16)
    w1_sb = kb.sb("w1_sb", [128, 8, 64], BF16)
    a1_sb = kb.sb("a1_sb", [128, 8, 64], BF16)
    g1_sb = kb.sb("g1_sb", [128, 8, 160], BF16)
    w2_sb = kb.sb("w2_sb", [64, 512], BF16)
    a2_sb = kb.sb("a2_sb", [64, 512], BF16)
    g2a_sb = kb.sb("g2a_sb", [128, 512], BF16)
    g2b_sb = kb.sb("g2b_sb", [32, 512], BF16)
    mu_sb = kb.sb("mu_sb", [128, 48])
    hp_sb = kb.sb("hp_sb", [64, 5 * NH])
    omka = kb.sb("omka", [64, NH])
    cmask_sb = kb.sb("cmask_sb", [64, RW_TW])
    mask4_sb = kb.sb("mask4_sb", [64, 512])
    maskl_sb = kb.sb("maskl_sb", [64, 128])
    id2_sb = kb.sb("id2_sb", [64, 128])
    ones64 = kb.sb("ones64", [64, 64])
    n = RW_TW
    NCH = n // CH
    xf = [kb.sb("xf0", [128, 8, n + 2])] * 2
    xx = kb.sb("xx", [128, 8, n])
    xm = [kb.sb("xm%d" % i, [128, 8, n], BF16) for i in range(6)]
    hidw = kb.sb("hidw", [64, n], BF16)
    hida = kb.sb("hida", [64, n], BF16)
    hidg = kb.sb("hidg", [128, n], BF16)
    hidg2 = kb.sb("hidg2", [32, n], BF16)
    Rt = kb.sb("Rt", [64, n])
    Kx = kb.sb("Kx", [64, n])
    LW = kb.sb("LW", [64, n])
    AS = kb.sb("AS", [64, n])
    KK = kb.sb("KK", [64, n])
    SQ = kb.sb("SQ", [64, n])
    NR = kb.sb("NR", [64, n])
    TT = kb.sb("TT", [64, n])
    KD = kb.sb("KD", [64, n])
    BV = kb.sb("BV", [64, n])
    CL = kb.sb("CL", [64, n])
    EG = kb.sb("EG", [64, n])
    EGi = kb.sb("EGi", [64, n])
    EX = kb.sb("EX", [64, n])
    AR = kb.sb("AR", [64, NH, NCH, 128])
    Bf = kb.sb("Bf", [64, NH, n])
    Kf = kb.sb("Kf", [64, NH, n])
    Vf = kb.sb("Vf", [64, NH, n])
    gC = kb.sb("gC", [64, NH, NCH])
    BON = [kb.sb("BON%d" % i, [64, n]) for i in range(2)]
    GO = [kb.sb("GO%d" % i, [64, n]) for i in range(2)]
    YO = kb.sb("YO", [64, NH, n])
    S0 = kb.sb("S0", [64, NH, 64])
    gram = [kb.sb("gram%d" % l, [64, 512]) for l in range(NLANE)]
    Lsb = [kb.sb("Lsb%d" % l, [64, 128]) for l in range(NLANE)]
    NL = [[kb.sb("NL%d_%d" % (l, i), [64, 256]) for i in range(2)] for l in range(NLANE)]
    Gm = [kb.sb("Gm%d" % l, [64, 128]) for l in range(NLANE)]
    tok = [kb.sb("tok%d" % l, [64, 384]) for l in range(NLANE)]
    Xs = [kb.sb("Xs%d" % l, [64, 128]) for l in range(NLANE)]
    Us = [kb.sb("Us%d" % l, [64, 128]) for l in range(NLANE)]
    tmpS = [kb.sb("tmpS%d" % l, [64, 128]) for l in range(NLANE)]

    kb.op("pool", lambda: nc.gpsimd.memset(ones64[:], 1.0), writes=["ones64"])
    kb.op("pool", lambda: nc.gpsimd.memset(S0[:], 0.0), writes=["S0_%d" % hh for hh in range(NH)])
    kb.dma(mu_sb[:], mu, writes=["par"])
    kb.dma(hp_sb[:], hp, writes=["par"])
    kb.dma(cmask_sb[:], cmask, writes=["par"])
    kb.dma(mask4_sb[:], mask4, writes=["par"])
    kb.dma(maskl_sb[:], maskl, writes=["par"])
    kb.dma(id2_sb[:], ident2, writes=["par"])
    kb.op("dve", lambda: nc.vector.tensor_scalar(out=omka[:], in0=hp_sb[:, 3 * NH:4 * NH], scalar1=-1.0, scalar2=1.0, op0=ALU.mult, op1=ALU.add),
          reads=["par"], writes=["omka"])
    xv = xT.rearrange("(c p) t -> p c t", p=128)
    kb.dma(xf[0][:, :, :tiles[0][1] + 2], xv[:, :, 0:tiles[0][1] + 2], writes=["xf0"])
    kb.load_w_bf16(w_rkv_sb, "w", w_rkv, D, 1536, nblk=1536)
    kb.load_w_bf16(w1_sb, "w", w1, D, 64)
    kb.load_w_bf16(a1_sb, "w", a1, D, 64)
    kb.load_w_bf16(g1_sb, "w", g1, D, 160)
    kb.dma(w2_sb[:], w2, writes=["w"], e="pool")
    kb.dma(a2_sb[:], a2, writes=["w"], e="pool")
    kb.dma(g2a_sb[:], g2[0:128, :], writes=["w"], e="pool")
    kb.dma(g2b_sb[:], g2[128:160, :], writes=["w"], e="pool")
    yv = yT.rearrange("(h k) t -> k h t", k=64)
    bv = bonT.rearrange("(h k) t -> k h t", k=64)
    gv = gT.rearrange("(h k) t -> k h t", k=64)
    P = kb.ps

    def hcol(q, hh):
        return hp_sb[:, q * NH + hh:q * NH + hh + 1]

    for ti, (t0, n) in enumerate(tiles):
        nch = n // CH
        x, xn = xf[0], "xf0"
        kb.op("pool", lambda: nc.gpsimd.tensor_tensor(out=xx[:, :, :n], in0=x[:, :, 0:n], in1=x[:, :, 2:n + 2], op=ALU.add), reads=[xn], writes=["xx"])
        kb.op("dve", lambda: nc.vector.scalar_tensor_tensor(out=xx[:, :, :n], in0=xx[:, :, :n], scalar=0.5, in1=x[:, :, 1:n + 1], op0=ALU.mult, op1=ALU.subtract),
              reads=["xx", xn], writes=["xx"])
        for i in range(6):
            for c in range(8):
                kb.op("dve", lambda: nc.vector.scalar_tensor_tensor(out=xm[i][:, c, :n], in0=xx[:, c, :n], scalar=mu_sb[:, i * 8 + c:i * 8 + c + 1], in1=x[:, c, 1:n + 1],
                                                                    op0=ALU.mult, op1=ALU.add), reads=["xx", xn, "par"], writes=["xm%d" % i])
        if ti + 1 < len(tiles):
            t1, n1 = tiles[ti + 1]
            kb.dma(xf[0][:, :, :n1 + 2], xv[:, :, t1:t1 + n1 + 2], writes=["xf0"])
        for kc in range(8):
            kb.op("pe", lambda: nc.tensor.matmul(P[6][0:64, :n], lhsT=w1_sb[:, kc, :], rhs=xm[1][:, kc, :n], start=(kc == 0), stop=(kc == 7)), reads=["w", "xm1"], writes=["ps6"])
        kb.op("act", lambda: nc.scalar.activation(out=hidw[:, :n], in_=P[6][0:64, :n], func=AF.Tanh), reads=["ps6"], writes=["hidw"])
        for kc in range(8):
            kb.op("pe", lambda: nc.tensor.matmul(P[6][0:64, :n], lhsT=a1_sb[:, kc, :], rhs=xm[4][:, kc, :n], start=(kc == 0), stop=(kc == 7)), reads=["w", "xm4"], writes=["ps6"])
        kb.op("act", lambda: nc.scalar.copy(out=hida[:, :n], in_=P[6][0:64, :n]), reads=["ps6"], writes=["hida"])
        for kc in range(8):
            kb.op("pe", lambda: nc.tensor.matmul(P[6][:, :n], lhsT=g1_sb[:, kc, 0:128], rhs=xm[5][:, kc, :n], start=(kc == 0), stop=(kc == 7)), reads=["w", "xm5"], writes=["ps6"])
        kb.op("act", lambda: nc.scalar.activation(out=hidg[:, :n], in_=P[6][:, :n], func=AF.Sigmoid), reads=["ps6"], writes=["hidg"])
        for kc in range(8):
            kb.op("pe", lambda: nc.tensor.matmul(P[6][0:32, :n], lhsT=g1_sb[:, kc, 128:160], rhs=xm[5][:, kc, :n], start=(kc == 0), stop=(kc == 7)), reads=["w", "xm5"], writes=["ps6"])
        kb.op("act", lambda: nc.scalar.activation(out=hidg2[:, :n], in_=P[6][0:32, :n], func=AF.Sigmoid), reads=["ps6"], writes=["hidg2"])
        for hh in range(NH):
            cs = slice(hh * 64, (hh + 1) * 64)
            for q, (pp, ppn, xi) in enumerate(((P[0], "ps0", 0), (P[1], "ps1", 2), (P[2], "ps2", 3))):
                for kc in range(8):
                    kb.op("pe", lambda: nc.tensor.matmul(pp[0:64, :n], lhsT=w_rkv_sb[:, kc, q * 512 + hh * 64:q * 512 + (hh + 1) * 64], rhs=xm[xi][:, kc, :n],
                                                         start=(kc == 0), stop=(kc == 7)), reads=["w", "xm%d" % xi], writes=[ppn])
            kb.op("pe", lambda: nc.tensor.matmul(P[3][0:64, :n], lhsT=w2_sb[:, cs], rhs=hidw[:, :n], start=True, stop=True), reads=["w", "hidw"], writes=["ps3"])
            kb.op("pe", lambda: nc.tensor.matmul(P[4][0:64, :n], lhsT=a2_sb[:, cs], rhs=hida[:, :n], start=True, stop=True), reads=["w", "hida"], writes=["ps4"])
            kb.op("pe", lambda: nc.tensor.matmul(P[7][0:64, :n], lhsT=g2a_sb[:, cs], rhs=hidg[:, :n], start=True, stop=False), reads=["w", "hidg"], writes=["ps7"])
            kb.op("pe", lambda: nc.tensor.matmul(P[7][0:64, :n], lhsT=g2b_sb[:, cs], rhs=hidg2[:, :n], start=False, stop=True), reads=["w", "hidg2"], writes=["ps7"])
            kb.op("act", lambda: nc.scalar.copy(out=Rt[:, :n], in_=P[0][0:64, :n]), reads=["ps0"], writes=["Rt"])
            kb.op("act", lambda: nc.scalar.copy(out=Kx[:, :n], in_=P[1][0:64, :n]), reads=["ps1"], writes=["Kx"])
            kb.op("act", lambda: nc.scalar.copy(out=Vf[:, hh, :n], in_=P[2][0:64, :n]), reads=["ps2"], writes=["Vf"])
            kb.op("act", lambda: nc.scalar.copy(out=GO[hh % 2][:, :n], in_=P[7][0:64, :n]), reads=["ps7"], writes=["GO%d" % (hh % 2)])
            kb.dma(gv[:, hh, t0:t0 + n], GO[hh % 2][:, :n], reads=["GO%d" % (hh % 2)])
            kb.op("act", lambda: nc.scalar.activation(out=LW[:, :n], in_=P[3][0:64, :n], func=AF.Sigmoid, bias=hcol(0, hh), scale=1.0), reads=["ps3", "par"], writes=["LW"])
            kb.op("pool", lambda: nc.gpsimd.tensor_scalar(out=LW[:, :n], in0=LW[:, :n], scalar1=-EXPM05, scalar2=None, op0=ALU.mult), reads=["LW"], writes=["LW"])
            kb.op("act", lambda: nc.scalar.activation(out=AS[:, :n], in_=P[4][0:64, :n], func=AF.Sigmoid, bias=hcol(1, hh), scale=1.0), reads=["ps4", "par"], writes=["AS"])
            kb.op("dve", lambda: nc.vector.tensor_scalar(out=KK[:, :n], in0=Kx[:, :n], scalar1=hcol(2, hh), scalar2=None, op0=ALU.mult), reads=["Kx", "par"], writes=["KK"])
            kb.op("act", lambda: nc.scalar.activation(out=SQ[:, :n], in_=KK[:, :n], func=AF.Square), reads=["KK"], writes=["SQ"])
            kb.op("pe", lambda: nc.tensor.matmul(P[5][0:64, :n], lhsT=ones64[:], rhs=SQ[:, :n], start=True, stop=True), reads=["ones64", "SQ"], writes=["ps5"])
            kb.op("dve", lambda: nc.vector.tensor_scalar(out=NR[:, :n], in0=P[5][0:64, :n], scalar1=1e-24, scalar2=None, op0=ALU.max), reads=["ps5"], writes=["NR"])
            kb.op("act", lambda: nc.scalar.activation(out=NR[:, :n], in_=NR[:, :n], func=AF.Sqrt), reads=["NR"], writes=["NR"])
            kb.op("dve", lambda: nc.vector.reciprocal(out=NR[:, :n], in_=NR[:, :n]), reads=["NR"], writes=["NR"])
            kb.op("pool", lambda: nc.gpsimd.tensor_tensor(out=KK[:, :n], in0=KK[:, :n], in1=NR[:, :n], op=ALU.mult), reads=["KK", "NR"], writes=["KK"])
            kb.op("dve", lambda: nc.vector.tensor_scalar(out=TT[:, :n], in0=AS[:, :n], scalar1=hcol(3, hh), scalar2=omka[:, hh:hh + 1], op0=ALU.mult, op1=ALU.add),
                  reads=["AS", "par", "omka"], writes=["TT"])
            kb.op("pool", lambda: nc.gpsimd.tensor_tensor(out=KD[:, :n], in0=Kx[:, :n], in1=TT[:, :n], op=ALU.mult), reads=["Kx", "TT"], writes=["KD"])
            kb.op("pool", lambda: nc.gpsimd.tensor_tensor(out=BV[:, :n], in0=KK[:, :n], in1=AS[:, :n], op=ALU.mult), reads=["KK", "AS"], writes=["BV"])
            kb.op("dve", lambda: nc.vector.scalar_tensor_tensor(out=SQ[:, :n], in0=Rt[:, :n], scalar=hcol(4, hh), in1=KD[:, :n], op0=ALU.mult, op1=ALU.mult),
                  reads=["Rt", "par", "KD", "SQ"], writes=["SQ"])
            kb.op("pe", lambda: nc.tensor.matmul(P[5][0:64, :n], lhsT=ones64[:], rhs=SQ[:, :n], start=True, stop=True), reads=["ones64", "SQ"], writes=["ps5"])
            kb.op("dve", lambda: nc.vector.tensor_tensor(out=BON[hh % 2][:, :n], in0=P[5][0:64, :n], in1=Vf[:, hh, :n], op=ALU.mult), reads=["ps5", "Vf"], writes=["BON%d" % (hh % 2)])
            kb.dma(bv[:, hh, t0:t0 + n], BON[hh % 2][:, :n], reads=["BON%d" % (hh % 2)])
            kb.op("dve", lambda: nc.vector.tensor_tensor_scan(out=CL[:, :n], data0=cmask_sb[:, :n], data1=LW[:, :n], initial=0.0, op0=ALU.mult, op1=ALU.add),
                  reads=["par", "LW"], writes=["CL"])
            kb.op("act", lambda: nc.scalar.activation(out=EG[:, :n], in_=CL[:, :n], func=AF.Exp), reads=["CL"], writes=["EG"])
            kb.op("act", lambda: nc.scalar.activation(out=EGi[:, :n], in_=CL[:, :n], func=AF.Exp, scale=-1.0), reads=["CL"], writes=["EGi"])
            kb.op("pool", lambda: nc.gpsimd.tensor_tensor(out=EX[:, :n], in0=CL[:, :n], in1=LW[:, :n], op=ALU.subtract), reads=["CL", "LW"], writes=["EX"])
            kb.op("act", lambda: nc.scalar.activation(out=EX[:, :n], in_=EX[:, :n], func=AF.Exp), reads=["EX"], writes=["EX"])
            kb.op("dve", lambda: nc.vector.scalar_tensor_tensor(out=AR[:, hh, :nch, 0:64], in0=KK[:, :n].rearrange("p (c t) -> p c t", t=CH), scalar=-1.0,
                                                                in1=EX[:, :n].rearrange("p (c t) -> p c t", t=CH), op0=ALU.mult, op1=ALU.mult),
                  reads=["KK", "EX"], writes=["AR"])
            kb.op("dve", lambda: nc.vector.tensor_tensor(out=AR[:, hh, :nch, 64:128], in0=Rt[:, :n].rearrange("p (c t) -> p c t", t=CH),
                                                         in1=EG[:, :n].rearrange("p (c t) -> p c t", t=CH), op=ALU.mult), reads=["Rt", "EG"], writes=["AR"])
            kb.op("pool", lambda: nc.gpsimd.tensor_tensor(out=Bf[:, hh, :n], in0=BV[:, :n], in1=EGi[:, :n], op=ALU.mult), reads=["BV", "EGi"], writes=["Bf"])
            kb.op("pool", lambda: nc.gpsimd.tensor_tensor(out=Kf[:, hh, :n], in0=KD[:, :n], in1=EGi[:, :n], op=ALU.mult), reads=["KD", "EGi"], writes=["Kf"])
            kb.op("dve", lambda: nc.vector.tensor_copy(out=gC[:, hh, :nch], in_=EG[:, :n].rearrange("p (c t) -> p c t", t=CH)[:, :, CH - 1]), reads=["EG"], writes=["gC"])
        kb.barrier()
        for c in range(nch):
            tsl = slice(c * CH, (c + 1) * CH)
            for lg in range(0, NH // 2, NLANE):
                lanes = list(range(NLANE))
                pairs = [(2 * (lg + l), 2 * (lg + l) + 1) for l in lanes]
                BA = [P[2 * l] for l in lanes]
                BB = [P[2 * l + 1] for l in lanes]
                nA = ["gA%d" % l for l in lanes]
                nBa = ["gB%da" % l for l in lanes]
                nBb = ["gB%db" % l for l in lanes]
                nBc = ["gB%dc" % l for l in lanes]
                for l in lanes:
                    for e, hh in enumerate(pairs[l]):
                        kb.op("pe", lambda: nc.tensor.matmul(BA[l][0:64, e * 256:e * 256 + 128], lhsT=Bf[:, hh, tsl], rhs=AR[:, hh, c, :], start=True, stop=True), reads=["Bf", "AR"], writes=[nA[l]])
                        kb.op("pe", lambda: nc.tensor.matmul(BA[l][0:64, e * 256 + 128:e * 256 + 256], lhsT=Kf[:, hh, tsl], rhs=AR[:, hh, c, :], start=True, stop=True), reads=["Kf", "AR"], writes=[nA[l]])
                        kb.op("pe", lambda: nc.tensor.matmul(BB[l][0:64, e * 64:(e + 1) * 64], lhsT=AR[:, hh, c, 0:64], rhs=Bf[:, hh, tsl], start=True, stop=True), reads=["Bf", "AR"], writes=[nBa[l]])
                for l in lanes:
                    gr, grn = gram[l], "gram%d" % l
                    kb.op("dve", lambda: nc.vector.tensor_tensor(out=gr[:, :], in0=BA[l][0:64, :], in1=mask4_sb[:, :], op=ALU.mult), reads=[nA[l], "par"], writes=[grn])
                    kb.op("dve", lambda: nc.vector.tensor_tensor(out=Lsb[l][:, :], in0=BB[l][0:64, 0:128], in1=maskl_sb[:, :], op=ALU.mult), reads=[nBa[l], "par"], writes=["Lsb%d" % l])
                    g3 = gr[:, :].rearrange("p (e x) -> p e x", e=2)
                    kb.op("pool", lambda: nc.gpsimd.tensor_tensor(out=Gm[l][:, :].rearrange("p (e x) -> p e x", e=2), in0=id2_sb[:, :].rearrange("p (e x) -> p e x", e=2),
                                                                  in1=g3[:, :, 0:64], op=ALU.add), reads=[grn, "par"], writes=["Gm%d" % l])
                for l in lanes:
                    for q, (src, sn) in enumerate(((Bf, "Bf"), (Kf, "Kf"), (Vf, "Vf"))):
                        for e, hh in enumerate(pairs[l]):
                            kb.op("pe", lambda: nc.tensor.matmul(BA[l][0:64, q * 128 + e * 64:q * 128 + (e + 1) * 64], lhsT=src[:, hh, tsl], rhs=id2_sb[:, 0:64], start=True, stop=True),
                                  reads=[sn, "par"], writes=[nA[l]])
                for l in lanes:
                    kb.op("act", lambda: nc.scalar.copy(out=tok[l][:, :], in_=BA[l][0:64, 0:384]), reads=[nA[l]], writes=["tok%d" % l])
                Ncur = [[gram[l][:, 0:64], gram[l][:, 256:320]] for l in lanes]
                Lcur = [[Lsb[l][:, 0:64], Lsb[l][:, 64:128]] for l in lanes]
                ncn = ["gram%d" % l for l in lanes]
                lcn = ["Lsb%d" % l for l in lanes]
                for lvl in range(5):
                    last = lvl == 4
                    for l in lanes:
                        for e in range(2):
                            if not last:
                                kb.op("pe", lambda: nc.tensor.matmul(BB[l][0:64, 128 + e * 64:128 + (e + 1) * 64], lhsT=Lcur[l][e], rhs=Ncur[l][e], start=True, stop=True), reads=[ncn[l], lcn[l]], writes=[nBb[l]])
                            kb.op("pe", lambda: nc.tensor.matmul(BB[l][0:64, 256 + e * 64:256 + (e + 1) * 64], lhsT=Ncur[l][e], rhs=Lcur[l][e], start=True, stop=True), reads=[ncn[l], lcn[l]], writes=[nBb[l]])
                    for l in lanes:
                        nl, nln = NL[l][lvl % 2], "NL%d_%d" % (l, lvl % 2)
                        if not last:
                            kb.op("act", lambda: nc.scalar.copy(out=nl[:, :], in_=BB[l][0:64, 128:384]), reads=[nBb[l]], writes=[nln])
                        else:
                            kb.op("act", lambda: nc.scalar.copy(out=nl[:, 128:256], in_=BB[l][0:64, 256:384]), reads=[nBb[l]], writes=[nln])
                        Ncur[l] = [nl[:, 0:64], nl[:, 64:128]]
                        Lcur[l] = [nl[:, 128:192], nl[:, 192:256]]
                        ncn[l] = lcn[l] = nln
                    for l in lanes:
                        for e in range(2):
                            kb.op("pe", lambda: nc.tensor.matmul(BB[l][0:64, 384 + e * 64:384 + (e + 1) * 64], lhsT=Lcur[l][e], rhs=Gm[l][:, e * 64:(e + 1) * 64], start=True, stop=True), reads=[lcn[l], "Gm%d" % l], writes=[nBc[l]])
                    for l in lanes:
                        kb.op("dve", lambda: nc.vector.tensor_tensor(out=Gm[l][:, :], in0=Gm[l][:, :], in1=BB[l][0:64, 384:512], op=ALU.add), reads=["Gm%d" % l, nBc[l]], writes=["Gm%d" % l])
                for l in lanes:
                    for e, hh in enumerate(pairs[l]):
                        kb.op("pe", lambda: nc.tensor.matmul(BB[l][0:64, e * 64:(e + 1) * 64], lhsT=AR[:, hh, c, 0:64], rhs=S0[:, hh, :], start=True, stop=False), reads=["AR", "S0_%d" % hh], writes=[nBa[l]])
                        kb.op("pe", lambda: nc.tensor.matmul(BB[l][0:64, e * 64:(e + 1) * 64], lhsT=gram[l][:, e * 256 + 128:e * 256 + 192], rhs=tok[l][:, 256 + e * 64:256 + (e + 1) * 64], start=False, stop=True),
                              reads=["gram%d" % l, "tok%d" % l], writes=[nBa[l]])
                for l in lanes:
                    kb.op("dve", lambda: nc.vector.tensor_copy(out=Xs[l][:, :], in_=BB[l][0:64, 0:128]), reads=[nBa[l]], writes=["Xs%d" % l])
                for l in lanes:
                    for e in range(2):
                        kb.op("pe", lambda: nc.tensor.matmul(BB[l][0:64, 128 + e * 64:128 + (e + 1) * 64], lhsT=Gm[l][:, e * 64:(e + 1) * 64], rhs=Xs[l][:, e * 64:(e + 1) * 64], start=True, stop=True), reads=["Gm%d" % l, "Xs%d" % l], writes=[nBb[l]])
                for l in lanes:
                    kb.op("act", lambda: nc.scalar.copy(out=Us[l][:, :], in_=BB[l][0:64, 128:256]), reads=[nBb[l]], writes=["Us%d" % l])
                for l in lanes:
                    for e, hh in enumerate(pairs[l]):
                        es = slice(384 + e * 64, 384 + (e + 1) * 64)
                        ue = slice(e * 64, (e + 1) * 64)
                        kb.op("pe", lambda: nc.tensor.matmul(BB[l][0:64, es], lhsT=S0[:, hh, :], rhs=AR[:, hh, c, 64:128], start=True, stop=False), reads=["S0_%d" % hh, "AR"], writes=[nBc[l]])
                        kb.op("pe", lambda: nc.tensor.matmul(BB[l][0:64, es], lhsT=Us[l][:, ue], rhs=gram[l][:, e * 256 + 64:e * 256 + 128], start=False, stop=False), reads=["Us%d" % l, "gram%d" % l], writes=[nBc[l]])
                        kb.op("pe", lambda: nc.tensor.matmul(BB[l][0:64, es], lhsT=tok[l][:, 256 + e * 64:256 + (e + 1) * 64], rhs=gram[l][:, e * 256 + 192:e * 256 + 256], start=False, stop=True),
                              reads=["tok%d" % l, "gram%d" % l], writes=[nBc[l]])
                    for e, hh in enumerate(pairs[l]):
                        es = slice(384 + e * 64, 384 + (e + 1) * 64)
                        ue = slice(e * 64, (e + 1) * 64)
                        kb.op("pe", lambda: nc.tensor.matmul(BA[l][0:64, es], lhsT=tok[l][:, e * 64:(e + 1) * 64], rhs=Us[l][:, ue], start=True, stop=False), reads=["tok%d" % l, "Us%d" % l], writes=[nA[l]])
                        kb.op("pe", lambda: nc.tensor.matmul(BA[l][0:64, es], lhsT=tok[l][:, 128 + e * 64:128 + (e + 1) * 64], rhs=tok[l][:, 256 + e * 64:256 + (e + 1) * 64], start=False, stop=True),
                              reads=["tok%d" % l], writes=[nA[l]])
                for l in lanes:
                    h0 = pairs[l][0]
                    kb.op("act", lambda: nc.scalar.copy(out=YO[:, h0:h0 + 2, tsl], in_=BB[l][0:64, 384:512].rearrange("p (e t) -> p e t", e=2)), reads=[nBc[l]], writes=["YO"])
                    for e, hh in enumerate(pairs[l]):
                        es = slice(384 + e * 64, 384 + (e + 1) * 64)
                        ue = slice(e * 64, (e + 1) * 64)
                        kb.op("dve", lambda: nc.vector.tensor_tensor(out=tmpS[l][:, ue], in0=BA[l][0:64, es], in1=S0[:, hh, :], op=ALU.add), reads=[nA[l], "S0_%d" % hh], writes=["tmpS%d" % l])
                        kb.op("dve", lambda: nc.vector.tensor_scalar(out=S0[:, hh, :], in0=tmpS[l][:, ue], scalar1=gC[:, hh, c:c + 1], scalar2=None, op0=ALU.mult),
                              reads=["tmpS%d" % l, "gC"], writes=["S0_%d" % hh])
        kb.barrier()
        kb.dma(yv[:, :, t0:t0 + n], YO[:, :, :n], reads=["YO"])
    return kb.finish()


def rwkv_consts():
    i = np.arange(64)
    su = (i[:, None] < i[None, :]).astype(np.float32)
    uu = (i[:, None] <= i[None, :]).astype(np.float32)
    blk = np.concatenate([su, uu, su, uu], axis=1)
    cm = np.ones((64, RW_TW), np.float32)
    cm[:, ::CH] = 0.0
    return {"cmask": cm, "mask4": np.ascontiguousarray(np.concatenate([blk, blk], axis=1)),
            "maskl": np.ascontiguousarray(np.concatenate([su.T, su.T], axis=1)),
            "ident2": np.ascontiguousarray(np.concatenate([np.eye(64, dtype=np.float32)] * 2, axis=1))}


def rwkv_core_inputs(core, x_b, p, TP):
    q = core % 4
    d, half = q // 2, q % 2
    T = x_b.shape[0]
    xx = x_b[::-1] if d == 1 else x_b
    xT = np.zeros((D, TP + 2), np.float32)
    xT[:, 1:T + 1] = xx.T
    cols = slice(half * 512, (half + 1) * 512)

    def hl(v):
        return np.asarray(v, np.float32)[cols].reshape(NH, 64).T

    m = {"xT": xT, "mu": np.ascontiguousarray(np.concatenate([_pcol(p["rw_mu"][i]) for i in range(6)], axis=1)),
         "w_rkv": np.ascontiguousarray(np.concatenate([p["rw_w_rkv"][i][:, cols] for i in range(3)], axis=1)),
         "w1": np.ascontiguousarray(p["rw_w1"][d]), "a1": np.ascontiguousarray(p["rw_a1"][d]), "g1": np.ascontiguousarray(p["rw_g1"]),
         "w2": np.ascontiguousarray(p["rw_w2"][d][:, cols]), "a2": np.ascontiguousarray(p["rw_a2"][d][:, cols]), "g2": np.ascontiguousarray(p["rw_g2"][:, cols]),
         "hp": np.ascontiguousarray(np.concatenate([hl(p["rw_w0"][d]), hl(p["rw_a0"][d]), hl(p["rw_k_k"]), hl(p["rw_k_a"]), hl(p["rw_r_k"])], axis=1))}
    m.update(rwkv_consts())
    return m, d, half


GN_EPS = 64e-5
GELU_C = 2.0 * float(np.sqrt(2.0 / np.pi))


def build_proj2(ntok, TW, mode):
    kb = KB()
    nc = kb.nc
    ntile = ntok // TW
    assert ntile * TW == ntok
    Dz = {"plain": 2048, "rwkv": 1024, "lru": 1280}[mode]
    NZ = Dz // 128
    hT = kb.din("hT", [D, ntok])
    w_o = kb.din("w_o", [Dz, D])
    lng = kb.din("lng", [128, 8])
    lnb = kb.din("lnb", [128, 8])
    yT = kb.dout("yT", [D, ntok])
    w_sb = kb.sb("w_sb", [128, NZ, D], BF16)
    g_sb = kb.sb("g_sb", [128, 8])
    b_sb = kb.sb("b_sb", [128, 8])
    zb = [kb.sb("zb%d" % i, [128, NZ, TW], BF16) for i in range(2)]
    ht = [kb.sb("h%d" % i, [128, 8, TW]) for i in range(2)]
    s2 = [kb.sb("s2_%d" % i, [128, 8, TW]) for i in range(2)]
    scr, ones, eps_t = kb.ln_scratch(TW)
    kb.dma(g_sb[:], lng, writes=["lnp"])
    kb.dma(b_sb[:], lnb, writes=["lnp"])
    hv = hT.rearrange("(c p) t -> p c t", p=128)
    yv = yT.rearrange("(c p) t -> p c t", p=128)
    if mode == "plain":
        zT = kb.din("zT", [Dz, ntok])
        zv = zT.rearrange("(c p) t -> p c t", p=128)
    elif mode == "rwkv":
        srcs = [kb.din(nm, [D, ntok]).rearrange("(c p) t -> p c t", p=128) for nm in ("y0T", "y1T", "b0T", "b1T", "gT")]
        gnp = kb.din("gnp", [128, 16])
        bones = kb.din("bones", [128, 128])
        gnp_sb = kb.sb("gnp_sb", [128, 16])
        bones_sb = kb.sb("bones_sb", [128, 128])
        geps = kb.sb("geps", [128, 1])
        rin = [kb.sb("rin%d" % i, [128, 8, TW]) for i in range(5)]
        t1 = kb.sb("t1", [128, TW])
        t2 = kb.sb("t2", [128, TW])
        t3 = kb.sb("t3", [128, TW])
        kb.dma(gnp_sb[:], gnp, writes=["gnp"])
        kb.dma(bones_sb[:], bones, writes=["bones"])
        kb.op("pool", lambda: nc.gpsimd.memset(geps[:], GN_EPS), writes=["geps"])
    else:
        srcs = [kb.din(nm, [LRU_W, ntok]).rearrange("(c p) t -> p c t", p=128) for nm in ("h0T", "h1T")]
        w_g = kb.din("w_g", [D, LRU_W])
        w_g_sb = kb.sb("w_g_sb", [128, 8, LRU_W], BF16)
        rin = [kb.sb("rin%d" % i, [128, NZ, TW]) for i in range(2)]
        xbf = kb.sb("xbf", [128, 8, TW], BF16)
        t1 = kb.sb("t1", [128, TW])
        t2 = kb.sb("t2", [128, TW])
        t3 = kb.sb("t3", [128, TW])
        kb.load_w_bf16(w_g_sb, "w_g", w_g, D, LRU_W, nblk=1280)
    kb.load_w_bf16(w_sb, "w_o", w_o, Dz, D, nblk=1024)
    for i in range(ntile):
        t0 = i * TW
        z, zn = zb[i % 2], "zb%d" % (i % 2)
        h, hn = ht[i % 2], "h%d" % (i % 2)
        s, sname = s2[i % 2], "s2_%d" % (i % 2)
        kb.dma(h[:, :, :], hv[:, :, t0:t0 + TW], writes=[hn])
        if mode == "plain":
            for c in range(NZ):
                kb.dma(z[:, c, :], zv[:, c, t0:t0 + TW], writes=[zn], e="pool")
        elif mode == "rwkv":
            for q in range(5):
                kb.dma(rin[q][:, :, :], srcs[q][:, :, t0:t0 + TW], writes=["rin%d" % q])
            y0, y1, b0, b1, gg = rin
            kb.op("pool", lambda: nc.gpsimd.tensor_tensor(out=y0[:, :, :], in0=y0[:, :, :], in1=y1[:, :, :], op=ALU.add), reads=["rin0", "rin1"], writes=["rin0"])
            kb.op("pool", lambda: nc.gpsimd.tensor_tensor(out=b0[:, :, :], in0=b0[:, :, :], in1=b1[:, :, :], op=ALU.add), reads=["rin2", "rin3"], writes=["rin2"])
            for c in range(8):
                pa, pb = kb.ps[c % 2], kb.ps[2 + c % 2]
                pan, pbn = "ps%d" % (c % 2), "ps%d" % (2 + c % 2)
                kb.op("pe", lambda: nc.tensor.matmul(pa[:, :TW], lhsT=bones_sb[:], rhs=y0[:, c, :], start=True, stop=True), reads=["bones", "rin0"], writes=[pan])
                kb.op("act", lambda: nc.scalar.activation(out=t1[:, :], in_=y0[:, c, :], func=AF.Square), reads=["rin0"], writes=["t1"])
                kb.op("pe", lambda: nc.tensor.matmul(pb[:, :TW], lhsT=bones_sb[:], rhs=t1[:, :], start=True, stop=True), reads=["bones", "t1"], writes=[pbn])
                kb.op("act", lambda: nc.scalar.activation(out=t2[:, :], in_=pa[:, :TW], func=AF.Copy, scale=1.0 / 64), reads=[pan], writes=["t2"])
                kb.op("dve", lambda: nc.vector.tensor_tensor(out=t3[:, :], in0=t2[:, :], in1=t2[:, :], op=ALU.mult), reads=["t2"], writes=["t3"])
                kb.op("dve", lambda: nc.vector.scalar_tensor_tensor(out=t3[:, :], in0=pb[:, :TW], scalar=1.0 / 64, in1=t3[:, :], op0=ALU.mult, op1=ALU.subtract),
                      reads=[pbn, "t3"], writes=["t3"])
                kb.op("act", lambda: nc.scalar.activation(out=t3[:, :], in_=t3[:, :], func=AF.Sqrt, bias=geps[:, 0:1], scale=1.0), reads=["t3", "geps"], writes=["t3"])
                kb.op("dve", lambda: nc.vector.reciprocal(out=t3[:, :], in_=t3[:, :]), reads=["t3"], writes=["t3"])
                kb.op("dve", lambda: nc.vector.tensor_tensor(out=t1[:, :], in0=y0[:, c, :], in1=t2[:, :], op=ALU.subtract), reads=["rin0", "t2", "t1"], writes=["t1"])
                kb.op("pool", lambda: nc.gpsimd.tensor_tensor(out=t1[:, :], in0=t1[:, :], in1=t3[:, :], op=ALU.mult), reads=["t1", "t3"], writes=["t1"])
                kb.op("act", lambda: nc.scalar.activation(out=t1[:, :], in_=t1[:, :], func=AF.Identity, scale=gnp_sb[:, c:c + 1], bias=gnp_sb[:, 8 + c:9 + c]),
                      reads=["t1", "gnp"], writes=["t1"])
                kb.op("pool", lambda: nc.gpsimd.tensor_tensor(out=t1[:, :], in0=t1[:, :], in1=b0[:, c, :], op=ALU.add), reads=["t1", "rin2"], writes=["t1"])
                kb.op("dve", lambda: nc.vector.tensor_tensor(out=z[:, c, :], in0=t1[:, :], in1=gg[:, c, :], op=ALU.mult), reads=["t1", "rin4"], writes=[zn])
        else:
            for q in range(2):
                kb.dma(rin[q][:, :, :], srcs[q][:, :, t0:t0 + TW], writes=["rin%d" % q])
            h0, h1 = rin
            kb.op("pool", lambda: nc.gpsimd.tensor_copy(out=xbf[:, :, :], in_=h[:, :, :]), reads=[hn], writes=["xbf"])
            kb.op("pool", lambda: nc.gpsimd.tensor_tensor(out=h0[:, :, :], in0=h0[:, :, :], in1=h1[:, :, :], op=ALU.add), reads=["rin0", "rin1"], writes=["rin0"])
            for c in range(NZ):
                pa, pan = kb.ps[c % 2], "ps%d" % (c % 2)
                for kc in range(8):
                    kb.op("pe", lambda: nc.tensor.matmul(pa[:, :TW], lhsT=w_g_sb[:, kc, c * 128:(c + 1) * 128], rhs=xbf[:, kc, :], start=(kc == 0), stop=(kc == 7)),
                          reads=["w_g", "xbf"], writes=[pan])
                kb.op("act", lambda: nc.scalar.activation(out=t1[:, :], in_=pa[:, :TW], func=AF.Square), reads=[pan], writes=["t1"])
                kb.op("dve", lambda: nc.vector.tensor_scalar(out=t1[:, :], in0=t1[:, :], scalar1=0.044715, scalar2=1.0, op0=ALU.mult, op1=ALU.add), reads=["t1"], writes=["t1"])
                kb.op("dve", lambda: nc.vector.tensor_tensor(out=t1[:, :], in0=t1[:, :], in1=pa[:, :TW], op=ALU.mult), reads=["t1", pan], writes=["t1"])
                kb.op("act", lambda: nc.scalar.activation(out=t2[:, :], in_=t1[:, :], func=AF.Sigmoid, scale=GELU_C), reads=["t1"], writes=["t2"])
                kb.op("dve", lambda: nc.vector.tensor_tensor(out=t3[:, :], in0=t2[:, :], in1=pa[:, :TW], op=ALU.mult), reads=["t2", pan], writes=["t3"])
                kb.op("pool", lambda: nc.gpsimd.tensor_tensor(out=z[:, c, :], in0=t3[:, :], in1=h0[:, c, :], op=ALU.mult), reads=["t3", "rin0"], writes=[zn])
        for oc in range(8):
            bf = 4 + oc % 2
            pf = kb.ps[bf]
            for kc in range(NZ):
                kb.op("pe", lambda: nc.tensor.matmul(pf[:, :TW], lhsT=w_sb[:, kc, oc * 128:(oc + 1) * 128], rhs=z[:, kc, :],
                                                     start=(kc == 0), stop=(kc == NZ - 1)), reads=["w_o", zn], writes=["ps%d" % bf])
            kb.op("dve", lambda: nc.vector.scalar_tensor_tensor(out=s[:, oc, :], in0=h[:, oc, :], scalar=ALPHA, in1=pf[:, :TW],
                                                                op0=ALU.mult, op1=ALU.add), reads=[hn, "ps%d" % bf], writes=[sname])
        kb.layernorm_inplace(s, sname, TW, g_sb, b_sb, scr, ones, eps_t)
        kb.dma(yv[:, :, t0:t0 + TW], s[:, :, :], reads=[sname])
    return kb.finish()


BATCH, SEQ, NMETA, DEPTH = 2, 16384, 16, 4
TT_ = SEQ + NMETA
NQ = TT_ // 4
NCORES = 8
_PROGS = {}


def _prog(key, fn):
    if key not in _PROGS:
        _PROGS[key] = fn()
    return _PROGS[key]


def _run(nc, in_maps):
    res = run_bass_kernel_spmd(nc, in_maps, core_ids=list(range(NCORES)))
    return res.results


def kernel(**inp):
    f = lambda k: np.asarray(inp[k], np.float32)
    x = f("x")
    positions = np.asarray(inp["positions"]).astype(np.int32)
    meta = f("meta_tokens")
    ln_g, ln_b = f("ln_g"), f("ln_b")
    T = TT_
    hT = [np.ascontiguousarray(np.concatenate([meta, x[b]], axis=0).T) for b in range(BATCH)]
    posbase = [np.concatenate([np.arange(NMETA, dtype=np.int32), positions[b]]) for b in range(BATCH)]
    off = np.concatenate([np.zeros(NMETA, np.float32), np.full(SEQ, float(NMETA), np.float32)])
    bones = np.zeros((128, 128), np.float32)
    bones[:64, :64] = 1.0
    bones[64:, 64:] = 1.0

    def lnp(i, r):
        return {"lng": _pcol(ln_g[i, r]), "lnb": _pcol(ln_b[i, r])}

    for i in range(DEPTH):
        kind, j = i % 3, i // 3
        if kind == 0:
            wk = mla_weights_host(f("mla_w_in")[j], f("mla_q_norm")[j], f("mla_w_q_up")[j], f("mla_kv_norm")[j], f("mla_w_kv_up")[j], 16)
            in_maps = []
            for c in range(NCORES):
                b, q0 = c // 4, (c % 4) * NQ
                m = dict(wk)
                m.update(mla_core_inputs(hT[b], posbase[b], off, q0, NQ))
                in_maps.append(m)
            res = _run(_prog(("mla",), lambda: build_mla(T, NQ, 410, 16)), in_maps)
            w_o = np.ascontiguousarray(f("mla_w_o")[j])
            in_maps = []
            for c in range(NCORES):
                b, q0 = c // 4, (c % 4) * NQ
                m = {"zT": res[c]["oT"], "hT": np.ascontiguousarray(hT[b][:, q0:q0 + NQ]), "w_o": w_o}
                m.update(lnp(i, 0))
                in_maps.append(m)
            res = _run(_prog(("proj", "plain"), lambda: build_proj2(NQ, 205, "plain")), in_maps)
        elif kind == 1:
            p = {k: f(k)[j] for k in ("rw_mu", "rw_w_rkv", "rw_w0", "rw_w1", "rw_w2", "rw_a0", "rw_a1", "rw_a2", "rw_g1", "rw_g2", "rw_k_k", "rw_k_a", "rw_r_k")}
            TP = ((T + 63) // 64) * 64
            in_maps, metas = [], []
            for c in range(NCORES):
                m, d, half = rwkv_core_inputs(c, hT[c // 4].T, p, TP)
                in_maps.append(m)
                metas.append((c // 4, d, half))
            res = _run(_prog(("rwkv",), lambda: build_rwkv(TP)), in_maps)
            ys = np.zeros((BATCH, 2, D, T), np.float32)
            bs = np.zeros((BATCH, 2, D, T), np.float32)
            gs = np.zeros((BATCH, D, T), np.float32)
            for c, (b, d, half) in enumerate(metas):
                rows = slice(half * 512, (half + 1) * 512)
                yy, bb = res[c]["yT"][:, :T], res[c]["bonT"][:, :T]
                ys[b, d, rows] = yy[:, ::-1] if d == 1 else yy
                bs[b, d, rows] = bb[:, ::-1] if d == 1 else bb
                if d == 0:
                    gs[b, rows] = res[c]["gT"][:, :T]
            w_o = np.ascontiguousarray(f("rw_w_o")[j])
            gnp = np.ascontiguousarray(np.concatenate([_pcol(f("rw_gn_g")[j]), _pcol(f("rw_gn_b")[j])], axis=1))
            in_maps = []
            for c in range(NCORES):
                b, q0 = c // 4, (c % 4) * NQ
                sl = slice(q0, q0 + NQ)
                m = {"y0T": np.ascontiguousarray(ys[b, 0][:, sl]), "y1T": np.ascontiguousarray(ys[b, 1][:, sl]),
                     "b0T": np.ascontiguousarray(bs[b, 0][:, sl]), "b1T": np.ascontiguousarray(bs[b, 1][:, sl]),
                     "gT": np.ascontiguousarray(gs[b][:, sl]), "gnp": gnp, "bones": bones,
                     "hT": np.ascontiguousarray(hT[b][:, sl]), "w_o": w_o}
                m.update(lnp(i, 0))
                in_maps.append(m)
            res = _run(_prog(("proj", "rwkv"), lambda: build_proj2(NQ, 205, "rwkv")), in_maps)
        else:
            p = {k: f(k)[j] for k in ("lru_w_in", "lru_conv_w", "lru_conv_b", "lru_gate_w", "lru_gate_b", "lru_lambda")}
            in_maps, metas = [], []
            for c in range(NCORES):
                m, d, sl = lru_core_inputs(c, hT[c // 4].T, p)
                in_maps.append(m)
                metas.append((c // 4, d, sl))
            res = _run(_prog(("lru",), lambda: build_lru(T)), in_maps)
            hd = np.zeros((BATCH, 2, LRU_W, T), np.float32)
            for c, (b, d, sl) in enumerate(metas):
                o = res[c]["hs"]
                o = o[:, ::-1] if d == 1 else o
                for jj, s in enumerate(sl):
                    hd[b, d, s * 128:(s + 1) * 128] = o[jj * 128:(jj + 1) * 128]
            w_o = np.ascontiguousarray(f("lru_w_o")[j])
            w_g = np.ascontiguousarray(p["lru_w_in"][:, :LRU_W])
            in_maps = []
            for c in range(NCORES):
                b, q0 = c // 4, (c % 4) * NQ
                sl = slice(q0, q0 + NQ)
                m = {"h0T": np.ascontiguousarray(hd[b, 0][:, sl]), "h1T": np.ascontiguousarray(hd[b, 1][:, sl]),
                     "w_g": w_g, "hT": np.ascontiguousarray(hT[b][:, sl]), "w_o": w_o}
                m.update(lnp(i, 0))
                in_maps.append(m)
            res = _run(_prog(("proj", "lru"), lambda: build_proj2(NQ, 205, "lru")), in_maps)
        h1 = [np.concatenate([res[b * 4 + q]["yT"] for q in range(4)], axis=1) for b in range(BATCH)]
        cwl = f("ffn_conv_w")[i]
        ffw = {"w_in": np.ascontiguousarray(f("ffn_w_in")[i]), "w_out": np.ascontiguousarray(f("ffn_w_out")[i]),
               "cw": np.ascontiguousarray(np.stack([_pcol(cwl[k]) for k in range(3)], axis=2).reshape(128, -1)),
               "cb": _pcol(f("ffn_conv_b")[i])}
        ffw.update(lnp(i, 1))
        in_maps = []
        for c in range(NCORES):
            b, q0 = c // 4, (c % 4) * NQ
            xT = np.zeros((D, NQ + 2), np.float32)
            lo, hi = max(q0 - 1, 0), min(q0 + NQ + 1, T)
            xT[:, lo - (q0 - 1):hi - (q0 - 1)] = h1[b][:, lo:hi]
            e = np.array([1.0 if q0 > 0 else 0.0, 1.0 if q0 + NQ < T else 0.0], np.float32)
            m = dict(ffw)
            m.update({"xT": xT, "edge": np.ascontiguousarray(np.tile(e[None, :], (128, 1)))})
            in_maps.append(m)
        res = _run(_prog(("ffn",), lambda: build_ffn(NQ, 205)), in_maps)
        hT = [np.ascontiguousarray(np.concatenate([res[b * 4 + q]["yT"] for q in range(4)], axis=1)) for b in range(BATCH)]
    out = np.stack([hT[b][:, NMETA:].T for b in range(BATCH)], axis=0)
    return np.ascontiguousarray(out.astype(np.float32))
```
